# Optimizing a Trainium2 kernel written in Bass

```python
import math
import jax
import jax.numpy as jnp
from jax import lax
import numpy as np

D_MODEL = 2048
BATCH = 1
SEQ = 16384
DEPTH = 2

CTX_LEN = 256
GRID_W = 64
EPS = 1e-6

SSD_HEADS = 16
SSD_HEAD_DIM = 64
SSD_INNER = SSD_HEADS * SSD_HEAD_DIM
SSD_GROUPS = 2
SSD_HPG = SSD_HEADS // SSD_GROUPS
SSD_STATE = 128
SSD_CONV = 3
SSD_CHUNK = 128
SSD_XBC = SSD_INNER + 2 * SSD_GROUPS * SSD_STATE
SSD_DT = 2 * SSD_HEADS
MLA_HEADS = 8
MLA_NOPE = 128
MLA_ROPE = 64
MLA_V = 128
MLA_Q_RANK = 512
MLA_KV_RANK = 512
ROPE_THETA = 10000.0
ATTN_BLOCK = 128
Q_SIDE = SSD_INNER + MLA_Q_RANK
KV_SIDE = SSD_XBC + SSD_DT + MLA_KV_RANK + MLA_ROPE
IN_COLS = Q_SIDE + KV_SIDE
MIX_WIDTH = MLA_HEADS * MLA_V + SSD_INNER
HY_ORDER = 2
HY_SHORT = 3
HY_EMB = 33
HY_HID = 64
HY_INNER = 2
HY_FAST_DECAY = 0.3
HY_SLOW_DECAY = 1.5
HY_TARGET = 1e-2
D_FF = -(-8 * D_MODEL // (3 * 256)) * 256
N_EVEN = (DEPTH + 1) // 2
N_ODD = DEPTH // 2

kernel_name = 'hybrid_ssd_mla_hyena_prefix_dit'


def rms_norm(x, w):
    xf = x.astype(jnp.float32)
    y = xf * lax.rsqrt(jnp.mean(xf * xf, axis=-1, keepdims=True) + EPS)
    return (y * w.astype(jnp.float32)).astype(x.dtype)


def modulate(h, shift, scale):
    return h * (1.0 + scale) + shift


def swiglu(h, w1, w3, w2):
    return (jax.nn.silu(h @ w1) * (h @ w3)) @ w2


def dwconv_centred(x, w, b):
    k = w.shape[0]
    y = lax.conv_general_dilated(x, w[:, None, :].astype(x.dtype), window_strides=(1,),
                                 padding=[(k // 2, k // 2)], dimension_numbers=('NWC', 'WIO', 'NWC'),
                                 feature_group_count=x.shape[-1])
    return y + b


def axial_rope(rows):
    n_freq = MLA_ROPE // 4
    row = jnp.repeat(jnp.arange(rows, dtype=jnp.float32), GRID_W)
    col = jnp.tile(jnp.arange(GRID_W, dtype=jnp.float32), rows)
    inv = ROPE_THETA ** (-jnp.arange(n_freq, dtype=jnp.float32) / n_freq)
    ang = jnp.stack([row[:, None] * inv, col[:, None] * inv], axis=1)
    return jnp.cos(ang), jnp.sin(ang)


def apply_rope(x, cos, sin):
    shp = x.shape
    xr = x.reshape(shp[:-1] + (2, 2, MLA_ROPE // 4))
    x1, x2 = xr[..., 0, :], xr[..., 1, :]
    cos = cos.astype(x.dtype)
    sin = sin.astype(x.dtype)
    out = jnp.stack([x1 * cos - x2 * sin, x1 * sin + x2 * cos], axis=-2)
    return out.reshape(shp)


def block_attention(q, k, v):
    nb, lq, nh, dk = q.shape
    scale = dk ** -0.5
    qb = q.reshape(nb, lq // ATTN_BLOCK, ATTN_BLOCK, nh, dk).transpose(1, 0, 2, 3, 4)

    def one(qi):
        s = jnp.einsum('bqhd,bkhd->bhqk', qi, k).astype(jnp.float32) * scale
        p = jax.nn.softmax(s, axis=-1).astype(v.dtype)
        return jnp.einsum('bhqk,bkhv->bqhv', p, v)

    o = lax.map(one, qb)
    return o.transpose(1, 0, 2, 3, 4).reshape(nb, lq, nh, v.shape[-1])


def ssd_inputs(xbc_raw, dt_raw, conv_w, conv_b, dt_bias):
    nb, seq_len, _ = xbc_raw.shape
    xbc = jax.nn.silu(dwconv_centred(xbc_raw, conv_w, conv_b))
    gn = SSD_GROUPS * SSD_STATE
    xs = xbc[..., :SSD_INNER].reshape(nb, seq_len, SSD_GROUPS, SSD_HPG, SSD_HEAD_DIM)
    bm = xbc[..., SSD_INNER:SSD_INNER + gn].reshape(nb, seq_len, SSD_GROUPS, SSD_STATE)
    cm = xbc[..., SSD_INNER + gn:].reshape(nb, seq_len, SSD_GROUPS, SSD_STATE)
    dt = jax.nn.softplus((dt_raw + dt_bias).astype(jnp.float32)).reshape(nb, seq_len, 2, SSD_GROUPS, SSD_HPG)
    return xs, bm, cm, dt[:, :, 0], dt[:, :, 1]


def ssd_chunked(xs, dt, a, bm, cm, h0):
    nb, seq_len, g, r, p = xs.shape
    n = bm.shape[-1]
    q = SSD_CHUNK
    nc = seq_len // q
    xdt = (xs.astype(jnp.float32) * dt[..., None]).reshape(nb, nc, q, g, r, p)
    bc = bm.astype(jnp.float32).reshape(nb, nc, q, g, n)
    cc = cm.astype(jnp.float32).reshape(nb, nc, q, g, n)
    a_cum = jnp.cumsum((dt * a).reshape(nb, nc, q, g, r), axis=2)
    lower = jnp.tril(jnp.ones((q, q), dtype=bool))[:, :, None, None]
    seg = a_cum[:, :, :, None] - a_cum[:, :, None, :]
    decay = jnp.exp(jnp.where(lower, seg, -jnp.inf))
    cb = jnp.einsum('bcign,bcjgn->bcijg', cc, bc)
    y_diag = jnp.einsum('bcijgr,bcjgrp->bcigrp', cb[..., None] * decay, xdt)
    to_end = jnp.exp(a_cum[:, :, -1:] - a_cum)
    states = jnp.einsum('bcqgn,bcqgrp->bcgrpn', bc, xdt * to_end[..., None])
    chunk_decay = jnp.exp(a_cum[:, :, -1])

    def step(h, inp):
        s_c, d_c = inp
        return h * d_c[..., None, None] + s_c, h

    _, h_prev = lax.scan(step, h0.astype(jnp.float32),
                         (jnp.moveaxis(states, 1, 0), jnp.moveaxis(chunk_decay, 1, 0)))
    h_prev = jnp.moveaxis(h_prev, 0, 1)
    y_off = jnp.einsum('bcign,bcgrpn->bcigrp', cc, h_prev) * jnp.exp(a_cum)[..., None]
    return (y_diag + y_off).reshape(nb, seq_len, g, r, p).astype(xs.dtype)


def ssd_final_state(xs, dt, a, bm):
    a_cum = jnp.cumsum(dt * a, axis=1)
    w = jnp.exp(a_cum[:, -1:] - a_cum) * dt
    return jnp.einsum('blgn,blgrp->bgrpn', bm.astype(jnp.float32), xs.astype(jnp.float32) * w[..., None])


def bi_ssd(xs, dt_f, dt_b, bm, cm, a, d_skip, h0_f, h0_b):
    rev = lambda t: jnp.flip(t, axis=1)
    y_f = ssd_chunked(xs, dt_f, a[0], bm, cm, h0_f)
    y_b = rev(ssd_chunked(rev(xs), rev(dt_b), a[1], rev(bm), rev(cm), h0_b))
    return y_f + y_b + (d_skip[0] + d_skip[1])[..., None] * xs


def ssd_gated_norm(y, z, w):
    nb, seq_len = z.shape[:2]
    gy = y.reshape(nb, seq_len, SSD_GROUPS, -1) * jax.nn.silu(z.reshape(nb, seq_len, SSD_GROUPS, -1))
    return rms_norm(gy, w.reshape(SSD_GROUPS, -1)).reshape(nb, seq_len, SSD_INNER)


def mla_q(cq, q_norm_w, w_uq, rope):
    nb, seq_len, _ = cq.shape
    q = (rms_norm(cq, q_norm_w) @ w_uq).reshape(nb, seq_len, MLA_HEADS, MLA_NOPE + MLA_ROPE)
    if rope is None:
        return q
    cos, sin = rope
    return jnp.concatenate([q[..., :MLA_NOPE], apply_rope(q[..., MLA_NOPE:], cos[:, None], sin[:, None])], axis=-1)


def mla_kv(ckv, kr, kv_norm_w, w_ukv, rope):
    nb, seq_len, _ = ckv.shape
    kv = (rms_norm(ckv, kv_norm_w) @ w_ukv).reshape(nb, seq_len, MLA_HEADS, MLA_NOPE + MLA_V)
    k_nope, v = kv[..., :MLA_NOPE], kv[..., MLA_NOPE:]
    if rope is not None:
        kr = apply_rope(kr, rope[0], rope[1])
    k_rope = jnp.broadcast_to(kr[:, :, None, :], (nb, seq_len, MLA_HEADS, MLA_ROPE))
    return jnp.concatenate([k_nope, k_rope], axis=-1), v


def even_mixer(n_lat, n_ctx, rope, ctx_out, w_in, conv_w, conv_b, dt_bias, a_log, d_skip, ssd_norm_w,
               q_norm_w, w_uq, kv_norm_w, w_ukv, w_o):
    a = -jnp.exp(a_log.astype(jnp.float32)).reshape(2, SSD_GROUPS, SSD_HPG)
    d_sk = d_skip.reshape(2, SSD_GROUPS, SSD_HPG)
    o1 = SSD_XBC
    o2 = o1 + SSD_DT
    o3 = o2 + MLA_KV_RANK

    def kv_side(pk, rope_):
        xs, bm, cm, dt_f, dt_b = ssd_inputs(pk[..., :o1], pk[..., o1:o2], conv_w, conv_b, dt_bias)
        k, v = mla_kv(pk[..., o2:o3], pk[..., o3:], kv_norm_w, w_ukv, rope_)
        return xs, bm, cm, dt_f, dt_b, k, v

    def q_side(pq, rope_):
        return pq[..., :SSD_INNER], mla_q(pq[..., SSD_INNER:], q_norm_w, w_uq, rope_)

    def merge(z, y_ssd, o_att):
        nb, seq_len = z.shape[:2]
        heads = jnp.concatenate([o_att.reshape(nb, seq_len, -1), ssd_gated_norm(y_ssd, z, ssd_norm_w)], axis=-1)
        return heads @ w_o

    rev = lambda t: jnp.flip(t, axis=1)
    if ctx_out:
        p_ctx = n_ctx @ w_in
        pq_c, pk_c = p_ctx[..., :Q_SIDE], p_ctx[..., Q_SIDE:]
    else:
        pk_c = n_ctx @ w_in[:, Q_SIDE:]
    xs_c, b_c, c_c, dtf_c, dtb_c, k_c, v_c = kv_side(pk_c, None)
    h0_f = ssd_final_state(xs_c, dtf_c, a[0], b_c)
    h0_b = ssd_final_state(rev(xs_c), rev(dtb_c), a[1], rev(b_c))
    p_lat = n_lat @ w_in
    z, q = q_side(p_lat[..., :Q_SIDE], rope)
    xs, bm, cm, dt_f, dt_b, k, v = kv_side(p_lat[..., Q_SIDE:], rope)
    y = bi_ssd(xs, dt_f, dt_b, bm, cm, a, d_sk, h0_f, h0_b)
    o = block_attention(q, jnp.concatenate([k_c, k], axis=1), jnp.concatenate([v_c, v], axis=1))
    out_lat = merge(z, y, o)
    if not ctx_out:
        return out_lat, None
    z_c, q_c = q_side(pq_c, None)
    zero = jnp.zeros_like(h0_f)
    y_c = bi_ssd(xs_c, dtf_c, dtb_c, b_c, c_c, a, d_sk, zero, zero)
    o_c = block_attention(q_c, k_c, v_c)
    return out_lat, merge(z_c, y_c, o_c)


def hyena_filter_hidden(seq_len, fw1, fb1, fw_mid, fb_mid, freq):
    t = jnp.linspace(0.0, 1.0, seq_len, dtype=jnp.float32)[:, None]
    bands = (HY_EMB - 1) // 2
    w = 2.0 * math.pi * jnp.arange(seq_len, dtype=jnp.float32)[:, None] / seq_len
    f = jnp.linspace(1e-4, bands - 1, bands, dtype=jnp.float32)[None, :]
    emb = jnp.concatenate([t, jnp.cos(f * w), -jnp.sin(f * w)], axis=-1)
    fr = freq.astype(jnp.float32)
    hid = jnp.sin(fr * (emb @ fw1.astype(jnp.float32) + fb1.astype(jnp.float32)))
    for j in range(HY_INNER):
        hid = jnp.sin(fr * (hid @ fw_mid[j].astype(jnp.float32) + fb_mid[j].astype(jnp.float32)))
    return hid, t


def hyena_decay(t):
    lo = math.log(HY_SLOW_DECAY) / HY_TARGET
    hi = math.log(HY_FAST_DECAY) / HY_TARGET
    deltas = jnp.abs(jnp.linspace(lo, hi, D_MODEL, dtype=jnp.float32))
    return jnp.exp(-t * deltas)


def long_conv_bidir(u, h_two):
    seq_len, d = u.shape[1], u.shape[2]
    k = jnp.concatenate([h_two[:, 0], jnp.zeros((1, d), h_two.dtype), jnp.flip(h_two[1:, 1], axis=0)], axis=0)
    k = k / jnp.sum(jnp.abs(k), axis=0, keepdims=True)
    kf = jnp.fft.rfft(k, axis=0)
    uf = jnp.fft.rfft(u.astype(jnp.float32), n=2 * seq_len, axis=1)
    return jnp.fft.irfft(uf * kf, n=2 * seq_len, axis=1)[:, :seq_len].astype(u.dtype)


def hyena(u, w_in, b_in, short_w, short_b, fw1, fb1, fw_mid, fb_mid, freq, fw_out, fbias, w_out, b_out):
    nb, seq_len, d = u.shape
    proj = dwconv_centred(u @ w_in + b_in, short_w, short_b)
    x1, x2, y = proj[..., :d], proj[..., d:2 * d], proj[..., 2 * d:]
    hid, t = hyena_filter_hidden(seq_len, fw1, fb1, fw_mid, fb_mid, freq)
    window = hyena_decay(t)
    for i, gate in enumerate((x1, x2)):
        h_two = jnp.einsum('lf,fkd->lkd', hid, fw_out[:, i].astype(jnp.float32)) * window[:, None, :]
        y = gate * (long_conv_bidir(y, h_two) + fbias[i] * y)
    return y @ w_out + b_out


def setup_inputs(seed: int = 0) -> dict:
    key = jax.random.key(seed)
    ks = iter(jax.random.split(key, 48))
    f32 = jnp.float32
    D = D_MODEL

    def nrm(shape, scale):
        return jax.random.normal(next(ks), shape, f32) * scale

    def gain(shape):
        return 1.0 + nrm(shape, 0.01)

    x = nrm((BATCH, SEQ, D), 1.0)
    c = nrm((BATCH, D), 1.0)
    ctx = nrm((BATCH, CTX_LEN, D), 1.0)
    c_ctx = nrm((D,), 1.0)
    dt0 = jnp.exp(jax.random.uniform(next(ks), (N_EVEN, SSD_DT), f32, math.log(1e-3), math.log(1e-1)))
    dt_bias = dt0 + jnp.log(-jnp.expm1(-dt0))
    a_log = jnp.log(jax.random.uniform(next(ks), (N_EVEN, 2, SSD_HEADS), f32, 1.0, 16.0))
    return {
        'x': x, 'c': c, 'ctx': ctx, 'c_ctx': c_ctx,
        'mod_w': nrm((DEPTH, D, 6 * D), 0.5 * D ** -0.5),
        'mod_b': nrm((DEPTH, 6 * D), 0.02),
        'norm_mix_w': gain((DEPTH, D)),
        'norm_ffn_w': gain((DEPTH, D)),
        'ffn_w1': nrm((DEPTH, D, D_FF), D ** -0.5),
        'ffn_w3': nrm((DEPTH, D, D_FF), D ** -0.5),
        'ffn_w2': nrm((DEPTH, D_FF, D), D_FF ** -0.5),
        'ev_w_in': nrm((N_EVEN, D, IN_COLS), D ** -0.5),
        'ev_conv_w': nrm((N_EVEN, SSD_CONV, SSD_XBC), SSD_CONV ** -0.5),
        'ev_conv_b': nrm((N_EVEN, SSD_XBC), 0.02),
        'ev_dt_bias': dt_bias,
        'ev_a_log': a_log,
        'ev_d_skip': gain((N_EVEN, 2, SSD_HEADS)),
        'ev_ssd_norm_w': gain((N_EVEN, SSD_INNER)),
        'ev_q_norm_w': gain((N_EVEN, MLA_Q_RANK)),
        'ev_w_uq': nrm((N_EVEN, MLA_Q_RANK, MLA_HEADS * (MLA_NOPE + MLA_ROPE)), MLA_Q_RANK ** -0.5),
        'ev_kv_norm_w': gain((N_EVEN, MLA_KV_RANK)),
        'ev_w_ukv': nrm((N_EVEN, MLA_KV_RANK, MLA_HEADS * (MLA_NOPE + MLA_V)), MLA_KV_RANK ** -0.5),
        'ev_w_o': nrm((N_EVEN, MIX_WIDTH, D), MIX_WIDTH ** -0.5),
        'hy_w_in': nrm((N_ODD, D, 3 * D), D ** -0.5),
        'hy_b_in': nrm((N_ODD, 3 * D), 0.02),
        'hy_short_w': nrm((N_ODD, HY_SHORT, 3 * D), HY_SHORT ** -0.5),
        'hy_short_b': nrm((N_ODD, 3 * D), 0.02),
        'hy_fw1': nrm((N_ODD, HY_EMB, HY_HID), HY_EMB ** -0.5),
        'hy_fb1': nrm((N_ODD, HY_HID), 0.02),
        'hy_fw_mid': nrm((N_ODD, HY_INNER, HY_HID, HY_HID), HY_HID ** -0.5),
        'hy_fb_mid': nrm((N_ODD, HY_INNER, HY_HID), 0.02),
        'hy_freq': gain((N_ODD, HY_HID)),
        'hy_fw_out': nrm((N_ODD, HY_HID, HY_ORDER, 2, D), HY_HID ** -0.5),
        'hy_fbias': nrm((N_ODD, HY_ORDER, D), 0.5),
        'hy_w_out': nrm((N_ODD, D, D), D ** -0.5),
        'hy_b_out': nrm((N_ODD, D), 0.02),
        'final_norm_w': gain((D,)),
    }


def reference(x, c, ctx, c_ctx, mod_w, mod_b, norm_mix_w, norm_ffn_w, ffn_w1, ffn_w3, ffn_w2,
              ev_w_in, ev_conv_w, ev_conv_b, ev_dt_bias, ev_a_log, ev_d_skip, ev_ssd_norm_w,
              ev_q_norm_w, ev_w_uq, ev_kv_norm_w, ev_w_ukv, ev_w_o,
              hy_w_in, hy_b_in, hy_short_w, hy_short_b, hy_fw1, hy_fb1, hy_fw_mid, hy_fb_mid,
              hy_freq, hy_fw_out, hy_fbias, hy_w_out, hy_b_out, final_norm_w):
    seq_len = x.shape[1]
    rows = seq_len // GRID_W
    rope = axial_rope(rows)
    c_act = jax.nn.silu(c)
    cc_act = jax.nn.silu(c_ctx)
    xc = ctx
    for i in range(DEPTH):
        even = i % 2 == 0
        ctx_later = any(j % 2 == 0 for j in range(i + 1, DEPTH))
        mod = (c_act @ mod_w[i] + mod_b[i])[:, None, :]
        sh1, sc1, g1, sh2, sc2, g2 = jnp.split(mod, 6, axis=-1)
        n_lat = modulate(rms_norm(x, norm_mix_w[i]), sh1, sc1)
        if ctx_later:
            csh1, csc1, cg1, csh2, csc2, cg2 = jnp.split(cc_act @ mod_w[i] + mod_b[i], 6)
        elif even:
            csh1, csc1 = jnp.split(cc_act @ mod_w[i][:, :2 * D_MODEL] + mod_b[i][:2 * D_MODEL], 2)
        if even or ctx_later:
            n_ctx = modulate(rms_norm(xc, norm_mix_w[i]), csh1, csc1)
        if even:
            e = i // 2
            o_lat, o_ctx = even_mixer(n_lat, n_ctx, rope, ctx_later, ev_w_in[e], ev_conv_w[e], ev_conv_b[e],
                                      ev_dt_bias[e], ev_a_log[e], ev_d_skip[e], ev_ssd_norm_w[e],
                                      ev_q_norm_w[e], ev_w_uq[e], ev_kv_norm_w[e], ev_w_ukv[e], ev_w_o[e])
        else:
            o = i // 2
            hy_args = (hy_w_in[o], hy_b_in[o], hy_short_w[o], hy_short_b[o], hy_fw1[o], hy_fb1[o],
                       hy_fw_mid[o], hy_fb_mid[o], hy_freq[o], hy_fw_out[o], hy_fbias[o], hy_w_out[o], hy_b_out[o])
            o_lat = hyena(n_lat, *hy_args)
            o_ctx = hyena(n_ctx, *hy_args) if ctx_later else None
        x = x + g1 * o_lat
        x = x + g2 * swiglu(modulate(rms_norm(x, norm_ffn_w[i]), sh2, sc2), ffn_w1[i], ffn_w3[i], ffn_w2[i])
        if ctx_later:
            xc = xc + cg1 * o_ctx
            xc = xc + cg2 * swiglu(modulate(rms_norm(xc, norm_ffn_w[i]), csh2, csc2), ffn_w1[i], ffn_w3[i], ffn_w2[i])
    return rms_norm(x, final_norm_w)
```

```python
from contextlib import ExitStack
import numpy as np
import concourse.bass as bass
import concourse.mybir as mybir
from concourse.bass_utils import run_bass_kernel_spmd

F32 = mybir.dt.float32
BF16 = mybir.dt.bfloat16
AF = mybir.ActivationFunctionType
ALU = mybir.AluOpType
AX = mybir.AxisListType

D = 2048
SEQ = 16384
CTX = 256
DFF = 5632
NCORE = 8
TPC = SEQ // NCORE
TT = 512
EPS = 1e-6


class TR:
    def __init__(self, h):
        self.h = h
        self.w = None
        self.r = {}

    def __getitem__(self, k):
        return self.h[k]


class Prog:
    def __init__(self, ndma=32):
        self.nc = bass.Bass("TRN2", target_bir_lowering=False)
        nc = self.nc
        self.es = ExitStack()
        self.es.enter_context(nc.allow_low_precision("bf16 matmul operands, fp32 accumulate"))
        self.engs = {'pe': nc.tensor, 'act': nc.scalar, 'dve': nc.vector, 'pool': nc.gpsimd, 'sp': nc.sync}
        self.sem = {}
        self.cnt = {}
        for e in ['pe', 'act', 'dve', 'pool']:
            self.sem[e] = self.es.enter_context(nc.semaphore("s_" + e))
            self.cnt[e] = 0
        self.ndma = ndma
        for i in range(ndma):
            self.sem[('d', i)] = self.es.enter_context(nc.semaphore("d%d" % i))
            self.cnt[('d', i)] = 0
        self.dnext = 0
        self.waited = {e: {} for e in self.engs}
        self.psum = [TR(self.es.enter_context(nc.psum_tensor("ps%d" % i, [128, 512], F32))) for i in range(8)]
        self.psn = 0
        self.out_events = []
        self.n_ins = 0

    def sb(self, name, shape, dtype=F32):
        return TR(self.es.enter_context(self.nc.sbuf_tensor(name, list(shape), dtype)))

    def din(self, name, shape, dtype=F32):
        return TR(self.nc.dram_tensor(name, list(shape), dtype, kind="ExternalInput").ap())

    def dout(self, name, shape, dtype=F32):
        return TR(self.nc.dram_tensor(name, list(shape), dtype, kind="ExternalOutput").ap())

    def dscratch(self, name, shape, dtype=F32):
        return TR(self.nc.dram_tensor(name, list(shape), dtype, kind="Internal").ap())

    def ps(self):
        p = self.psum[self.psn % 8]
        self.psn += 1
        return p

    def _wait(self, e, ev):
        key, val = ev
        if self.waited[e].get(key, 0) >= val:
            return
        self.engs[e].wait_ge(self.sem[key], val)
        self.waited[e][key] = val

    def _deps(self, e, reads, writes):
        deps = []
        for t in reads:
            if t.w is not None:
                deps.append(t.w)
        for t in writes:
            if t.w is not None:
                deps.append(t.w)
            deps.extend(t.r.items())
        for ev in deps:
            if e == 'pe' and ev[0] == 'pe':
                continue
            self._wait(e, ev)

    def _record(self, ev, reads, writes):
        for t in writes:
            t.w = ev
            t.r = {}
        for t in reads:
            if t.r.get(ev[0], 0) < ev[1]:
                t.r[ev[0]] = ev[1]

    def op(self, e, fn, reads=(), writes=()):
        self._deps(e, reads, writes)
        ins = fn(self.engs[e])
        self.cnt[e] += 1
        ins.then_inc(self.sem[e], 1)
        ev = (e, self.cnt[e])
        self._record(ev, reads, writes)
        self.n_ins += 1
        return ev

    def dma(self, out_ap, in_ap, reads=(), writes=(), q='sp', is_out=False):
        self._deps(q, reads, writes)
        i = self.dnext
        self.dnext = (self.dnext + 1) % self.ndma
        key = ('d', i)
        if self.cnt[key] > 0:
            self._wait(q, (key, self.cnt[key] * 16))
        self.cnt[key] += 1
        self.engs[q].dma_start(out=out_ap, in_=in_ap).then_inc(self.sem[key], 16)
        ev = (key, self.cnt[key] * 16)
        self._record(ev, reads, writes)
        if is_out:
            self.out_events.append(ev)
        self.n_ins += 1
        return ev

    def barrier(self):
        for e in self.engs:
            for key, cnt in self.cnt.items():
                if cnt > 0:
                    val = cnt * 16 if isinstance(key, tuple) else cnt
                    self._wait(e, (key, val))

    def push(self):
        self._saved_es = getattr(self, '_saved_es', [])
        self._saved_es.append(self.es)
        self.es = ExitStack()

    def pop(self):
        self.barrier()
        self.es.close()
        self.es = self._saved_es.pop()

    def finish(self):
        for ev in self.out_events:
            self._wait('sp', ev)
        for e in ['pe', 'act', 'dve', 'pool']:
            if self.cnt[e] > 0:
                self._wait('sp', (e, self.cnt[e]))
        self.es.close()
        return self.nc


def run_spmd(prog, in_maps):
    nc = prog.finish()
    res = run_bass_kernel_spmd(nc, in_maps, core_ids=list(range(NCORE)))
    return res.results


class WStream:
    def __init__(self, P, maxk=DFF, tag="w"):
        self.P = P
        self.stg = [P.sb("%s_stg%d" % (tag, i), [128, maxk], F32) for i in range(2)]
        self.wb = [P.sb("%s_wb%d" % (tag, i), [128, maxk], BF16) for i in range(2)]
        self.n = 0
        self.maxel = maxk

    def load(self, W, col0, ncol, K):
        P = self.P
        KC = K // 128
        assert KC * ncol <= self.maxel
        i = self.n % 2
        self.n += 1
        stg, wb = self.stg[i], self.wb[i]
        src = W[:, col0:col0 + ncol].rearrange("(kc p) f -> p kc f", p=128)
        dst = stg[:, 0:KC * ncol].rearrange("p (kc f) -> p kc f", kc=KC)
        q = 'sp'
        P.dma(dst, src, reads=[W], writes=[stg], q=q)
        ce = 'pool' if (self.n % 2 == 0) else 'act'
        if ce == 'pool':
            P.op('pool', lambda e: e.tensor_copy(out=wb[:, 0:KC * ncol], in_=stg[:, 0:KC * ncol]), reads=[stg], writes=[wb])
        else:
            P.op('act', lambda e: e.copy(out=wb[:, 0:KC * ncol], in_=stg[:, 0:KC * ncol]), reads=[stg], writes=[wb])

        def view(kc, c0, c1):
            return wb[:, kc * ncol + c0: kc * ncol + c1]
        return wb, view


def emit_consts(P):
    c = {}
    c['ones_f'] = P.sb("ones_f", [128, 128], F32)
    P.op('dve', lambda e: e.memset(c['ones_f'][:], 1.0), writes=[c['ones_f']])
    return c


def emit_rmsnorm_mod(P, c, x, h, scale_t, shift_t, ntok, sq, rstd, nch=16, dim=None, ki=0, ko=0, ks=0):
    dim = dim or nch * 128
    ps = P.ps()
    for kc in range(nch):
        P.op('act', lambda e: e.activation(out=sq[:, :ntok], in_=x[:, ki + kc, :ntok], func=AF.Square), reads=[x], writes=[sq])
        P.op('pe', lambda e: e.matmul(ps[:, :ntok], c['ones_f'][:], sq[:, :ntok], start=(kc == 0), stop=(kc == nch - 1)),
             reads=[sq, c['ones_f']], writes=[ps])
    P.op('dve', lambda e: e.tensor_scalar(out=rstd[:, :ntok], in0=ps[:, :ntok], scalar1=1.0 / dim, scalar2=EPS, op0=ALU.mult, op1=ALU.add),
         reads=[ps], writes=[rstd])
    P.op('act', lambda e: e.activation(out=rstd[:, :ntok], in_=rstd[:, :ntok], func=AF.Sqrt), reads=[rstd], writes=[rstd])
    P.op('dve', lambda e: e.reciprocal(out=rstd[:, :ntok], in_=rstd[:, :ntok]), reads=[rstd], writes=[rstd])
    for kc in range(nch):
        if shift_t is not None:
            P.op('dve', lambda e: e.scalar_tensor_tensor(out=sq[:, :ntok], in0=x[:, ki + kc, :ntok], scalar=scale_t[:, ks + kc:ks + kc + 1], in1=rstd[:, :ntok],
                                                          op0=ALU.mult, op1=ALU.mult), reads=[x, scale_t, rstd], writes=[sq])
            P.op('act', lambda e: e.activation(out=h[:, ko + kc, :ntok], in_=sq[:, :ntok], func=AF.Identity, bias=shift_t[:, ks + kc:ks + kc + 1], scale=1.0),
                 reads=[sq, shift_t], writes=[h])
        else:
            P.op('dve', lambda e: e.scalar_tensor_tensor(out=h[:, ko + kc, :ntok], in0=x[:, ki + kc, :ntok], scalar=scale_t[:, ks + kc:ks + kc + 1], in1=rstd[:, :ntok],
                                                          op0=ALU.mult, op1=ALU.mult), reads=[x, scale_t, rstd], writes=[h])


def emit_linear(P, ws, W, K, f0, f1, h, ntok, evac, cb=256):
    KC = K // 128
    f = f0
    while f < f1:
        n = min(cb, f1 - f)
        wb, view = ws.load(W, f, n, K)
        for s in range(0, n, 128):
            m = min(128, n - s)
            ps = P.ps()
            for kc in range(KC):
                P.op('pe', lambda e: e.matmul(ps[:m, :ntok], view(kc, s, s + m), h[:, kc, :ntok], start=(kc == 0), stop=(kc == KC - 1)),
                     reads=[wb, h], writes=[ps])
            evac((f + s) // 128, ps, m)
        f += n


def emit_ffn(P, c, ws, x, h, u, tmp, w1, w3, w2, g2, ntok):
    KC = D // 128
    FC = DFF // 128
    f = 0
    cb = 256
    while f < DFF:
        n = min(cb, DFF - f)
        wb1, v1 = ws.load(w1, f, n, D)
        wb3, v3 = ws.load(w3, f, n, D)
        for s in range(0, n, 128):
            p1 = P.ps()
            p3 = P.ps()
            for kc in range(KC):
                P.op('pe', lambda e: e.matmul(p1[:, :ntok], v1(kc, s, s + 128), h[:, kc, :ntok], start=(kc == 0), stop=(kc == KC - 1)),
                     reads=[wb1, h], writes=[p1])
            for kc in range(KC):
                P.op('pe', lambda e: e.matmul(p3[:, :ntok], v3(kc, s, s + 128), h[:, kc, :ntok], start=(kc == 0), stop=(kc == KC - 1)),
                     reads=[wb3, h], writes=[p3])
            fc = (f + s) // 128
            P.op('act', lambda e: e.activation(out=tmp[:, :ntok], in_=p1[:, :ntok], func=AF.Silu), reads=[p1], writes=[tmp])
            P.op('dve', lambda e: e.tensor_tensor(out=u[:, fc, :ntok], in0=tmp[:, :ntok], in1=p3[:, :ntok], op=ALU.mult), reads=[tmp, p3], writes=[u])
        f += n
    for dc in range(KC):
        wb, v = ws.load(w2, dc * 128, 128, DFF)
        ps = P.ps()
        for fc in range(FC):
            P.op('pe', lambda e: e.matmul(ps[:, :ntok], v(fc, 0, 128), u[:, fc, :ntok], start=(fc == 0), stop=(fc == FC - 1)),
                 reads=[wb, u], writes=[ps])
        P.op('dve', lambda e: e.scalar_tensor_tensor(out=x[:, dc, :ntok], in0=ps[:, :ntok], scalar=g2[:, dc:dc + 1], in1=x[:, dc, :ntok],
                                                      op0=ALU.mult, op1=ALU.add), reads=[ps, g2, x], writes=[x])


def load_vec(P, name, n=16):
    d = P.din(name, [128, n])
    t = P.sb(name + "_sb", [128, n])
    P.dma(t[:], d[:], reads=[d], writes=[t])
    return t


def build_l0():
    P = Prog()
    NCOL = 12288 // NCORE
    cc = P.din("cc", [128, 16, 2])
    mw = P.din("mw", [2, D, NCOL])
    mb = P.din("mb", [2, 2, NCOL])
    out = P.dout("out", [2, 2, NCOL])
    cs = P.sb("cs", [128, 16, 2])
    ca = P.sb("ca", [128, 16, 2])
    P.dma(cs[:], cc[:], reads=[cc], writes=[cs])
    P.op('act', lambda e: e.activation(out=ca[:], in_=cs[:], func=AF.Silu), reads=[cs], writes=[ca])
    bsb = P.sb("bsb", [2, 2, NCOL])
    P.dma(bsb[:], mb[:].rearrange("l r n -> r l n"), reads=[mb], writes=[bsb])
    wt = [P.sb("wt%d" % i, [128, 16, 512]) for i in range(2)]
    res = P.sb("res", [2, 2, NCOL])
    n = 0
    for l in range(2):
        for cb in range(NCOL // 512):
            w = wt[n % 2]
            n += 1
            P.dma(w[:], mw[l, :, cb * 512:(cb + 1) * 512].rearrange("(kc p) f -> p kc f", p=128), reads=[mw], writes=[w])
            ps = P.ps()
            for kc in range(16):
                P.op('pe', lambda e: e.matmul(ps[:2, :], ca[:, kc, :], w[:, kc, :], start=(kc == 0), stop=(kc == 15)), reads=[ca, w], writes=[ps])
            P.op('dve', lambda e: e.tensor_tensor(out=res[:, l, cb * 512:(cb + 1) * 512], in0=ps[:2, :], in1=bsb[:, l, cb * 512:(cb + 1) * 512], op=ALU.add),
                 reads=[ps, bsb], writes=[res])
    P.dma(out[:].rearrange("l r n -> r l n"), res[:], reads=[res], writes=[out], is_out=True)
    return P


def vec_layout(v):
    v = np.asarray(v, np.float32)
    return np.ascontiguousarray(v.reshape(-1, 128).T)


def run_l0(c, c_ctx, mod_w, mod_b):
    P = build_l0()
    NCOL = 12288 // NCORE
    cc = np.stack([vec_layout(c.reshape(-1)), vec_layout(c_ctx.reshape(-1))], axis=-1)
    in_maps = []
    for i in range(NCORE):
        sl = slice(i * NCOL, (i + 1) * NCOL)
        in_maps.append({"cc": np.ascontiguousarray(cc),
                        "mw": np.ascontiguousarray(mod_w[:, :, sl]),
                        "mb": np.ascontiguousarray(np.repeat(mod_b[:, None, sl], 2, axis=1))})
    res = run_spmd(P, in_maps)
    out = np.concatenate([r["out"] for r in res], axis=-1)
    return out


def build_l6(do_proj=True, do_final=True):
    P = Prog()
    c = emit_consts(P)
    xT = P.din("xT", [D, TPC])
    out = P.dout("outT", [D, TPC])
    w1 = P.din("w1", [D, DFF])
    w3 = P.din("w3", [D, DFF])
    w2 = P.din("w2", [DFF, D])
    nw = load_vec(P, "nw")
    sc2 = load_vec(P, "sc2")
    sh2 = load_vec(P, "sh2")
    g2 = load_vec(P, "g2")
    if do_proj:
        yT = P.din("yT", [D, TPC])
        wo = P.din("wo", [D, D])
        bo = load_vec(P, "bo")
        g1 = load_vec(P, "g1")
        gb = P.sb("gb", [128, 16])
        P.op('dve', lambda e: e.tensor_tensor(out=gb[:], in0=g1[:], in1=bo[:], op=ALU.mult), reads=[g1, bo], writes=[gb])
    if do_final:
        fw = load_vec(P, "fw")
    scl = P.sb("scl", [128, 16])
    P.op('dve', lambda e: e.scalar_tensor_tensor(out=scl[:], in0=sc2[:], scalar=1.0, in1=nw[:], op0=ALU.add, op1=ALU.mult), reads=[sc2, nw], writes=[scl])
    ws = WStream(P)
    x = P.sb("x", [128, 16, TT])
    h = P.sb("h", [128, 16, TT], BF16)
    u = P.sb("u", [128, DFF // 128, TT], BF16)
    sq = P.sb("sq", [128, TT])
    rstd = P.sb("rstd", [128, TT])
    tmp = P.sb("tmp", [128, TT])
    xv = xT[:, :].rearrange("(kc p) t -> p kc t", p=128)
    ov = out[:, :].rearrange("(kc p) t -> p kc t", p=128)
    for tt in range(TPC // TT):
        t0 = tt * TT
        P.dma(x[:], xv[:, :, t0:t0 + TT], reads=[xT], writes=[x])
        if do_proj:
            yv = yT[:, :].rearrange("(kc p) t -> p kc t", p=128)
            for kc in range(16):
                P.dma(tmp[:], yv[:, kc, t0:t0 + TT], reads=[yT], writes=[tmp])
                P.op('dve', lambda e: e.tensor_copy(out=h[:, kc, :], in_=tmp[:]), reads=[tmp], writes=[h])

            def evac(dc, ps, m):
                P.op('dve', lambda e: e.scalar_tensor_tensor(out=x[:, dc, :], in0=ps[:, :TT], scalar=g1[:, dc:dc + 1], in1=x[:, dc, :],
                                                              op0=ALU.mult, op1=ALU.add), reads=[ps, g1, x], writes=[x])
                P.op('dve', lambda e: e.tensor_scalar(out=x[:, dc, :], in0=x[:, dc, :], scalar1=gb[:, dc:dc + 1], scalar2=None, op0=ALU.add),
                     reads=[x, gb], writes=[x])
            emit_linear(P, ws, wo, D, 0, D, h, TT, evac)
        emit_rmsnorm_mod(P, c, x, h, scl, sh2, TT, sq, rstd)
        emit_ffn(P, c, ws, x, h, u, tmp, w1, w3, w2, g2, TT)
        if do_final:
            ps = P.ps()
            for kc in range(16):
                P.op('act', lambda e: e.activation(out=sq[:], in_=x[:, kc, :], func=AF.Square), reads=[x], writes=[sq])
                P.op('pe', lambda e: e.matmul(ps[:, :TT], c['ones_f'][:], sq[:], start=(kc == 0), stop=(kc == 15)), reads=[sq, c['ones_f']], writes=[ps])
            P.op('dve', lambda e: e.tensor_scalar(out=rstd[:], in0=ps[:, :TT], scalar1=1.0 / D, scalar2=EPS, op0=ALU.mult, op1=ALU.add), reads=[ps], writes=[rstd])
            P.op('act', lambda e: e.activation(out=rstd[:], in_=rstd[:], func=AF.Sqrt), reads=[rstd], writes=[rstd])
            P.op('dve', lambda e: e.reciprocal(out=rstd[:], in_=rstd[:]), reads=[rstd], writes=[rstd])
            for kc in range(16):
                P.op('dve', lambda e: e.scalar_tensor_tensor(out=x[:, kc, :], in0=x[:, kc, :], scalar=fw[:, kc:kc + 1], in1=rstd[:],
                                                              op0=ALU.mult, op1=ALU.mult), reads=[x, fw, rstd], writes=[x])
        P.dma(ov[:, :, t0:t0 + TT], x[:], reads=[x], writes=[out], is_out=True)
    return P


NQ = 1024 + 512
XBC0, DT0, CKV0, KR0 = 1536, 3072, 3104, 3616
L1COLS = 8 + TPC + CTX
QSCALE = 192.0 ** -0.5


def build_l1():
    P = Prog()
    c = emit_consts(P)
    xh = P.din("xh", [D, L1COLS])
    hmask = P.din("hmask", [128, 8])
    cosd = P.din("cosT", [64, TPC + CTX])
    sind = P.din("sinT", [64, TPC + CTX])
    w_in = P.din("w_in", [D, 3680])
    w_in_sw = P.din("w_in_sw", [D, 64])
    w_uq = P.din("w_uq", [512, 1536])
    w_uq_sw = P.din("w_uq_sw", [512, 512])
    w_ukv = P.din("w_ukv", [512, 2048])
    nmw = load_vec(P, "nmw")
    mods = {k: load_vec(P, k) for k in ["sh1", "sc1", "csh1", "csc1"]}
    qnw = load_vec(P, "qnw", 4)
    kvnw = load_vec(P, "kvnw", 4)
    cw = P.din("convw", [128, 12, 3])
    cwt = P.sb("cwt", [128, 12, 3])
    P.dma(cwt[:], cw[:], reads=[cw], writes=[cwt])
    cb = load_vec(P, "convb", 12)
    dtb_d = P.din("dtb", [32, 1])
    dtb = P.sb("dtb_sb", [32, 1])
    P.dma(dtb[:], dtb_d[:], reads=[dtb_d], writes=[dtb])
    hm = P.sb("hm", [128, 8])
    P.dma(hm[:], hmask[:], reads=[hmask], writes=[hm])
    zT = P.dout("zT", [1024, TPC])
    qnT = P.dout("qnT", [8, 128, TPC])
    qrT = P.dout("qrT", [8, 64, TPC])
    xbcT = P.dout("xbcT", [1536, TPC + CTX])
    dtT = P.dout("dtT", [32, TPC + CTX])
    kvT = P.dout("kvT", [2048, TPC + CTX])
    krT = P.dout("krT", [64, TPC + CTX])
    scl = {}
    for k, s in [("lat", "sc1"), ("ctx", "csc1")]:
        scl[k] = P.sb("scl_" + k, [128, 16])
        P.op('dve', lambda e: e.scalar_tensor_tensor(out=scl[k][:], in0=mods[s][:], scalar=1.0, in1=nmw[:], op0=ALU.add, op1=ALU.mult),
             reads=[mods[s], nmw], writes=[scl[k]])
    shf = {"lat": mods["sh1"], "ctx": mods["csh1"]}
    ws = WStream(P, maxk=4096)
    x = P.sb("x", [128, 16, TT])
    h = P.sb("h", [128, 16, TT], BF16)
    sq = P.sb("sq", [128, TT])
    rstd = P.sb("rstd", [128, TT])
    tmp = P.sb("tmp", [128, TT])
    tmp2 = P.sb("tmp2", [128, TT])
    xbc = P.sb("xbc", [128, 12, TT + 2])
    xbch = P.sb("xbch", [128, 12, 8])
    cq = P.sb("cq", [128, 4, TT])
    hq = P.sb("hq", [128, 4, TT], BF16)
    cost = P.sb("cost", [64, TT])
    sint = P.sb("sint", [64, TT])
    xv = xh[:, :].rearrange("(kc p) t -> p kc t", p=128)

    P.dma(x[:, :, 0:8], xv[:, :, 0:8], reads=[xh], writes=[x])
    emit_rmsnorm_mod(P, c, x, h, scl["lat"], shf["lat"], 8, sq, rstd)

    def ev_h(fc, ps, m):
        i = fc - XBC0 // 128
        P.op('dve', lambda e: e.tensor_tensor(out=xbch[:, i, :], in0=ps[:, 0:8], in1=hm[:], op=ALU.mult), reads=[ps, hm], writes=[xbch])
    emit_linear(P, ws, w_in, D, XBC0, DT0, h, 8, ev_h)

    def rope_out(psA, psB, n, dst_ap, dst_tr, scale):
        P.op('dve', lambda e: e.tensor_tensor(out=tmp[:64, :n], in0=psA[:64, :n], in1=cost[:, :n], op=ALU.mult), reads=[psA, cost], writes=[tmp])
        P.op('dve', lambda e: e.tensor_tensor(out=tmp2[:64, :n], in0=psB[:64, :n], in1=sint[:, :n], op=ALU.mult), reads=[psB, sint], writes=[tmp2])
        P.op('dve', lambda e: e.scalar_tensor_tensor(out=tmp[:64, :n], in0=tmp[:64, :n], scalar=scale, in1=tmp2[:64, :n], op0=ALU.mult, op1=ALU.add),
             reads=[tmp, tmp2], writes=[tmp])
        P.dma(dst_ap, tmp[:64, :n], reads=[tmp], writes=[dst_tr], is_out=True)

    segs = [("lat", 8 + TT * j, TT, TT * j, j) for j in range(TPC // TT)] + [("ctx", 8 + TPC, CTX, TPC, None)]
    for kind, c0, n, oc, hj in segs:
        P.dma(x[:, :, :n], xv[:, :, c0:c0 + n], reads=[xh], writes=[x])
        P.dma(cost[:, :n], cosd[:, oc:oc + n], reads=[cosd], writes=[cost])
        P.dma(sint[:, :n], sind[:, oc:oc + n], reads=[sind], writes=[sint])
        emit_rmsnorm_mod(P, c, x, h, scl[kind], shf[kind], n, sq, rstd)
        if kind == "lat":
            def ev_z(fc, ps, m):
                P.op('act', lambda e: e.copy(out=x[:, fc, :n], in_=ps[:, :n]), reads=[ps], writes=[x])
            emit_linear(P, ws, w_in, D, 0, 1024, h, n, ev_z)
            P.dma(zT[:, oc:oc + n].rearrange("(c p) t -> p c t", p=128), x[:, 0:8, :n], reads=[x], writes=[zT], is_out=True)
            def ev_cq(fc, ps, m):
                P.op('act', lambda e: e.copy(out=cq[:, fc - 8, :n], in_=ps[:, :n]), reads=[ps], writes=[cq])
            emit_linear(P, ws, w_in, D, 1024, 1536, h, n, ev_cq)
            emit_rmsnorm_mod(P, c, cq, hq, qnw, None, n, sq, rstd, nch=4)
            for hh in range(8):
                def ev_qn(fc, ps, m):
                    P.op('act', lambda e: e.activation(out=x[:, hh, :n], in_=ps[:, :n], func=AF.Copy, scale=QSCALE), reads=[ps], writes=[x])
                emit_linear(P, ws, w_uq, 512, 192 * hh, 192 * hh + 128, hq, n, ev_qn)
            P.dma(qnT[:, :, oc:oc + n].rearrange("h p t -> p h t"), x[:, 0:8, :n], reads=[x], writes=[qnT], is_out=True)
            for hh in range(8):
                got = {}

                def ev_a(fc, ps, m):
                    got['a'] = ps

                def ev_b(fc, ps, m):
                    got['b'] = ps
                emit_linear(P, ws, w_uq, 512, 192 * hh + 128, 192 * hh + 192, hq, n, ev_a)
                emit_linear(P, ws, w_uq_sw, 512, 64 * hh, 64 * hh + 64, hq, n, ev_b)
                P.op('dve', lambda e: e.tensor_scalar(out=tmp2[:64, :n], in0=got['b'][:64, :n], scalar1=QSCALE, scalar2=None, op0=ALU.mult), reads=[got['b']], writes=[tmp2])
                P.op('dve', lambda e: e.tensor_tensor(out=tmp2[:64, :n], in0=tmp2[:64, :n], in1=sint[:, :n], op=ALU.mult), reads=[tmp2, sint], writes=[tmp2])
                P.op('dve', lambda e: e.tensor_tensor(out=tmp[:64, :n], in0=got['a'][:64, :n], in1=cost[:, :n], op=ALU.mult), reads=[got['a'], cost], writes=[tmp])
                P.op('dve', lambda e: e.scalar_tensor_tensor(out=tmp[:64, :n], in0=tmp[:64, :n], scalar=QSCALE, in1=tmp2[:64, :n], op0=ALU.mult, op1=ALU.add),
                     reads=[tmp, tmp2], writes=[tmp])
                P.dma(qrT[hh, :, oc:oc + n], tmp[:64, :n], reads=[tmp], writes=[qrT], is_out=True)
        def ev_x(fc, ps, m):
            P.op('act', lambda e: e.copy(out=xbc[:, fc - XBC0 // 128, 1:n + 1], in_=ps[:, :n]), reads=[ps], writes=[xbc])
        emit_linear(P, ws, w_in, D, XBC0, DT0, h, n, ev_x)
        if kind == "lat":
            P.op('dve', lambda e: e.tensor_copy(out=xbc[:, :, 0:1], in_=xbch[:, :, 2 * hj:2 * hj + 1]), reads=[xbch], writes=[xbc])
            P.op('dve', lambda e: e.tensor_copy(out=xbc[:, :, n + 1:n + 2], in_=xbch[:, :, 2 * hj + 1:2 * hj + 2]), reads=[xbch], writes=[xbc])
        else:
            P.op('dve', lambda e: e.memset(xbc[:, :, 0:1], 0.0), writes=[xbc])
            P.op('dve', lambda e: e.memset(xbc[:, :, n + 1:n + 2], 0.0), writes=[xbc])
        for ci in range(12):
            P.op('dve', lambda e: e.tensor_scalar(out=tmp[:, :n], in0=xbc[:, ci, 0:n], scalar1=cwt[:, ci, 0:1], scalar2=None, op0=ALU.mult), reads=[xbc, cwt], writes=[tmp])
            P.op('dve', lambda e: e.scalar_tensor_tensor(out=tmp[:, :n], in0=xbc[:, ci, 1:n + 1], scalar=cwt[:, ci, 1:2], in1=tmp[:, :n], op0=ALU.mult, op1=ALU.add),
                 reads=[xbc, cwt, tmp], writes=[tmp])
            P.op('dve', lambda e: e.scalar_tensor_tensor(out=tmp[:, :n], in0=xbc[:, ci, 2:n + 2], scalar=cwt[:, ci, 2:3], in1=tmp[:, :n], op0=ALU.mult, op1=ALU.add),
                 reads=[xbc, cwt, tmp], writes=[tmp])
            P.op('act', lambda e: e.activation(out=x[:, ci, :n], in_=tmp[:, :n], func=AF.Silu, bias=cb[:, ci:ci + 1], scale=1.0), reads=[tmp, cb], writes=[x])
        P.dma(xbcT[:, oc:oc + n].rearrange("(c p) t -> p c t", p=128), x[:, 0:12, :n], reads=[x], writes=[xbcT], is_out=True)
        def ev_dt(fc, ps, m):
            P.op('act', lambda e: e.activation(out=tmp[:32, :n], in_=ps[:32, :n], func=AF.Exp, bias=dtb[:, 0:1], scale=1.0), reads=[ps, dtb], writes=[tmp])
            P.op('act', lambda e: e.activation(out=tmp[:32, :n], in_=tmp[:32, :n], func=AF.Ln, bias=1.0, scale=1.0), reads=[tmp], writes=[tmp])
            P.dma(dtT[:, oc:oc + n], tmp[:32, :n], reads=[tmp], writes=[dtT], is_out=True)
        emit_linear(P, ws, w_in, D, DT0, CKV0, h, n, ev_dt)
        def ev_ckv(fc_unused, ps, m, _st=[0]):
            i = _st[0] % 4
            _st[0] += 1
            P.op('act', lambda e: e.copy(out=cq[:, i, :n], in_=ps[:, :n]), reads=[ps], writes=[cq])
        emit_linear(P, ws, w_in, D, CKV0, KR0, h, n, ev_ckv)
        emit_rmsnorm_mod(P, c, cq, hq, kvnw, None, n, sq, rstd, nch=4)

        def ev_kv(fc, ps, m):
            P.op('act', lambda e: e.copy(out=x[:, fc, :n], in_=ps[:, :n]), reads=[ps], writes=[x])
        emit_linear(P, ws, w_ukv, 512, 0, 2048, hq, n, ev_kv)
        P.dma(kvT[:, oc:oc + n].rearrange("(c p) t -> p c t", p=128), x[:, :, :n], reads=[x], writes=[kvT], is_out=True)
        got = {}

        def ev_a(fc, ps, m):
            got['a'] = ps

        def ev_b(fc, ps, m):
            got['b'] = ps
        emit_linear(P, ws, w_in, D, KR0, 3680, h, n, ev_a)
        emit_linear(P, ws, w_in_sw, D, 0, 64, h, n, ev_b)
        P.op('dve', lambda e: e.tensor_tensor(out=tmp2[:64, :n], in0=got['b'][:64, :n], in1=sint[:, :n], op=ALU.mult), reads=[got['b'], sint], writes=[tmp2])
        P.op('dve', lambda e: e.tensor_tensor(out=tmp[:64, :n], in0=got['a'][:64, :n], in1=cost[:, :n], op=ALU.mult), reads=[got['a'], cost], writes=[tmp])
        P.op('dve', lambda e: e.tensor_tensor(out=tmp[:64, :n], in0=tmp[:64, :n], in1=tmp2[:64, :n], op=ALU.add), reads=[tmp, tmp2], writes=[tmp])
        P.dma(krT[:, oc:oc + n], tmp[:64, :n], reads=[tmp], writes=[krT], is_out=True)
    return P


def rope_tables(t0, n):
    t = np.arange(t0, t0 + n)
    row = (t // 64).astype(np.float32)
    col = (t % 64).astype(np.float32)
    inv = (np.float32(10000.0) ** (-np.arange(16, dtype=np.float32) / np.float32(16))).astype(np.float32)
    cosT = np.zeros((64, n), np.float32)
    sinT = np.zeros((64, n), np.float32)
    for d in range(64):
        pos = row if d < 32 else col
        ang = (pos * inv[d % 16]).astype(np.float32)
        cosT[d] = np.cos(ang)
        sinT[d] = np.sin(ang) * (-1.0 if (d % 32) < 16 else 1.0)
    return cosT, sinT


def swap_cols64(w):
    idx = np.array([d + 16 if (d % 32) < 16 else d - 16 for d in range(64)])
    return np.ascontiguousarray(w[..., idx])


def split_mod(m):
    return [np.ascontiguousarray(m[i * D:(i + 1) * D]) for i in range(6)]


def run_l1(inp, mod):
    P = build_l1()
    x = inp['x'][0]
    ctx = inp['ctx'][0]
    xT = np.ascontiguousarray(x.T)
    ctxT = np.ascontiguousarray(ctx.T)
    sh1, sc1 = split_mod(mod[0, 0])[:2]
    csh1, csc1 = split_mod(mod[0, 1])[:2]
    w_in = np.ascontiguousarray(inp['ev_w_in'][0])
    w_uq = np.ascontiguousarray(inp['ev_w_uq'][0])
    w_uq_sw = np.concatenate([swap_cols64(w_uq[:, 192 * h + 128:192 * h + 192]) for h in range(8)], axis=1)
    common = {
        "w_in": w_in, "w_in_sw": swap_cols64(w_in[:, KR0:3680]), "w_uq": w_uq, "w_uq_sw": np.ascontiguousarray(w_uq_sw),
        "w_ukv": np.ascontiguousarray(inp['ev_w_ukv'][0]),
        "nmw": vec_layout(inp['norm_mix_w'][0]), "sh1": vec_layout(sh1), "sc1": vec_layout(sc1),
        "csh1": vec_layout(csh1), "csc1": vec_layout(csc1),
        "qnw": vec_layout(inp['ev_q_norm_w'][0]), "kvnw": vec_layout(inp['ev_kv_norm_w'][0]),
        "convw": np.ascontiguousarray(inp['ev_conv_w'][0].T.reshape(12, 128, 3).transpose(1, 0, 2)),
        "convb": vec_layout(inp['ev_conv_b'][0]),
        "dtb": np.ascontiguousarray(inp['ev_dt_bias'][0].reshape(32, 1)),
    }
    in_maps = []
    for i in range(NCORE):
        t0 = i * TPC
        halo = np.zeros((D, 8), np.float32)
        hmask = np.zeros((128, 8), np.float32)
        for j in range(4):
            for s, tok in ((0, t0 + TT * j - 1), (1, t0 + TT * j + TT)):
                if 0 <= tok < SEQ:
                    halo[:, 2 * j + s] = xT[:, tok]
                    hmask[:, 2 * j + s] = 1.0
        xh = np.concatenate([halo, xT[:, t0:t0 + TPC], ctxT], axis=1)
        cosT, sinT = rope_tables(t0, TPC)
        cosT = np.concatenate([cosT, np.ones((64, CTX), np.float32)], axis=1)
        sinT = np.concatenate([sinT, np.zeros((64, CTX), np.float32)], axis=1)
        m = dict(common)
        m.update({"xh": np.ascontiguousarray(xh), "hmask": hmask, "cosT": np.ascontiguousarray(cosT), "sinT": np.ascontiguousarray(sinT)})
        in_maps.append(m)
    res = run_spmd(P, in_maps)
    o = {}
    o['zT'] = np.concatenate([r['zT'] for r in res], axis=1)
    o['qnT'] = np.concatenate([r['qnT'] for r in res], axis=2)
    o['qrT'] = np.concatenate([r['qrT'] for r in res], axis=2)
    for k in ['xbcT', 'dtT', 'kvT', 'krT']:
        lat = np.concatenate([r[k][:, :TPC] for r in res], axis=1)
        o[k] = np.concatenate([res[0][k][:, TPC:], lat], axis=1)
    return o


NT = CTX + SEQ
NCH = NT // 128
QB = 512


def build_l2():
    P = Prog()
    P.psum_ring = 6
    c = emit_consts(P)
    ring = P.psum[:6]
    state = {'n': 0}

    def ps():
        p = ring[state['n'] % 6]
        state['n'] += 1
        return p
    ps_o, ps_s = P.psum[6], P.psum[7]
    din = {}
    for d_ in "fb":
        din["xs_" + d_] = P.din("xs_" + d_, [NT, 128])
        din["dt_" + d_] = P.din("dt_" + d_, [NT, 2])
        din["Bt_" + d_] = P.din("Bt_" + d_, [128, NT])
        din["Ct_" + d_] = P.din("Ct_" + d_, [128, NT])
        din["Bk_" + d_] = P.din("Bk_" + d_, [NT, 128])
    alog = P.din("alog", [128, 4])
    tri_d = P.din("tri", [128, 128])
    mneg_d = P.din("mneg", [128, 128])
    qa_d = P.din("qa", [128, SEQ])
    qb_d = P.din("qb", [64, SEQ])
    ka_d = P.din("ka", [128, NT])
    kb_d = P.din("kb", [64, NT])
    v_d = P.din("v", [NT, 128])
    y_out = {d_: P.dout("y_" + d_, [SEQ, 128]) for d_ in "fb"}
    oT = P.dout("oT", [128, SEQ])
    tri = P.sb("tri_sb", [128, 128])
    mneg = P.sb("mneg_sb", [128, 128])
    P.dma(tri[:], tri_d[:], reads=[tri_d], writes=[tri])
    P.dma(mneg[:], mneg_d[:], reads=[mneg_d], writes=[mneg])
    A = P.sb("A_sb", [128, 4])
    P.dma(A[:], alog[:], reads=[alog], writes=[A])
    P.op('act', lambda e: e.activation(out=A[:], in_=A[:], func=AF.Exp), reads=[A], writes=[A])
    P.op('dve', lambda e: e.tensor_scalar(out=A[:], in0=A[:], scalar1=-1.0, scalar2=None, op0=ALU.mult), reads=[A], writes=[A])
    ones_b = P.sb("ones_b", [128, 128], BF16)
    P.op('dve', lambda e: e.memset(ones_b[:], 1.0), writes=[ones_b])

    hst = {}
    hbf = {}
    for d_ in "fb":
        for hh in range(2):
            hst[d_, hh] = P.sb("h_%s%d" % (d_, hh), [128, 64])
            hbf[d_, hh] = P.sb("hb_%s%d" % (d_, hh), [128, 64], BF16)
            P.op('dve', lambda e: e.memset(hst[d_, hh][:], 0.0), writes=[hst[d_, hh]])
            P.op('dve', lambda e: e.memset(hbf[d_, hh][:], 0.0), writes=[hbf[d_, hh]])
    NB = 2
    tl = []
    for b in range(NB):
        t = {}
        for nm, shp, dt_ in [("xc", [128, 128], F32), ("dtc", [128, 2], F32), ("btc", [128, 128], F32), ("ctc", [128, 128], F32), ("bkc", [128, 128], F32),
                             ("bt_bf", [128, 128], BF16), ("ct_bf", [128, 128], BF16), ("bk_bf", [128, 128], BF16),
                             ("a_t", [128, 2], F32), ("nacum", [128, 2], F32), ("eac", [128, 2], F32), ("ysb", [128, 128], F32)]:
            t[nm] = P.sb("%s_%d" % (nm, b), shp, dt_)
        for hh in range(2):
            for nm, shp, dt_ in [("abc", [128, 128], F32), ("alast", [128, 1], F32), ("seg", [128, 128], F32), ("Lm", [128, 128], F32),
                                 ("M", [128, 128], BF16), ("xdt", [128, 64], BF16), ("yd", [128, 64], F32), ("toend", [128, 1], F32),
                                 ("w2", [128, 1], F32), ("xw", [128, 64], BF16), ("cd", [128, 1], F32)]:
                t[nm, hh] = P.sb("%s_%d_%d" % (nm, b, hh), shp, dt_)
        tl.append(t)

    def ssd_unit(d_, ci, t):
        di = 0 if d_ == "f" else 1
        t0 = ci * 128
        lat = ci >= CTX // 128
        P.dma(t["xc"][:], din["xs_" + d_][t0:t0 + 128, :], reads=[din["xs_" + d_]], writes=[t["xc"]])
        P.dma(t["dtc"][:], din["dt_" + d_][t0:t0 + 128, :], reads=[din["dt_" + d_]], writes=[t["dtc"]])
        P.dma(t["btc"][:], din["Bt_" + d_][:, t0:t0 + 128], reads=[din["Bt_" + d_]], writes=[t["btc"]])
        P.dma(t["ctc"][:], din["Ct_" + d_][:, t0:t0 + 128], reads=[din["Ct_" + d_]], writes=[t["ctc"]])
        P.dma(t["bkc"][:], din["Bk_" + d_][t0:t0 + 128, :], reads=[din["Bk_" + d_]], writes=[t["bkc"]])
        for s, dd in (("btc", "bt_bf"), ("ctc", "ct_bf"), ("bkc", "bk_bf")):
            P.op('pool', lambda e: e.tensor_copy(out=t[dd][:], in_=t[s][:]), reads=[t[s]], writes=[t[dd]])
        P.op('dve', lambda e: e.tensor_tensor(out=t["a_t"][:], in0=t["dtc"][:], in1=A[:, 2 * di:2 * di + 2], op=ALU.mult), reads=[t["dtc"], A], writes=[t["a_t"]])
        p_acj = ps()
        P.op('pe', lambda e: e.matmul(p_acj[:, 0:2], tri[:], t["a_t"][:], start=True, stop=True), reads=[tri, t["a_t"]], writes=[p_acj])
        P.op('dve', lambda e: e.tensor_scalar(out=t["nacum"][:], in0=p_acj[:, 0:2], scalar1=-1.0, scalar2=None, op0=ALU.mult), reads=[p_acj], writes=[t["nacum"]])
        P.op('act', lambda e: e.activation(out=t["eac"][:], in_=p_acj[:, 0:2], func=AF.Exp), reads=[p_acj], writes=[t["eac"]])
        if lat:
            p_g = ps()
            P.op('pe', lambda e: e.matmul(p_g[:, 0:128], t["bt_bf"][:], t["ct_bf"][:], start=True, stop=True), reads=[t["bt_bf"], t["ct_bf"]], writes=[p_g])
        for hh in range(2):
            hs = slice(64 * hh, 64 * hh + 64)
            abc, alast, seg, Lm, M, xdt, yd, toend, w2, xw, cd = [t[nm, hh] for nm in ("abc", "alast", "seg", "Lm", "M", "xdt", "yd", "toend", "w2", "xw", "cd")]
            P.op('dve', lambda e: e.tensor_scalar(out=abc[:], in0=c['ones_f'][:], scalar1=t["a_t"][:, hh:hh + 1], scalar2=None, op0=ALU.mult),
                 reads=[c['ones_f'], t["a_t"]], writes=[abc])
            p_row = ps()
            P.op('pe', lambda e: e.matmul(p_row[:, 0:128], abc[:], tri[:], start=True, stop=True), reads=[abc, tri], writes=[p_row])
            P.op('act', lambda e: e.copy(out=alast[:], in_=p_row[:, 127:128]), reads=[p_row], writes=[alast])
            if lat:
                P.op('dve', lambda e: e.tensor_tensor(out=seg[:], in0=p_row[:, 0:128], in1=mneg[:], op=ALU.add), reads=[p_row, mneg], writes=[seg])
                P.op('act', lambda e: e.activation(out=Lm[:], in_=seg[:], func=AF.Exp, bias=t["nacum"][:, hh:hh + 1], scale=1.0), reads=[seg, t["nacum"]], writes=[Lm])
                P.op('dve', lambda e: e.tensor_tensor(out=M[:], in0=p_g[:, 0:128], in1=Lm[:], op=ALU.mult), reads=[p_g, Lm], writes=[M])
                P.op('dve', lambda e: e.tensor_scalar(out=xdt[:], in0=t["xc"][:, hs], scalar1=t["dtc"][:, hh:hh + 1], scalar2=None, op0=ALU.mult),
                     reads=[t["xc"], t["dtc"]], writes=[xdt])
                p_yd = ps()
                P.op('pe', lambda e: e.matmul(p_yd[:, 0:64], M[:], xdt[:], start=True, stop=True), reads=[M, xdt], writes=[p_yd])
                p_yo = ps()
                P.op('pe', lambda e: e.matmul(p_yo[:, 0:64], t["ct_bf"][:], hbf[d_, hh][:], start=True, stop=True), reads=[t["ct_bf"], hbf[d_, hh]], writes=[p_yo])
                P.op('act', lambda e: e.copy(out=yd[:], in_=p_yd[:, 0:64]), reads=[p_yd], writes=[yd])
                P.op('dve', lambda e: e.scalar_tensor_tensor(out=t["ysb"][:, hs], in0=p_yo[:, 0:64], scalar=t["eac"][:, hh:hh + 1], in1=yd[:], op0=ALU.mult, op1=ALU.add),
                     reads=[p_yo, t["eac"], yd], writes=[t["ysb"]])
            P.op('act', lambda e: e.activation(out=toend[:], in_=t["nacum"][:, hh:hh + 1], func=AF.Exp, bias=alast[:, 0:1], scale=1.0), reads=[t["nacum"], alast], writes=[toend])
            P.op('dve', lambda e: e.tensor_tensor(out=w2[:], in0=toend[:], in1=t["dtc"][:, hh:hh + 1], op=ALU.mult), reads=[toend, t["dtc"]], writes=[w2])
            P.op('dve', lambda e: e.tensor_scalar(out=xw[:], in0=t["xc"][:, hs], scalar1=w2[:, 0:1], scalar2=None, op0=ALU.mult), reads=[t["xc"], w2], writes=[xw])
            p_st = ps()
            P.op('pe', lambda e: e.matmul(p_st[:, 0:64], t["bk_bf"][:], xw[:], start=True, stop=True), reads=[t["bk_bf"], xw], writes=[p_st])
            P.op('act', lambda e: e.activation(out=cd[:], in_=alast[:], func=AF.Exp), reads=[alast], writes=[cd])
            P.op('dve', lambda e: e.scalar_tensor_tensor(out=hst[d_, hh][:], in0=hst[d_, hh][:], scalar=cd[:, 0:1], in1=p_st[:, 0:64], op0=ALU.mult, op1=ALU.add),
                 reads=[hst[d_, hh], cd, p_st], writes=[hst[d_, hh]])
            P.op('pool', lambda e: e.tensor_copy(out=hbf[d_, hh][:], in_=hst[d_, hh][:]), reads=[hst[d_, hh]], writes=[hbf[d_, hh]])
        if lat:
            o0 = (ci - CTX // 128) * 128
            P.dma(y_out[d_][o0:o0 + 128, :], t["ysb"][:], reads=[t["ysb"]], writes=[y_out[d_]], is_out=True)

    def ssd_gen():
        n = 0
        for ci in range(NCH):
            for d_ in "fb":
                ssd_unit(d_, ci, tl[n % NB])
                n += 1
                yield

    ka = P.sb("ka_bf", [128, NT], BF16)
    kb = P.sb("kb_bf", [64, NT], BF16)
    vb = P.sb("v_bf", [128, NCH, 128], BF16)
    stg = [P.sb("astg%d" % i, [128, 1280]) for i in range(2)]
    qa = [P.sb("qa%d" % i, [128, QB], BF16) for i in range(2)]
    qb = [P.sb("qb%d" % i, [64, QB], BF16) for i in range(2)]
    Et = [P.sb("E%d" % i, [128, QB], BF16) for i in range(3)]
    osb = P.sb("osb", [128, QB])
    rs = P.sb("rs", [128, QB])

    def attn_gen():
        n = 0
        CH = 1040
        for i in range(NT // CH):
            s = stg[n % 2]
            n += 1
            P.dma(s[:, :CH], ka_d[:, i * CH:(i + 1) * CH], reads=[ka_d], writes=[s])
            P.op('act', lambda e: e.copy(out=ka[:, i * CH:(i + 1) * CH], in_=s[:, :CH]), reads=[s], writes=[ka])
            s = stg[n % 2]
            n += 1
            P.dma(s[:64, :CH], kb_d[:, i * CH:(i + 1) * CH], reads=[kb_d], writes=[s])
            P.op('act', lambda e: e.copy(out=kb[:, i * CH:(i + 1) * CH], in_=s[:64, :CH]), reads=[s], writes=[kb])
        vv = v_d[:, :].rearrange("(kt p) d -> p kt d", p=128)
        for i in range(NCH // 10):
            s = stg[n % 2]
            n += 1
            P.dma(s[:, :1280].rearrange("p (k d) -> p k d", k=10), vv[:, i * 10:(i + 1) * 10, :], reads=[v_d], writes=[s])
            P.op('act', lambda e: e.copy(out=vb[:, i * 10:(i + 1) * 10, :], in_=s[:, :1280].rearrange("p (k d) -> p k d", k=10)), reads=[s], writes=[vb])
        yield
        ne = 0
        for qi in range(SEQ // QB):
            q0 = qi * QB
            qat, qbt = qa[qi % 2], qb[qi % 2]
            s = stg[n % 2]
            n += 1
            P.dma(s[:, :QB], qa_d[:, q0:q0 + QB], reads=[qa_d], writes=[s])
            P.op('act', lambda e: e.copy(out=qat[:], in_=s[:, :QB]), reads=[s], writes=[qat])
            s = stg[n % 2]
            n += 1
            P.dma(s[:64, :QB], qb_d[:, q0:q0 + QB], reads=[qb_d], writes=[s])
            P.op('act', lambda e: e.copy(out=qbt[:], in_=s[:64, :QB]), reads=[s], writes=[qbt])
            for kt in range(NCH):
                p_s = ps()
                P.op('pe', lambda e: e.matmul(p_s[:, :QB], ka[:, kt * 128:(kt + 1) * 128], qat[:], start=True, stop=False), reads=[ka, qat], writes=[p_s])
                P.op('pe', lambda e: e.matmul(p_s[:, :QB], kb[:, kt * 128:(kt + 1) * 128], qbt[:], start=False, stop=True), reads=[kb, qbt], writes=[p_s])
                E = Et[ne % 3]
                ne += 1
                P.op('act', lambda e: e.activation(out=E[:], in_=p_s[:, :QB], func=AF.Exp), reads=[p_s], writes=[E])
                P.op('pe', lambda e: e.matmul(ps_o[:, :QB], vb[:, kt, :], E[:], start=(kt == 0), stop=(kt == NCH - 1)), reads=[vb, E], writes=[ps_o])
                P.op('pe', lambda e: e.matmul(ps_s[:, :QB], ones_b[:], E[:], start=(kt == 0), stop=(kt == NCH - 1)), reads=[ones_b, E], writes=[ps_s])
                yield
            P.op('dve', lambda e: e.reciprocal(out=rs[:], in_=ps_s[:, :QB]), reads=[ps_s], writes=[rs])
            P.op('dve', lambda e: e.tensor_tensor(out=osb[:], in0=ps_o[:, :QB], in1=rs[:], op=ALU.mult), reads=[ps_o, rs], writes=[osb])
            P.dma(oT[:, q0:q0 + QB], osb[:], reads=[osb], writes=[oT], is_out=True)

    ag = attn_gen()
    sg = ssd_gen()
    a_done = s_done = False
    k = 0
    while not (a_done and s_done):
        if not a_done:
            try:
                next(ag)
            except StopIteration:
                a_done = True
        k += 1
        if (k % 16 == 0 or a_done) and not s_done:
            try:
                next(sg)
            except StopIteration:
                s_done = True
    return P


def run_l2(inp, o1):
    P = build_l2()
    xbcT, dtT, kvT, krT = o1['xbcT'], o1['dtT'], o1['kvT'], o1['krT']
    idx_b = np.concatenate([np.arange(CTX - 1, -1, -1), CTX + np.arange(SEQ - 1, -1, -1)])
    a_log = inp['ev_a_log'][0]
    tri = np.triu(np.ones((128, 128), np.float32))
    mneg = np.where(np.arange(128)[:, None] <= np.arange(128)[None, :], 0.0, -30000.0).astype(np.float32)
    in_maps = []
    for i in range(NCORE):
        g = i // 4
        xs_f = np.ascontiguousarray(xbcT[128 * i:128 * i + 128, :].T)
        dt_f = np.ascontiguousarray(dtT[2 * i:2 * i + 2, :].T)
        dt_b = np.ascontiguousarray(dtT[16 + 2 * i:16 + 2 * i + 2, :].T[idx_b])
        Bt_f = np.ascontiguousarray(xbcT[1024 + 128 * g:1024 + 128 * g + 128, :])
        Ct_f = np.ascontiguousarray(xbcT[1280 + 128 * g:1280 + 128 * g + 128, :])
        Bt_b = np.ascontiguousarray(Bt_f[:, idx_b])
        Ct_b = np.ascontiguousarray(Ct_f[:, idx_b])
        al = np.array([a_log[0, 2 * i], a_log[0, 2 * i + 1], a_log[1, 2 * i], a_log[1, 2 * i + 1]], np.float32)
        m = {
            "xs_f": xs_f, "xs_b": np.ascontiguousarray(xs_f[idx_b]), "dt_f": dt_f, "dt_b": dt_b,
            "Bt_f": Bt_f, "Bt_b": Bt_b, "Ct_f": Ct_f, "Ct_b": Ct_b,
            "Bk_f": np.ascontiguousarray(Bt_f.T), "Bk_b": np.ascontiguousarray(Bt_b.T),
            "alog": np.ascontiguousarray(np.tile(al[None, :], (128, 1))), "tri": tri, "mneg": mneg,
            "qa": np.ascontiguousarray(o1['qnT'][i]), "qb": np.ascontiguousarray(o1['qrT'][i]),
            "ka": np.ascontiguousarray(kvT[256 * i:256 * i + 128, :]), "kb": np.ascontiguousarray(krT),
            "v": np.ascontiguousarray(kvT[256 * i + 128:256 * i + 256, :].T),
        }
        in_maps.append(m)
    res = run_spmd(P, in_maps)
    o = {}
    o['yfT'] = np.ascontiguousarray(np.concatenate([r['y_f'] for r in res], axis=1).T)
    o['ybT'] = np.ascontiguousarray(np.concatenate([r['y_b'][::-1] for r in res], axis=1).T)
    o['oT'] = np.concatenate([r['oT'] for r in res], axis=0)
    return o


def build_l4():
    P = Prog()
    c = emit_consts(P)
    xT = P.din("xT", [D, TPC])
    srcs = {k: P.din(k, [1024, TPC]) for k in ["zT", "yfT", "ybT", "xsT", "oT"]}
    dsk_d = P.din("dsk", [128, 8, 2])
    w_o = P.din("w_o", [D, D])
    w1 = P.din("w1", [D, DFF])
    w3 = P.din("w3", [D, DFF])
    w2 = P.din("w2", [DFF, D])
    hw = P.din("hy_w_in", [D, 3 * D])
    x2T = P.dout("x2T", [D, TPC])
    projT = P.dout("projT", [3 * D, TPC])
    snw = load_vec(P, "snw", 8)
    g1 = load_vec(P, "g1")
    nfw = load_vec(P, "nfw")
    sc2 = load_vec(P, "sc2")
    sh2 = load_vec(P, "sh2")
    g2 = load_vec(P, "g2")
    nmw1 = load_vec(P, "nmw1")
    sc1b = load_vec(P, "sc1b")
    sh1b = load_vec(P, "sh1b")
    hb = load_vec(P, "hb", 48)
    dsk2 = P.sb("dsk2", [128, 8, 2])
    P.dma(dsk2[:], dsk_d[:], reads=[dsk_d], writes=[dsk2])
    dsk = P.sb("dsk_sum", [128, 8])
    P.op('dve', lambda e: e.tensor_tensor(out=dsk[:], in0=dsk2[:, :, 0], in1=dsk2[:, :, 1], op=ALU.add), reads=[dsk2], writes=[dsk])
    scl2 = P.sb("scl2", [128, 16])
    P.op('dve', lambda e: e.scalar_tensor_tensor(out=scl2[:], in0=sc2[:], scalar=1.0, in1=nfw[:], op0=ALU.add, op1=ALU.mult), reads=[sc2, nfw], writes=[scl2])
    scl1b = P.sb("scl1b", [128, 16])
    P.op('dve', lambda e: e.scalar_tensor_tensor(out=scl1b[:], in0=sc1b[:], scalar=1.0, in1=nmw1[:], op0=ALU.add, op1=ALU.mult), reads=[sc1b, nmw1], writes=[scl1b])
    ws = WStream(P)
    x = P.sb("x", [128, 16, TT])
    h = P.sb("h", [128, 16, TT], BF16)
    u = P.sb("u", [128, DFF // 128, TT], BF16)
    gy = P.sb("gy", [128, 8, TT])
    sq = P.sb("sq", [128, TT])
    rstd = P.sb("rstd", [128, TT])
    tmp = P.sb("tmp", [128, TT])
    ld = {k: P.sb("ld_" + k, [128, TT]) for k in ["zT", "yfT", "ybT", "xsT"]}
    stage = [P.sb("stage%d" % i, [128, TT]) for i in range(2)]
    xv = xT[:, :].rearrange("(kc p) t -> p kc t", p=128)
    x2v = x2T[:, :].rearrange("(kc p) t -> p kc t", p=128)
    sv = {k: v[:, :].rearrange("(kc p) t -> p kc t", p=128) for k, v in srcs.items()}
    pv = projT[:, :].rearrange("(kc p) t -> p kc t", p=128)
    ns = 0
    for tt in range(TPC // TT):
        t0 = tt * TT
        P.dma(x[:], xv[:, :, t0:t0 + TT], reads=[xT], writes=[x])
        for kc in range(8):
            for k in ["zT", "yfT", "ybT", "xsT"]:
                P.dma(ld[k][:], sv[k][:, kc, t0:t0 + TT], reads=[srcs[k]], writes=[ld[k]])
            P.op('dve', lambda e: e.tensor_tensor(out=tmp[:], in0=ld["yfT"][:], in1=ld["ybT"][:], op=ALU.add), reads=[ld["yfT"], ld["ybT"]], writes=[tmp])
            P.op('dve', lambda e: e.scalar_tensor_tensor(out=tmp[:], in0=ld["xsT"][:], scalar=dsk[:, kc:kc + 1], in1=tmp[:], op0=ALU.mult, op1=ALU.add),
                 reads=[ld["xsT"], dsk, tmp], writes=[tmp])
            P.op('act', lambda e: e.activation(out=sq[:], in_=ld["zT"][:], func=AF.Silu), reads=[ld["zT"]], writes=[sq])
            P.op('dve', lambda e: e.tensor_tensor(out=gy[:, kc, :], in0=tmp[:], in1=sq[:], op=ALU.mult), reads=[tmp, sq], writes=[gy])
            P.dma(ld["zT"][:], sv["oT"][:, kc, t0:t0 + TT], reads=[srcs["oT"]], writes=[ld["zT"]])
            P.op('act', lambda e: e.copy(out=h[:, kc, :], in_=ld["zT"][:]), reads=[ld["zT"]], writes=[h])
        for g in range(2):
            emit_rmsnorm_mod(P, c, gy, h, snw, None, TT, sq, rstd, nch=4, dim=512, ki=4 * g, ko=8 + 4 * g, ks=4 * g)

        def ev_o(dc, ps, m):
            P.op('dve', lambda e: e.scalar_tensor_tensor(out=x[:, dc, :], in0=ps[:, :TT], scalar=g1[:, dc:dc + 1], in1=x[:, dc, :], op0=ALU.mult, op1=ALU.add),
                 reads=[ps, g1, x], writes=[x])
        emit_linear(P, ws, w_o, D, 0, D, h, TT, ev_o)
        emit_rmsnorm_mod(P, c, x, h, scl2, sh2, TT, sq, rstd)
        emit_ffn(P, c, ws, x, h, u, tmp, w1, w3, w2, g2, TT)
        P.dma(x2v[:, :, t0:t0 + TT], x[:], reads=[x], writes=[x2T], is_out=True)
        emit_rmsnorm_mod(P, c, x, h, scl1b, sh1b, TT, sq, rstd)

        def ev_p(fc, ps, m):
            nonlocal ns
            st = stage[ns % 2]
            ns += 1
            P.op('act', lambda e: e.activation(out=st[:], in_=ps[:, :TT], func=AF.Identity, bias=hb[:, fc:fc + 1], scale=1.0), reads=[ps, hb], writes=[st])
            P.dma(pv[:, fc, t0:t0 + TT], st[:], reads=[st], writes=[projT], is_out=True)
        emit_linear(P, ws, hw, D, 0, 3 * D, h, TT, ev_p)
    return P


def tok_shard(a, i):
    return np.ascontiguousarray(a[:, i * TPC:(i + 1) * TPC])


def run_l4(inp, mod, o1, o2):
    P = build_l4()
    xT = np.ascontiguousarray(inp['x'][0].T)
    m0 = split_mod(mod[0, 0])
    m1 = split_mod(mod[1, 0])
    dsk = np.repeat(inp['ev_d_skip'][0].T, 64, axis=0)
    common = {
        "dsk": np.ascontiguousarray(dsk.reshape(8, 128, 2).transpose(1, 0, 2)),
        "w_o": np.ascontiguousarray(inp['ev_w_o'][0]), "w1": np.ascontiguousarray(inp['ffn_w1'][0]), "w3": np.ascontiguousarray(inp['ffn_w3'][0]),
        "w2": np.ascontiguousarray(inp['ffn_w2'][0]), "hy_w_in": np.ascontiguousarray(inp['hy_w_in'][0]),
        "snw": vec_layout(inp['ev_ssd_norm_w'][0]), "g1": vec_layout(m0[2]), "nfw": vec_layout(inp['norm_ffn_w'][0]),
        "sc2": vec_layout(m0[4]), "sh2": vec_layout(m0[3]), "g2": vec_layout(m0[5]),
        "nmw1": vec_layout(inp['norm_mix_w'][1]), "sc1b": vec_layout(m1[1]), "sh1b": vec_layout(m1[0]),
        "hb": vec_layout(inp['hy_b_in'][0]),
    }
    xsT = o1['xbcT'][:1024, CTX:]
    in_maps = []
    for i in range(NCORE):
        m = dict(common)
        m.update({"xT": tok_shard(xT, i), "zT": tok_shard(o1['zT'], i), "yfT": tok_shard(o2['yfT'], i), "ybT": tok_shard(o2['ybT'], i),
                  "xsT": tok_shard(xsT, i), "oT": tok_shard(o2['oT'], i)})
        in_maps.append(m)
    res = run_spmd(P, in_maps)
    return {"x2T": np.concatenate([r['x2T'] for r in res], axis=1), "projT": np.concatenate([r['projT'] for r in res], axis=1)}


CPC = D // NCORE
CBK = 32
NFFT = 2 * SEQ
PI = float(np.pi)


def build_l5():
    P = Prog()
    c = emit_consts(P)
    pj = P.din("pj", [3, CPC, SEQ + 2])
    sw_d = P.din("sw", [128, 6, 3])
    sb_d = P.din("sb", [128, 6])
    embs = [P.din("embT", [33, SEQ]), P.din("embTr", [33, SEQ])]
    fw1_d = P.din("fw1", [33, 64])
    fwm_d = P.din("fwm", [2, 64, 64])
    fq_d = P.din("fq", [64, 1])
    fb_d = P.din("fb", [64, 3])
    fwo_d = P.din("fwo", [64, 2, 2, CPC])
    delta_d = P.din("delta", [128, CPC])
    tcols = [P.din("tcol", [128, 128]), P.din("tcolr", [128, 128])]
    fbias_d = P.din("fbias", [128, 2, CPC])
    cn = {}
    for nm, shp in [("W1", [2, 128, 512]), ("TWrr", [128, 512]), ("TWii", [128, 512]), ("W2r", [128, 128]), ("W2i", [128, 128]),
                    ("WIa", [128, 256]), ("WIb", [128, 256]), ("TWIrr", [2, 128, 256]), ("TWIii", [2, 128, 256]),
                    ("Cos", [2, 128, 128]), ("NSin", [2, 128, 128])]:
        cn[nm] = (P.din(nm, shp), shp)
    y2 = P.dout("y2", [CPC, SEQ])
    cv = P.dscratch("cv", [3, CPC, SEQ])
    kd = P.dscratch("kd", [2, 2, 128, CPC, 128])
    rn = P.sb("rn", [128, 2, CPC])
    fbias = P.sb("fbias_sb", [128, 2, CPC])
    P.dma(fbias[:], fbias_d[:], reads=[fbias_d], writes=[fbias])

    P.push()
    swt = P.sb("swt", [128, 6, 3])
    sbt = P.sb("sbt", [128, 6])
    P.dma(swt[:], sw_d[:], reads=[sw_d], writes=[swt])
    P.dma(sbt[:], sb_d[:], reads=[sb_d], writes=[sbt])
    HC = SEQ // 2
    U = [P.sb("U%d" % i, [128, HC + 2]) for i in range(2)]
    V = [P.sb("V%d" % i, [128, HC]) for i in range(2)]
    n = 0
    for s_ in range(3):
        for gq in range(2):
            for hc in range(2):
                u_, v_ = U[n % 2], V[n % 2]
                n += 1
                P.dma(u_[:], pj[s_, gq * 128:(gq + 1) * 128, hc * HC:hc * HC + HC + 2], reads=[pj], writes=[u_])
                k = s_ * 2 + gq
                for b0 in range(0, HC, 2048):
                    sl = slice(b0, b0 + 2048)
                    P.op('dve', lambda e: e.tensor_scalar(out=v_[:, sl], in0=u_[:, b0:b0 + 2048], scalar1=swt[:, k, 0:1], scalar2=None, op0=ALU.mult), reads=[u_, swt], writes=[v_])
                    P.op('dve', lambda e: e.scalar_tensor_tensor(out=v_[:, sl], in0=u_[:, b0 + 1:b0 + 2049], scalar=swt[:, k, 1:2], in1=v_[:, sl], op0=ALU.mult, op1=ALU.add),
                         reads=[u_, swt, v_], writes=[v_])
                    P.op('dve', lambda e: e.scalar_tensor_tensor(out=v_[:, sl], in0=u_[:, b0 + 2:b0 + 2050], scalar=swt[:, k, 2:3], in1=v_[:, sl], op0=ALU.mult, op1=ALU.add),
                         reads=[u_, swt, v_], writes=[v_])
                    P.op('act', lambda e: e.activation(out=v_[:, sl], in_=v_[:, sl], func=AF.Identity, bias=sbt[:, k:k + 1], scale=1.0), reads=[v_, sbt], writes=[v_])
                P.dma(cv[s_, gq * 128:(gq + 1) * 128, hc * HC:(hc + 1) * HC], v_[:], reads=[v_], writes=[cv])
    P.pop()

    P.push()
    fw1 = P.sb("fw1_sb", [33, 64])
    fwm = P.sb("fwm_sb", [64, 2, 64])
    fq = P.sb("fq_sb", [64, 1])
    fb = P.sb("fb_sb", [64, 3])
    fbq = P.sb("fbq", [64, 3])
    fwo = P.sb("fwo_sb", [64, 2, 2, CPC])
    delta = P.sb("delta_sb", [128, CPC])
    negpi = P.sb("negpi", [128, 1])
    P.op('dve', lambda e: e.memset(negpi[:], -PI), writes=[negpi])
    P.dma(fw1[:], fw1_d[:], reads=[fw1_d], writes=[fw1])
    P.dma(fwm[:], fwm_d[:].rearrange("j a b -> a j b"), reads=[fwm_d], writes=[fwm])
    P.dma(fq[:], fq_d[:], reads=[fq_d], writes=[fq])
    P.dma(fb[:], fb_d[:], reads=[fb_d], writes=[fb])
    P.dma(fwo[:], fwo_d[:], reads=[fwo_d], writes=[fwo])
    P.dma(delta[:], delta_d[:], reads=[delta_d], writes=[delta])
    P.op('dve', lambda e: e.tensor_scalar(out=fbq[:], in0=fb[:], scalar1=fq[:, 0:1], scalar2=None, op0=ALU.mult), reads=[fb, fq], writes=[fbq])
    tcol = [P.sb("tcol_sb%d" % i, [128, 128]) for i in range(2)]
    for i in range(2):
        P.dma(tcol[i][:], tcols[i][:], reads=[tcols[i]], writes=[tcol[i]])
    hid = P.sb("hid", [64, SEQ])
    embt = [P.sb("embt%d" % i, [33, 512]) for i in range(2)]
    arg = [P.sb("arg%d" % i, [64, 512]) for i in range(2)]
    hcur = [P.sb("hcur%d" % i, [64, 512]) for i in range(2)]
    kt = [P.sb("kt%d" % i, [128, CBK, 128]) for i in range(2)]
    win = [P.sb("win%d" % i, [128, CBK]) for i in range(2)]
    part = P.sb("part", [128, CBK])
    wr1 = P.sb("wr1", [64, 512])
    wr2 = P.sb("wr2", [64, 512])
    nk = 0
    nw = 0

    def sin_layer(ps, j, dst_ap, dst_tr, a):
        P.op('dve', lambda e: e.tensor_scalar(out=a[:], in0=ps[:64, :512], scalar1=fq[:, 0:1], scalar2=fbq[:, j:j + 1], op0=ALU.mult, op1=ALU.add), reads=[ps, fq, fbq], writes=[a])
        for _ in range(2):
            P.op('dve', lambda e: e.tensor_scalar(out=wr1[:], in0=a[:], scalar1=PI, scalar2=-2.0 * PI, op0=ALU.is_gt, op1=ALU.mult), reads=[a], writes=[wr1])
            P.op('dve', lambda e: e.tensor_scalar(out=wr2[:], in0=a[:], scalar1=-PI, scalar2=2.0 * PI, op0=ALU.is_lt, op1=ALU.mult), reads=[a], writes=[wr2])
            P.op('dve', lambda e: e.tensor_tensor(out=a[:], in0=a[:], in1=wr1[:], op=ALU.add), reads=[a, wr1], writes=[a])
            P.op('dve', lambda e: e.tensor_tensor(out=a[:], in0=a[:], in1=wr2[:], op=ALU.add), reads=[a, wr2], writes=[a])
        P.op('act', lambda e: e.activation(out=dst_ap, in_=a[:], func=AF.Sin), reads=[a], writes=[dst_tr])

    ps_norm = {}
    for half in range(2):
        for ti in range(SEQ // 512):
            et = embt[ti % 2]
            a = arg[ti % 2]
            hc_ = hcur[ti % 2]
            P.dma(et[:], embs[half][:, ti * 512:(ti + 1) * 512], reads=[embs[half]], writes=[et])
            ps = P.ps()
            P.op('pe', lambda e: e.matmul(ps[:64, :512], fw1[:], et[:], start=True, stop=True), reads=[fw1, et], writes=[ps])
            sin_layer(ps, 0, hc_[:], hc_, a)
            ps = P.ps()
            P.op('pe', lambda e: e.matmul(ps[:64, :512], fwm[:, 0, :], hc_[:], start=True, stop=True), reads=[fwm, hc_], writes=[ps])
            sin_layer(ps, 1, hc_[:], hc_, a)
            ps = P.ps()
            P.op('pe', lambda e: e.matmul(ps[:64, :512], fwm[:, 1, :], hc_[:], start=True, stop=True), reads=[fwm, hc_], writes=[ps])
            sin_layer(ps, 2, hid[:, ti * 512:(ti + 1) * 512], hid, a)
        hv = hid[:, :].rearrange("f (a b) -> f b a", b=128)
        for o in range(2):
            for db in range(CPC // CBK):
                k_ = kt[nk % 2]
                nk += 1
                for n2 in range(128):
                    w_ = win[nw % 2]
                    nw += 1
                    ps = P.ps()
                    P.op('pe', lambda e: e.matmul(ps[:, :CBK], hv[:, n2, :], fwo[:, o, half, db * CBK:(db + 1) * CBK], start=True, stop=True), reads=[hid, fwo], writes=[ps])
                    P.op('act', lambda e: e.activation(out=w_[:], in_=delta[:, db * CBK:(db + 1) * CBK], func=AF.Exp, scale=tcol[half][:, n2:n2 + 1]), reads=[delta, tcol[half]], writes=[w_])
                    P.op('dve', lambda e: e.tensor_tensor(out=k_[:, :, n2], in0=ps[:, :CBK], in1=w_[:], op=ALU.mult), reads=[ps, w_], writes=[k_])
                if half == 1:
                    P.op('dve', lambda e: e.memset(k_[0:1, :, 0:1], 0.0), writes=[k_])
                P.op('dve', lambda e: e.tensor_reduce(out=part[:], in_=k_[:], axis=AX.X, op=ALU.add, apply_absolute_value=True), reads=[k_], writes=[part])
                if half == 0:
                    ps_norm[o, db] = None
                psn = P.ps()
                P.op('pe', lambda e: e.matmul(psn[:, :CBK], c['ones_f'][:], part[:], start=True, stop=True), reads=[c['ones_f'], part], writes=[psn])
                if half == 0:
                    P.op('dve', lambda e: e.tensor_copy(out=rn[:, o, db * CBK:(db + 1) * CBK], in_=psn[:, :CBK]), reads=[psn], writes=[rn])
                else:
                    P.op('dve', lambda e: e.tensor_tensor(out=rn[:, o, db * CBK:(db + 1) * CBK], in0=rn[:, o, db * CBK:(db + 1) * CBK], in1=psn[:, :CBK], op=ALU.add), reads=[psn, rn], writes=[rn])
                P.dma(kd[o, half, :, db * CBK:(db + 1) * CBK, :], k_[:], reads=[k_], writes=[kd])
    P.op('dve', lambda e: e.reciprocal(out=rn[:], in_=rn[:]), reads=[rn], writes=[rn])
    P.pop()

    P.push()
    ct = {}
    for nm, (dr, shp) in cn.items():
        if len(shp) == 3:
            t = P.sb(nm + "_sb", [shp[1], shp[0], shp[2]])
            P.dma(t[:], dr[:].rearrange("a p f -> p a f"), reads=[dr], writes=[t])
        else:
            t = P.sb(nm + "_sb", shp)
            P.dma(t[:], dr[:], reads=[dr], writes=[t])
        ct[nm] = t
    Yt = [P.sb("Yt%d" % i, [128, CBK, 128]) for i in range(2)]
    Xg = [P.sb("Xg%d" % i, [128, CBK, 128]) for i in range(2)]
    Kp = P.sb("Kp", [128, CBK, 128])
    Kf = P.sb("Kf", [128, CBK, 128])
    W = {nm: P.sb("w_" + nm, [128, 512]) for nm in ["P1", "P2", "B", "B2", "XkRR", "XkII", "Zr", "Zi"]}
    Dr = [P.sb("Dr%d" % i, [128, 128]) for i in range(2)]
    Di = [P.sb("Di%d" % i, [128, 128]) for i in range(2)]
    t1 = P.sb("t1", [128, 128])

    def cv_view(s_, db):
        return cv[s_, db * CBK:(db + 1) * CBK, :].rearrange("d (p j) -> p d j", j=128)

    def twiddle(psA, rr, ii, width, outs):
        w = width
        P.op('dve', lambda e: e.tensor_tensor(out=W["P1"][:, :2 * w], in0=psA[:, :2 * w], in1=rr, op=ALU.mult), reads=[psA] + outs['tabs'], writes=[W["P1"]])
        P.op('dve', lambda e: e.tensor_tensor(out=W["P2"][:, :2 * w], in0=psA[:, :2 * w], in1=ii, op=ALU.mult), reads=[psA] + outs['tabs'], writes=[W["P2"]])

    def fwd_fft(lhs_list, rhs_list, reads):
        psA = P.ps()
        nl = len(lhs_list)
        for i in range(nl):
            P.op('pe', lambda e: e.matmul(psA[:, :512], lhs_list[i], rhs_list[i], start=(i == 0), stop=(i == nl - 1)), reads=reads + [ct["W1"]], writes=[psA])
        twiddle(psA, ct["TWrr"][:], ct["TWii"][:], 256, {'tabs': [ct["TWrr"], ct["TWii"]]})
        P.op('pool', lambda e: e.tensor_tensor(out=W["B"][:, 0:256], in0=W["P1"][:, 0:256], in1=W["P2"][:, 256:512], op=ALU.subtract), reads=[W["P1"], W["P2"]], writes=[W["B"]])
        P.op('pool', lambda e: e.tensor_tensor(out=W["B"][:, 256:512], in0=W["P2"][:, 0:256], in1=W["P1"][:, 256:512], op=ALU.add), reads=[W["P1"], W["P2"]], writes=[W["B"]])
        P.op('act', lambda e: e.copy(out=W["B2"][:, 256:512], in_=W["B"][:, 0:256]), reads=[W["B"]], writes=[W["B2"]])
        P.op('act', lambda e: e.activation(out=W["B2"][:, 0:256], in_=W["B"][:, 256:512], func=AF.Copy, scale=-1.0), reads=[W["B"]], writes=[W["B2"]])
        psX = P.ps()
        P.op('pe', lambda e: e.matmul(psX[:, :512], ct["W2r"][:], W["B"][:], start=True, stop=False), reads=[ct["W2r"], W["B"]], writes=[psX])
        P.op('pe', lambda e: e.matmul(psX[:, :512], ct["W2i"][:], W["B2"][:], start=False, stop=True), reads=[ct["W2i"], W["B2"]], writes=[psX])
        return psX

    ny = 0
    for db in range(CPC // CBK):
        ycur = Yt[ny % 2]
        ny += 1
        P.dma(ycur[:], cv_view(2, db), reads=[cv], writes=[ycur])
        for o in range(2):
            xg = Xg[o]
            P.dma(xg[:], cv_view(o, db), reads=[cv], writes=[xg])
            P.dma(Kp[:], kd[o, 0, :, db * CBK:(db + 1) * CBK, :], reads=[kd], writes=[Kp])
            P.dma(Kf[:], kd[o, 1, :, db * CBK:(db + 1) * CBK, :], reads=[kd], writes=[Kf])
            ynew = Yt[ny % 2]
            ny += 1
            for dl in range(CBK):
                dch = db * CBK + dl
                psXk = fwd_fft([Kp[:, dl, :], Kf[:, dl, :]], [ct["W1"][:, 0, :], ct["W1"][:, 1, :]], [Kp, Kf])
                P.op('act', lambda e: e.copy(out=W["XkRR"][:, 0:256], in_=psXk[:, 0:256]), reads=[psXk], writes=[W["XkRR"]])
                P.op('act', lambda e: e.copy(out=W["XkRR"][:, 256:512], in_=psXk[:, 0:256]), reads=[psXk], writes=[W["XkRR"]])
                P.op('act', lambda e: e.copy(out=W["XkII"][:, 0:256], in_=psXk[:, 256:512]), reads=[psXk], writes=[W["XkII"]])
                P.op('act', lambda e: e.copy(out=W["XkII"][:, 256:512], in_=psXk[:, 256:512]), reads=[psXk], writes=[W["XkII"]])
                psXu = fwd_fft([ycur[:, dl, :]], [ct["W1"][:, 0, :]], [ycur])
                P.op('dve', lambda e: e.tensor_tensor(out=W["P1"][:], in0=psXu[:, :512], in1=W["XkRR"][:], op=ALU.mult), reads=[psXu, W["XkRR"]], writes=[W["P1"]])
                P.op('dve', lambda e: e.tensor_tensor(out=W["P2"][:], in0=psXu[:, :512], in1=W["XkII"][:], op=ALU.mult), reads=[psXu, W["XkII"]], writes=[W["P2"]])
                P.op('pool', lambda e: e.tensor_tensor(out=W["Zr"][:, 0:256], in0=W["P1"][:, 0:256], in1=W["P2"][:, 256:512], op=ALU.subtract), reads=[W["P1"], W["P2"]], writes=[W["Zr"]])
                P.op('pool', lambda e: e.tensor_tensor(out=W["Zi"][:, 0:256], in0=W["P2"][:, 0:256], in1=W["P1"][:, 256:512], op=ALU.add), reads=[W["P1"], W["P2"]], writes=[W["Zi"]])
                for hf in range(2):
                    psC = P.ps()
                    P.op('pe', lambda e: e.matmul(psC[:, :256], W["Zr"][:, 128 * hf:128 * hf + 128], ct["WIa"][:], start=True, stop=False), reads=[W["Zr"], ct["WIa"]], writes=[psC])
                    P.op('pe', lambda e: e.matmul(psC[:, :256], W["Zi"][:, 128 * hf:128 * hf + 128], ct["WIb"][:], start=False, stop=True), reads=[W["Zi"], ct["WIb"]], writes=[psC])
                    P.op('dve', lambda e: e.tensor_tensor(out=W["P1"][:, :256], in0=psC[:, :256], in1=ct["TWIrr"][:, hf, :], op=ALU.mult), reads=[psC, ct["TWIrr"]], writes=[W["P1"]])
                    P.op('dve', lambda e: e.tensor_tensor(out=W["P2"][:, :256], in0=psC[:, :256], in1=ct["TWIii"][:, hf, :], op=ALU.mult), reads=[psC, ct["TWIii"]], writes=[W["P2"]])
                    P.op('pool', lambda e: e.tensor_tensor(out=Dr[hf][:], in0=W["P1"][:, 0:128], in1=W["P2"][:, 128:256], op=ALU.subtract), reads=[W["P1"], W["P2"]], writes=[Dr[hf]])
                    P.op('pool', lambda e: e.tensor_tensor(out=Di[hf][:], in0=W["P2"][:, 0:128], in1=W["P1"][:, 128:256], op=ALU.add), reads=[W["P1"], W["P2"]], writes=[Di[hf]])
                psY = P.ps()
                for hf in range(2):
                    P.op('pe', lambda e: e.matmul(psY[:, :128], ct["Cos"][:, hf, :], Dr[hf][:], start=(hf == 0), stop=False), reads=[ct["Cos"], Dr[hf]], writes=[psY])
                    P.op('pe', lambda e: e.matmul(psY[:, :128], ct["NSin"][:, hf, :], Di[hf][:], start=False, stop=(hf == 1)), reads=[ct["NSin"], Di[hf]], writes=[psY])
                P.op('dve', lambda e: e.tensor_scalar(out=t1[:], in0=psY[:, :128], scalar1=rn[:, o, dch:dch + 1], scalar2=None, op0=ALU.mult), reads=[psY, rn], writes=[t1])
                P.op('dve', lambda e: e.scalar_tensor_tensor(out=t1[:], in0=ycur[:, dl, :], scalar=fbias[:, o, dch:dch + 1], in1=t1[:], op0=ALU.mult, op1=ALU.add),
                     reads=[ycur, fbias, t1], writes=[t1])
                P.op('pool', lambda e: e.tensor_tensor(out=ynew[:, dl, :], in0=t1[:], in1=xg[:, dl, :], op=ALU.mult), reads=[t1, xg], writes=[ynew])
            ycur = ynew
        P.dma(y2[db * CBK:(db + 1) * CBK, :].rearrange("d (p j) -> p d j", j=128), ycur[:], reads=[ycur], writes=[y2], is_out=True)
    P.pop()
    return P


def hyena_consts():
    L = SEQ
    N = NFFT
    f64 = np.float64
    n1 = np.arange(256, dtype=f64)[:, None]
    k1 = np.arange(256, dtype=f64)[None, :]
    th = 2 * np.pi * n1 * k1 / 256
    W1 = np.concatenate([np.cos(th), -np.sin(th)], axis=1).reshape(2, 128, 512)
    n2 = np.arange(128, dtype=f64)[:, None]
    tw = 2 * np.pi * n2 * k1 / N
    TWr, TWi = np.cos(tw), -np.sin(tw)
    k2 = np.arange(128, dtype=f64)[None, :]
    t2 = 2 * np.pi * n2 * k2 / 128
    W2r, W2i = np.cos(t2), -np.sin(t2)
    Wr_, Wi_ = np.cos(t2), np.sin(t2)
    WIa = np.concatenate([Wr_, Wi_], axis=1)
    WIb = np.concatenate([-Wi_, Wr_], axis=1)
    k1c = np.arange(256, dtype=f64)[:, None]
    n2r = np.arange(128, dtype=f64)[None, :]
    ti = 2 * np.pi * k1c * n2r / N
    Tr, Ti = np.cos(ti), np.sin(ti)
    TWIrr = np.concatenate([Tr, Tr], axis=1).reshape(2, 128, 256)
    TWIii = np.concatenate([Ti, Ti], axis=1).reshape(2, 128, 256)
    n1r = np.arange(128, dtype=f64)[None, :]
    t3 = 2 * np.pi * k1c * n1r / 256
    Cos = (np.cos(t3) / N).reshape(2, 128, 128)
    NSin = (-np.sin(t3) / N).reshape(2, 128, 128)
    d = {"W1": W1, "TWrr": np.concatenate([TWr, TWr], axis=1), "TWii": np.concatenate([TWi, TWi], axis=1), "W2r": W2r, "W2i": W2i,
         "WIa": WIa, "WIb": WIb, "TWIrr": TWIrr, "TWIii": TWIii, "Cos": Cos, "NSin": NSin}
    d = {k: np.ascontiguousarray(v.astype(np.float32)) for k, v in d.items()}
    f32 = np.float32
    t = np.linspace(0.0, 1.0, L, dtype=f32)
    w = (f32(2.0 * np.pi) * np.arange(L, dtype=f32) / f32(L)).astype(f32)
    f = np.linspace(1e-4, 15.0, 16, dtype=f32)
    fwm_ = (f[None, :] * w[:, None]).astype(f32)
    emb = np.concatenate([t[:, None], np.cos(fwm_), -np.sin(fwm_)], axis=1).astype(f32)
    ridx = np.concatenate([[0], L - np.arange(1, L)])
    d["embT"] = np.ascontiguousarray(emb.T)
    d["embTr"] = np.ascontiguousarray(emb[ridx].T)
    d["tcol"] = np.ascontiguousarray((-t).reshape(128, 128))
    d["tcolr"] = np.ascontiguousarray((-t[ridx]).reshape(128, 128))
    lo = np.log(1.5) / 1e-2
    hi = np.log(0.3) / 1e-2
    d["_deltas"] = np.abs(np.linspace(lo, hi, D, dtype=f32)).astype(f32)
    return d


def run_l5(inp, o4):
    P = build_l5()
    hc = hyena_consts()
    deltas = hc.pop("_deltas")
    projT = o4['projT']
    sw = inp['hy_short_w'][0]
    sb = inp['hy_short_b'][0]
    common = dict(hc)
    common.update({
        "fw1": np.ascontiguousarray(inp['hy_fw1'][0]), "fwm": np.ascontiguousarray(inp['hy_fw_mid'][0]),
        "fq": np.ascontiguousarray(inp['hy_freq'][0].reshape(64, 1)),
        "fb": np.ascontiguousarray(np.stack([inp['hy_fb1'][0], inp['hy_fb_mid'][0][0], inp['hy_fb_mid'][0][1]], axis=1)),
    })
    in_maps = []
    for i in range(NCORE):
        d0 = i * CPC
        pj = np.zeros((3, CPC, SEQ + 2), np.float32)
        swl = np.zeros((128, 6, 3), np.float32)
        sbl = np.zeros((128, 6), np.float32)
        for s_ in range(3):
            pj[s_, :, 1:SEQ + 1] = projT[s_ * D + d0:s_ * D + d0 + CPC, :]
            for gq in range(2):
                idx = s_ * D + d0 + gq * 128 + np.arange(128)
                swl[:, s_ * 2 + gq, :] = sw[:, idx].T
                sbl[:, s_ * 2 + gq] = sb[idx]
        m = dict(common)
        m.update({"pj": pj, "sw": swl, "sb": sbl,
                  "fwo": np.ascontiguousarray(inp['hy_fw_out'][0][:, :, :, d0:d0 + CPC]),
                  "delta": np.ascontiguousarray(np.tile(deltas[None, d0:d0 + CPC], (128, 1))),
                  "fbias": np.ascontiguousarray(np.tile(inp['hy_fbias'][0][None, :, d0:d0 + CPC], (128, 1, 1)))})
        in_maps.append(m)
    res = run_spmd(P, in_maps)
    return {"y2T": np.concatenate([r['y2'] for r in res], axis=0)}


def run_l6(inp, mod, o4, o5):
    P = build_l6()
    m1 = split_mod(mod[1, 0])
    common = {
        "w1": np.ascontiguousarray(inp['ffn_w1'][1]), "w3": np.ascontiguousarray(inp['ffn_w3'][1]), "w2": np.ascontiguousarray(inp['ffn_w2'][1]),
        "wo": np.ascontiguousarray(inp['hy_w_out'][0]), "nw": vec_layout(inp['norm_ffn_w'][1]), "sc2": vec_layout(m1[4]), "sh2": vec_layout(m1[3]),
        "g2": vec_layout(m1[5]), "bo": vec_layout(inp['hy_b_out'][0]), "g1": vec_layout(m1[2]), "fw": vec_layout(inp['final_norm_w']),
    }
    in_maps = []
    for i in range(NCORE):
        m = dict(common)
        m.update({"xT": tok_shard(o4['x2T'], i), "yT": tok_shard(o5['y2T'], i)})
        in_maps.append(m)
    res = run_spmd(P, in_maps)
    outT = np.concatenate([r['outT'] for r in res], axis=1)
    return np.ascontiguousarray(outT.T)[None].astype(np.float32)


def kernel(**inputs):
    inp = {k: np.asarray(v) for k, v in inputs.items()}
    mod = run_l0(inp['c'], inp['c_ctx'], inp['mod_w'], inp['mod_b'])
    o1 = run_l1(inp, mod)
    o2 = run_l2(inp, o1)
    o4 = run_l4(inp, mod, o1, o2)
    o5 = run_l5(inp, o4)
    return run_l6(inp, mod, o4, o5)
```

```python
from contextlib import ExitStack
import numpy as np
import concourse.bass as bass
import concourse.mybir as mybir
from concourse.bass_utils import run_bass_kernel_spmd

F32 = mybir.dt.float32
BF16 = mybir.dt.bfloat16
AF = mybir.ActivationFunctionType
ALU = mybir.AluOpType
AX = mybir.AxisListType

D = 2048
SEQ = 16384
CTX = 256
DFF = 5632
NCORE = 8
TPC = SEQ // NCORE
TT = 512
EPS = 1e-6


class TR:
    def __init__(self, h):
        self.h = h
        self.w = None
        self.r = {}

    def __getitem__(self, k):
        return self.h[k]


class Prog:
    def __init__(self, ndma=32):
        self.nc = bass.Bass("TRN2", target_bir_lowering=False)
        nc = self.nc
        self.es = ExitStack()
        self.es.enter_context(nc.allow_low_precision("bf16 matmul operands, fp32 accumulate"))
        self.engs = {'pe': nc.tensor, 'act': nc.scalar, 'dve': nc.vector, 'pool': nc.gpsimd, 'sp': nc.sync}
        self.sem = {}
        self.cnt = {}
        for e in ['pe', 'act', 'dve', 'pool']:
            self.sem[e] = self.es.enter_context(nc.semaphore("s_" + e))
            self.cnt[e] = 0
        self.ndma = ndma
        for i in range(ndma):
            self.sem[('d', i)] = self.es.enter_context(nc.semaphore("d%d" % i))
            self.cnt[('d', i)] = 0
        self.dnext = 0
        self.waited = {e: {} for e in self.engs}
        self.psum = [TR(self.es.enter_context(nc.psum_tensor("ps%d" % i, [128, 512], F32))) for i in range(8)]
        self.psn = 0
        self.out_events = []
        self.n_ins = 0

    def sb(self, name, shape, dtype=F32):
        return TR(self.es.enter_context(self.nc.sbuf_tensor(name, list(shape), dtype)))

    def din(self, name, shape, dtype=F32):
        return TR(self.nc.dram_tensor(name, list(shape), dtype, kind="ExternalInput").ap())

    def dout(self, name, shape, dtype=F32):
        return TR(self.nc.dram_tensor(name, list(shape), dtype, kind="ExternalOutput").ap())

    def dscratch(self, name, shape, dtype=F32):
        return TR(self.nc.dram_tensor(name, list(shape), dtype, kind="Internal").ap())

    def ps(self):
        p = self.psum[self.psn % 8]
        self.psn += 1
        return p

    def _wait(self, e, ev):
        key, val = ev
        if self.waited[e].get(key, 0) >= val:
            return
        self.engs[e].wait_ge(self.sem[key], val)
        self.waited[e][key] = val

    def _deps(self, e, reads, writes):
        deps = []
        for t in reads:
            if t.w is not None:
                deps.append(t.w)
        for t in writes:
            if t.w is not None:
                deps.append(t.w)
            deps.extend(t.r.items())
        for ev in deps:
            if e == 'pe' and ev[0] == 'pe':
                continue
            self._wait(e, ev)

    def _record(self, ev, reads, writes):
        for t in writes:
            t.w = ev
            t.r = {}
        for t in reads:
            if t.r.get(ev[0], 0) < ev[1]:
                t.r[ev[0]] = ev[1]

    def op(self, e, fn, reads=(), writes=()):
        self._deps(e, reads, writes)
        ins = fn(self.engs[e])
        self.cnt[e] += 1
        ins.then_inc(self.sem[e], 1)
        ev = (e, self.cnt[e])
        self._record(ev, reads, writes)
        self.n_ins += 1
        return ev

    def dma(self, out_ap, in_ap, reads=(), writes=(), q='sp', is_out=False):
        self._deps(q, reads, writes)
        i = self.dnext
        self.dnext = (self.dnext + 1) % self.ndma
        key = ('d', i)
        if self.cnt[key] > 0:
            self._wait(q, (key, self.cnt[key] * 16))
        self.cnt[key] += 1
        self.engs[q].dma_start(out=out_ap, in_=in_ap).then_inc(self.sem[key], 16)
        ev = (key, self.cnt[key] * 16)
        self._record(ev, reads, writes)
        if is_out:
            self.out_events.append(ev)
        self.n_ins += 1
        return ev

    def barrier(self):
        for e in self.engs:
            for key, cnt in self.cnt.items():
                if cnt > 0:
                    val = cnt * 16 if isinstance(key, tuple) else cnt
                    self._wait(e, (key, val))

    def push(self):
        self._saved_es = getattr(self, '_saved_es', [])
        self._saved_es.append(self.es)
        self.es = ExitStack()

    def pop(self):
        self.barrier()
        self.es.close()
        self.es = self._saved_es.pop()

    def finish(self):
        for ev in self.out_events:
            self._wait('sp', ev)
        for e in ['pe', 'act', 'dve', 'pool']:
            if self.cnt[e] > 0:
                self._wait('sp', (e, self.cnt[e]))
        self.es.close()
        return self.nc


_TIMES = []


def run_spmd(prog, in_maps):
    nc = prog.finish()
    if _TRACE[0]:
        res = run_bass_kernel_spmd(nc, in_maps, core_ids=list(range(NCORE)), trace=True)
        _TIMES.append(getattr(res, "exec_time_ns", None))
        print("LAUNCH exec_time_ns", _TIMES[-1], "n_ins", prog.n_ins, flush=True)
        st = getattr(res, "per_core_scope_times", None)
        if st:
            print("SCOPES", {k: v.get(0) for k, v in st.items()}, flush=True)
    else:
        res = run_bass_kernel_spmd(nc, in_maps, core_ids=list(range(NCORE)))
    return res.results


_TRACE = [False]


class WStream:
    def __init__(self, P, maxk=4096, tag="w", nstg=2, nwb=4):
        self.P = P
        self.stg = [P.sb("%s_stg%d" % (tag, i), [128, maxk], F32) for i in range(nstg)]
        self.wb = [P.sb("%s_wb%d" % (tag, i), [128, maxk], BF16) for i in range(nwb)]
        self.n = 0
        self.maxel = maxk

    def load(self, W, col0, ncol, K, row0=0):
        P = self.P
        KC = K // 128
        assert KC * ncol <= self.maxel, (KC, ncol, self.maxel)
        stg = self.stg[self.n % len(self.stg)]
        wb = self.wb[self.n % len(self.wb)]
        self.n += 1
        src = W[row0:row0 + K, col0:col0 + ncol].rearrange("(kc p) f -> p kc f", p=128)
        dst = stg[:, 0:KC * ncol].rearrange("p (kc f) -> p kc f", kc=KC)
        P.dma(dst, src, reads=[W], writes=[stg], q='sp')
        if self.n % 2 == 0:
            P.op('pool', lambda e: e.tensor_copy(out=wb[:, 0:KC * ncol], in_=stg[:, 0:KC * ncol]), reads=[stg], writes=[wb])
        else:
            P.op('act', lambda e: e.copy(out=wb[:, 0:KC * ncol], in_=stg[:, 0:KC * ncol]), reads=[stg], writes=[wb])

        def view(kc, c0, c1):
            return wb[:, kc * ncol + c0: kc * ncol + c1]
        return wb, view


def emit_consts(P):
    c = {}
    c['ones_f'] = P.sb("ones_f", [128, 128], F32)
    P.op('dve', lambda e: e.memset(c['ones_f'][:], 1.0), writes=[c['ones_f']])
    return c


def emit_rmsnorm_mod(P, c, x, h, scale_t, shift_t, ntok, sq, rstd, nch=16, dim=None, ki=0, ko=0, ks=0):
    dim = dim or nch * 128
    ps = P.ps()
    for kc in range(nch):
        P.op('act', lambda e: e.activation(out=sq[:, :ntok], in_=x[:, ki + kc, :ntok], func=AF.Square), reads=[x], writes=[sq])
        P.op('pe', lambda e: e.matmul(ps[:, :ntok], c['ones_f'][:], sq[:, :ntok], start=(kc == 0), stop=(kc == nch - 1)),
             reads=[sq, c['ones_f']], writes=[ps])
    P.op('dve', lambda e: e.tensor_scalar(out=rstd[:, :ntok], in0=ps[:, :ntok], scalar1=1.0 / dim, scalar2=EPS, op0=ALU.mult, op1=ALU.add),
         reads=[ps], writes=[rstd])
    P.op('act', lambda e: e.activation(out=rstd[:, :ntok], in_=rstd[:, :ntok], func=AF.Sqrt), reads=[rstd], writes=[rstd])
    P.op('dve', lambda e: e.reciprocal(out=rstd[:, :ntok], in_=rstd[:, :ntok]), reads=[rstd], writes=[rstd])
    for kc in range(nch):
        if shift_t is not None:
            P.op('dve', lambda e: e.scalar_tensor_tensor(out=sq[:, :ntok], in0=x[:, ki + kc, :ntok], scalar=scale_t[:, ks + kc:ks + kc + 1], in1=rstd[:, :ntok],
                                                          op0=ALU.mult, op1=ALU.mult), reads=[x, scale_t, rstd], writes=[sq])
            P.op('act', lambda e: e.activation(out=h[:, ko + kc, :ntok], in_=sq[:, :ntok], func=AF.Identity, bias=shift_t[:, ks + kc:ks + kc + 1], scale=1.0),
                 reads=[sq, shift_t], writes=[h])
        else:
            P.op('dve', lambda e: e.scalar_tensor_tensor(out=h[:, ko + kc, :ntok], in0=x[:, ki + kc, :ntok], scalar=scale_t[:, ks + kc:ks + kc + 1], in1=rstd[:, :ntok],
                                                          op0=ALU.mult, op1=ALU.mult), reads=[x, scale_t, rstd], writes=[h])


def emit_linear(P, ws, W, K, f0, f1, h, ntok, evac, cb=256):
    KC = K // 128
    f = f0
    while f < f1:
        n = min(cb, f1 - f)
        wb, view = ws.load(W, f, n, K)
        for s in range(0, n, 128):
            m = min(128, n - s)
            ps = P.ps()
            for kc in range(KC):
                P.op('pe', lambda e: e.matmul(ps[:m, :ntok], view(kc, s, s + m), h[:, kc, :ntok], start=(kc == 0), stop=(kc == KC - 1)),
                     reads=[wb, h], writes=[ps])
            evac((f + s) // 128, ps, m)
        f += n


def emit_ffn(P, c, ws, x, h, u, tmp, w1, w3, w2, g2, ntok):
    KC = D // 128
    FC = DFF // 128
    f = 0
    cb = 256
    while f < DFF:
        n = min(cb, DFF - f)
        wb1, v1 = ws.load(w1, f, n, D)
        wb3, v3 = ws.load(w3, f, n, D)
        for s in range(0, n, 128):
            p1 = P.ps()
            p3 = P.ps()
            for kc in range(KC):
                P.op('pe', lambda e: e.matmul(p1[:, :ntok], v1(kc, s, s + 128), h[:, kc, :ntok], start=(kc == 0), stop=(kc == KC - 1)),
                     reads=[wb1, h], writes=[p1])
            for kc in range(KC):
                P.op('pe', lambda e: e.matmul(p3[:, :ntok], v3(kc, s, s + 128), h[:, kc, :ntok], start=(kc == 0), stop=(kc == KC - 1)),
                     reads=[wb3, h], writes=[p3])
            fc = (f + s) // 128
            P.op('act', lambda e: e.activation(out=tmp[:, :ntok], in_=p1[:, :ntok], func=AF.Silu), reads=[p1], writes=[tmp])
            P.op('dve', lambda e: e.tensor_tensor(out=u[:, fc, :ntok], in0=tmp[:, :ntok], in1=p3[:, :ntok], op=ALU.mult), reads=[tmp, p3], writes=[u])
        f += n
    HK = FC // 2
    for dc in range(KC):
        wbA, vA = ws.load(w2, dc * 128, 128, HK * 128, row0=0)
        wbB, vB = ws.load(w2, dc * 128, 128, HK * 128, row0=HK * 128)
        ps = P.ps()
        for fc in range(FC):
            wb_, v_, kk = (wbA, vA, fc) if fc < HK else (wbB, vB, fc - HK)
            P.op('pe', lambda e: e.matmul(ps[:, :ntok], v_(kk, 0, 128), u[:, fc, :ntok], start=(fc == 0), stop=(fc == FC - 1)),
                 reads=[wb_, u], writes=[ps])
        P.op('dve', lambda e: e.scalar_tensor_tensor(out=x[:, dc, :ntok], in0=ps[:, :ntok], scalar=g2[:, dc:dc + 1], in1=x[:, dc, :ntok],
                                                      op0=ALU.mult, op1=ALU.add), reads=[ps, g2, x], writes=[x])


def load_vec(P, name, n=16):
    d = P.din(name, [128, n])
    t = P.sb(name + "_sb", [128, n])
    P.dma(t[:], d[:], reads=[d], writes=[t])
    return t


def build_l0():
    P = Prog()
    NCOL = 12288 // NCORE
    cc = P.din("cc", [128, 16, 2])
    mw = P.din("mw", [2, D, NCOL])
    mb = P.din("mb", [2, 2, NCOL])
    out = P.dout("out", [2, 2, NCOL])
    cs = P.sb("cs", [128, 16, 2])
    ca = P.sb("ca", [128, 16, 2])
    P.dma(cs[:], cc[:], reads=[cc], writes=[cs])
    P.op('act', lambda e: e.activation(out=ca[:], in_=cs[:], func=AF.Silu), reads=[cs], writes=[ca])
    bsb = P.sb("bsb", [2, 2, NCOL])
    P.dma(bsb[:], mb[:].rearrange("l r n -> r l n"), reads=[mb], writes=[bsb])
    wt = [P.sb("wt%d" % i, [128, 16, 512]) for i in range(2)]
    res = P.sb("res", [2, 2, NCOL])
    n = 0
    for l in range(2):
        for cb in range(NCOL // 512):
            w = wt[n % 2]
            n += 1
            P.dma(w[:], mw[l, :, cb * 512:(cb + 1) * 512].rearrange("(kc p) f -> p kc f", p=128), reads=[mw], writes=[w])
            ps = P.ps()
            for kc in range(16):
                P.op('pe', lambda e: e.matmul(ps[:2, :], ca[:, kc, :], w[:, kc, :], start=(kc == 0), stop=(kc == 15)), reads=[ca, w], writes=[ps])
            P.op('dve', lambda e: e.tensor_tensor(out=res[:, l, cb * 512:(cb + 1) * 512], in0=ps[:2, :], in1=bsb[:, l, cb * 512:(cb + 1) * 512], op=ALU.add),
                 reads=[ps, bsb], writes=[res])
    P.dma(out[:].rearrange("l r n -> r l n"), res[:], reads=[res], writes=[out], is_out=True)
    return P


def vec_layout(v):
    v = np.asarray(v, np.float32)
    return np.ascontiguousarray(v.reshape(-1, 128).T)


def run_l0(c, c_ctx, mod_w, mod_b):
    P = build_l0()
    NCOL = 12288 // NCORE
    cc = np.stack([vec_layout(c.reshape(-1)), vec_layout(c_ctx.reshape(-1))], axis=-1)
    in_maps = []
    for i in range(NCORE):
        sl = slice(i * NCOL, (i + 1) * NCOL)
        in_maps.append({"cc": np.ascontiguousarray(cc),
                        "mw": np.ascontiguousarray(mod_w[:, :, sl]),
                        "mb": np.ascontiguousarray(np.repeat(mod_b[:, None, sl], 2, axis=1))})
    res = run_spmd(P, in_maps)
    out = np.concatenate([r["out"] for r in res], axis=-1)
    return out


def build_l6(do_proj=True, do_final=True):
    P = Prog()
    c = emit_consts(P)
    xT = P.din("xT", [D, TPC])
    out = P.dout("outT", [D, TPC])
    w1 = P.din("w1", [D, DFF])
    w3 = P.din("w3", [D, DFF])
    w2 = P.din("w2", [DFF, D])
    nw = load_vec(P, "nw")
    sc2 = load_vec(P, "sc2")
    sh2 = load_vec(P, "sh2")
    g2 = load_vec(P, "g2")
    if do_proj:
        yT = P.din("yT", [D, TPC])
        wo = P.din("wo", [D, D])
        bo = load_vec(P, "bo")
        g1 = load_vec(P, "g1")
        gb = P.sb("gb", [128, 16])
        P.op('dve', lambda e: e.tensor_tensor(out=gb[:], in0=g1[:], in1=bo[:], op=ALU.mult), reads=[g1, bo], writes=[gb])
    if do_final:
        fw = load_vec(P, "fw")
    scl = P.sb("scl", [128, 16])
    P.op('dve', lambda e: e.scalar_tensor_tensor(out=scl[:], in0=sc2[:], scalar=1.0, in1=nw[:], op0=ALU.add, op1=ALU.mult), reads=[sc2, nw], writes=[scl])
    ws = WStream(P)
    x = P.sb("x", [128, 16, TT])
    h = P.sb("h", [128, 16, TT], BF16)
    u = P.sb("u", [128, DFF // 128, TT], BF16)
    sq = P.sb("sq", [128, TT])
    rstd = P.sb("rstd", [128, TT])
    tmp = P.sb("tmp", [128, TT])
    xv = xT[:, :].rearrange("(kc p) t -> p kc t", p=128)
    ov = out[:, :].rearrange("(kc p) t -> p kc t", p=128)
    for tt in range(TPC // TT):
        t0 = tt * TT
        P.dma(x[:], xv[:, :, t0:t0 + TT], reads=[xT], writes=[x])
        if do_proj:
            yv = yT[:, :].rearrange("(kc p) t -> p kc t", p=128)
            for kc in range(16):
                P.dma(tmp[:], yv[:, kc, t0:t0 + TT], reads=[yT], writes=[tmp])
                P.op('dve', lambda e: e.tensor_copy(out=h[:, kc, :], in_=tmp[:]), reads=[tmp], writes=[h])

            def evac(dc, ps, m):
                P.op('dve', lambda e: e.scalar_tensor_tensor(out=x[:, dc, :], in0=ps[:, :TT], scalar=g1[:, dc:dc + 1], in1=x[:, dc, :],
                                                              op0=ALU.mult, op1=ALU.add), reads=[ps, g1, x], writes=[x])
                P.op('dve', lambda e: e.tensor_scalar(out=x[:, dc, :], in0=x[:, dc, :], scalar1=gb[:, dc:dc + 1], scalar2=None, op0=ALU.add),
                     reads=[x, gb], writes=[x])
            emit_linear(P, ws, wo, D, 0, D, h, TT, evac)
        emit_rmsnorm_mod(P, c, x, h, scl, sh2, TT, sq, rstd)
        emit_ffn(P, c, ws, x, h, u, tmp, w1, w3, w2, g2, TT)
        if do_final:
            ps = P.ps()
            for kc in range(16):
                P.op('act', lambda e: e.activation(out=sq[:], in_=x[:, kc, :], func=AF.Square), reads=[x], writes=[sq])
                P.op('pe', lambda e: e.matmul(ps[:, :TT], c['ones_f'][:], sq[:], start=(kc == 0), stop=(kc == 15)), reads=[sq, c['ones_f']], writes=[ps])
            P.op('dve', lambda e: e.tensor_scalar(out=rstd[:], in0=ps[:, :TT], scalar1=1.0 / D, scalar2=EPS, op0=ALU.mult, op1=ALU.add), reads=[ps], writes=[rstd])
            P.op('act', lambda e: e.activation(out=rstd[:], in_=rstd[:], func=AF.Sqrt), reads=[rstd], writes=[rstd])
            P.op('dve', lambda e: e.reciprocal(out=rstd[:], in_=rstd[:]), reads=[rstd], writes=[rstd])
            for kc in range(16):
                P.op('dve', lambda e: e.scalar_tensor_tensor(out=x[:, kc, :], in0=x[:, kc, :], scalar=fw[:, kc:kc + 1], in1=rstd[:],
                                                              op0=ALU.mult, op1=ALU.mult), reads=[x, fw, rstd], writes=[x])
        P.dma(ov[:, :, t0:t0 + TT], x[:], reads=[x], writes=[out], is_out=True)
    return P


NQ = 1024 + 512
XBC0, DT0, CKV0, KR0 = 1536, 3072, 3104, 3616
L1COLS = 8 + TPC + CTX
QSCALE = 192.0 ** -0.5


def build_l1():
    P = Prog()
    c = emit_consts(P)
    xh = P.din("xh", [D, L1COLS])
    hmask = P.din("hmask", [128, 8])
    cosd = P.din("cosT", [64, TPC + CTX])
    sind = P.din("sinT", [64, TPC + CTX])
    w_in = P.din("w_in", [D, 3680])
    w_in_sw = P.din("w_in_sw", [D, 64])
    w_uq = P.din("w_uq", [512, 1536])
    w_uq_sw = P.din("w_uq_sw", [512, 512])
    w_ukv = P.din("w_ukv", [512, 2048])
    nmw = load_vec(P, "nmw")
    mods = {k: load_vec(P, k) for k in ["sh1", "sc1", "csh1", "csc1"]}
    qnw = load_vec(P, "qnw", 4)
    kvnw = load_vec(P, "kvnw", 4)
    cw = P.din("convw", [128, 12, 3])
    cwt = P.sb("cwt", [128, 12, 3])
    P.dma(cwt[:], cw[:], reads=[cw], writes=[cwt])
    cb = load_vec(P, "convb", 12)
    dtb_d = P.din("dtb", [32, 1])
    dtb = P.sb("dtb_sb", [32, 1])
    P.dma(dtb[:], dtb_d[:], reads=[dtb_d], writes=[dtb])
    hm = P.sb("hm", [128, 8])
    P.dma(hm[:], hmask[:], reads=[hmask], writes=[hm])
    zT = P.dout("zT", [1024, TPC])
    qnT = P.dout("qnT", [8, 128, TPC])
    qrT = P.dout("qrT", [8, 64, TPC])
    xbcT = P.dout("xbcT", [1536, TPC + CTX])
    dtT = P.dout("dtT", [32, TPC + CTX])
    kvT = P.dout("kvT", [2048, TPC + CTX])
    krT = P.dout("krT", [64, TPC + CTX])
    scl = {}
    for k, s in [("lat", "sc1"), ("ctx", "csc1")]:
        scl[k] = P.sb("scl_" + k, [128, 16])
        P.op('dve', lambda e: e.scalar_tensor_tensor(out=scl[k][:], in0=mods[s][:], scalar=1.0, in1=nmw[:], op0=ALU.add, op1=ALU.mult),
             reads=[mods[s], nmw], writes=[scl[k]])
    shf = {"lat": mods["sh1"], "ctx": mods["csh1"]}
    ws = WStream(P)
    x = P.sb("x", [128, 16, TT])
    h = P.sb("h", [128, 16, TT], BF16)
    sq = P.sb("sq", [128, TT])
    rstd = P.sb("rstd", [128, TT])
    tmp = P.sb("tmp", [128, TT])
    tmp2 = P.sb("tmp2", [128, TT])
    xbc = P.sb("xbc", [128, 12, TT + 2])
    xbch = P.sb("xbch", [128, 12, 8])
    cq = P.sb("cq", [128, 4, TT])
    hq = P.sb("hq", [128, 4, TT], BF16)
    cost = P.sb("cost", [64, TT])
    sint = P.sb("sint", [64, TT])
    xv = xh[:, :].rearrange("(kc p) t -> p kc t", p=128)

    P.dma(x[:, :, 0:8], xv[:, :, 0:8], reads=[xh], writes=[x])
    emit_rmsnorm_mod(P, c, x, h, scl["lat"], shf["lat"], 8, sq, rstd)

    def ev_h(fc, ps, m):
        i = fc - XBC0 // 128
        P.op('dve', lambda e: e.tensor_tensor(out=xbch[:, i, :], in0=ps[:, 0:8], in1=hm[:], op=ALU.mult), reads=[ps, hm], writes=[xbch])
    emit_linear(P, ws, w_in, D, XBC0, DT0, h, 8, ev_h)

    def rope_out(psA, psB, n, dst_ap, dst_tr, scale):
        P.op('dve', lambda e: e.tensor_tensor(out=tmp[:64, :n], in0=psA[:64, :n], in1=cost[:, :n], op=ALU.mult), reads=[psA, cost], writes=[tmp])
        P.op('dve', lambda e: e.tensor_tensor(out=tmp2[:64, :n], in0=psB[:64, :n], in1=sint[:, :n], op=ALU.mult), reads=[psB, sint], writes=[tmp2])
        P.op('dve', lambda e: e.scalar_tensor_tensor(out=tmp[:64, :n], in0=tmp[:64, :n], scalar=scale, in1=tmp2[:64, :n], op0=ALU.mult, op1=ALU.add),
             reads=[tmp, tmp2], writes=[tmp])
        P.dma(dst_ap, tmp[:64, :n], reads=[tmp], writes=[dst_tr], is_out=True)

    segs = [("lat", 8 + TT * j, TT, TT * j, j) for j in range(TPC // TT)] + [("ctx", 8 + TPC, CTX, TPC, None)]
    for kind, c0, n, oc, hj in segs:
        P.dma(x[:, :, :n], xv[:, :, c0:c0 + n], reads=[xh], writes=[x])
        P.dma(cost[:, :n], cosd[:, oc:oc + n], reads=[cosd], writes=[cost])
        P.dma(sint[:, :n], sind[:, oc:oc + n], reads=[sind], writes=[sint])
        emit_rmsnorm_mod(P, c, x, h, scl[kind], shf[kind], n, sq, rstd)
        if kind == "lat":
            def ev_z(fc, ps, m):
                P.op('act', lambda e: e.copy(out=x[:, fc, :n], in_=ps[:, :n]), reads=[ps], writes=[x])
            emit_linear(P, ws, w_in, D, 0, 1024, h, n, ev_z)
            P.dma(zT[:, oc:oc + n].rearrange("(c p) t -> p c t", p=128), x[:, 0:8, :n], reads=[x], writes=[zT], is_out=True)
            def ev_cq(fc, ps, m):
                P.op('act', lambda e: e.copy(out=cq[:, fc - 8, :n], in_=ps[:, :n]), reads=[ps], writes=[cq])
            emit_linear(P, ws, w_in, D, 1024, 1536, h, n, ev_cq)
            emit_rmsnorm_mod(P, c, cq, hq, qnw, None, n, sq, rstd, nch=4)
            for hh in range(8):
                def ev_qn(fc, ps, m):
                    P.op('act', lambda e: e.activation(out=x[:, hh, :n], in_=ps[:, :n], func=AF.Copy, scale=QSCALE), reads=[ps], writes=[x])
                emit_linear(P, ws, w_uq, 512, 192 * hh, 192 * hh + 128, hq, n, ev_qn)
            P.dma(qnT[:, :, oc:oc + n].rearrange("h p t -> p h t"), x[:, 0:8, :n], reads=[x], writes=[qnT], is_out=True)
            for hh in range(8):
                got = {}

                def ev_a(fc, ps, m):
                    got['a'] = ps

                def ev_b(fc, ps, m):
                    got['b'] = ps
                emit_linear(P, ws, w_uq, 512, 192 * hh + 128, 192 * hh + 192, hq, n, ev_a)
                emit_linear(P, ws, w_uq_sw, 512, 64 * hh, 64 * hh + 64, hq, n, ev_b)
                P.op('dve', lambda e: e.tensor_scalar(out=tmp2[:64, :n], in0=got['b'][:64, :n], scalar1=QSCALE, scalar2=None, op0=ALU.mult), reads=[got['b']], writes=[tmp2])
                P.op('dve', lambda e: e.tensor_tensor(out=tmp2[:64, :n], in0=tmp2[:64, :n], in1=sint[:, :n], op=ALU.mult), reads=[tmp2, sint], writes=[tmp2])
                P.op('dve', lambda e: e.tensor_tensor(out=tmp[:64, :n], in0=got['a'][:64, :n], in1=cost[:, :n], op=ALU.mult), reads=[got['a'], cost], writes=[tmp])
                P.op('dve', lambda e: e.scalar_tensor_tensor(out=tmp[:64, :n], in0=tmp[:64, :n], scalar=QSCALE, in1=tmp2[:64, :n], op0=ALU.mult, op1=ALU.add),
                     reads=[tmp, tmp2], writes=[tmp])
                P.dma(qrT[hh, :, oc:oc + n], tmp[:64, :n], reads=[tmp], writes=[qrT], is_out=True)
        def ev_x(fc, ps, m):
            P.op('act', lambda e: e.copy(out=xbc[:, fc - XBC0 // 128, 1:n + 1], in_=ps[:, :n]), reads=[ps], writes=[xbc])
        emit_linear(P, ws, w_in, D, XBC0, DT0, h, n, ev_x)
        if kind == "lat":
            P.op('dve', lambda e: e.tensor_copy(out=xbc[:, :, 0:1], in_=xbch[:, :, 2 * hj:2 * hj + 1]), reads=[xbch], writes=[xbc])
            P.op('dve', lambda e: e.tensor_copy(out=xbc[:, :, n + 1:n + 2], in_=xbch[:, :, 2 * hj + 1:2 * hj + 2]), reads=[xbch], writes=[xbc])
        else:
            P.op('dve', lambda e: e.memset(xbc[:, :, 0:1], 0.0), writes=[xbc])
            P.op('dve', lambda e: e.memset(xbc[:, :, n + 1:n + 2], 0.0), writes=[xbc])
        for ci in range(12):
            P.op('dve', lambda e: e.tensor_scalar(out=tmp[:, :n], in0=xbc[:, ci, 0:n], scalar1=cwt[:, ci, 0:1], scalar2=None, op0=ALU.mult), reads=[xbc, cwt], writes=[tmp])
            P.op('dve', lambda e: e.scalar_tensor_tensor(out=tmp[:, :n], in0=xbc[:, ci, 1:n + 1], scalar=cwt[:, ci, 1:2], in1=tmp[:, :n], op0=ALU.mult, op1=ALU.add),
                 reads=[xbc, cwt, tmp], writes=[tmp])
            P.op('dve', lambda e: e.scalar_tensor_tensor(out=tmp[:, :n], in0=xbc[:, ci, 2:n + 2], scalar=cwt[:, ci, 2:3], in1=tmp[:, :n], op0=ALU.mult, op1=ALU.add),
                 reads=[xbc, cwt, tmp], writes=[tmp])
            P.op('act', lambda e: e.activation(out=x[:, ci, :n], in_=tmp[:, :n], func=AF.Silu, bias=cb[:, ci:ci + 1], scale=1.0), reads=[tmp, cb], writes=[x])
        P.dma(xbcT[:, oc:oc + n].rearrange("(c p) t -> p c t", p=128), x[:, 0:12, :n], reads=[x], writes=[xbcT], is_out=True)
        def ev_dt(fc, ps, m):
            P.op('act', lambda e: e.activation(out=tmp[:32, :n], in_=ps[:32, :n], func=AF.Exp, bias=dtb[:, 0:1], scale=1.0), reads=[ps, dtb], writes=[tmp])
            P.op('act', lambda e: e.activation(out=tmp[:32, :n], in_=tmp[:32, :n], func=AF.Ln, bias=1.0, scale=1.0), reads=[tmp], writes=[tmp])
            P.dma(dtT[:, oc:oc + n], tmp[:32, :n], reads=[tmp], writes=[dtT], is_out=True)
        emit_linear(P, ws, w_in, D, DT0, CKV0, h, n, ev_dt)
        def ev_ckv(fc_unused, ps, m, _st=[0]):
            i = _st[0] % 4
            _st[0] += 1
            P.op('act', lambda e: e.copy(out=cq[:, i, :n], in_=ps[:, :n]), reads=[ps], writes=[cq])
        emit_linear(P, ws, w_in, D, CKV0, KR0, h, n, ev_ckv)
        emit_rmsnorm_mod(P, c, cq, hq, kvnw, None, n, sq, rstd, nch=4)

        def ev_kv(fc, ps, m):
            P.op('act', lambda e: e.copy(out=x[:, fc, :n], in_=ps[:, :n]), reads=[ps], writes=[x])
        emit_linear(P, ws, w_ukv, 512, 0, 2048, hq, n, ev_kv)
        P.dma(kvT[:, oc:oc + n].rearrange("(c p) t -> p c t", p=128), x[:, :, :n], reads=[x], writes=[kvT], is_out=True)
        got = {}

        def ev_a(fc, ps, m):
            got['a'] = ps

        def ev_b(fc, ps, m):
            got['b'] = ps
        emit_linear(P, ws, w_in, D, KR0, 3680, h, n, ev_a)
        emit_linear(P, ws, w_in_sw, D, 0, 64, h, n, ev_b)
        P.op('dve', lambda e: e.tensor_tensor(out=tmp2[:64, :n], in0=got['b'][:64, :n], in1=sint[:, :n], op=ALU.mult), reads=[got['b'], sint], writes=[tmp2])
        P.op('dve', lambda e: e.tensor_tensor(out=tmp[:64, :n], in0=got['a'][:64, :n], in1=cost[:, :n], op=ALU.mult), reads=[got['a'], cost], writes=[tmp])
        P.op('dve', lambda e: e.tensor_tensor(out=tmp[:64, :n], in0=tmp[:64, :n], in1=tmp2[:64, :n], op=ALU.add), reads=[tmp, tmp2], writes=[tmp])
        P.dma(krT[:, oc:oc + n], tmp[:64, :n], reads=[tmp], writes=[krT], is_out=True)
    return P


def rope_tables(t0, n):
    t = np.arange(t0, t0 + n)
    row = (t // 64).astype(np.float32)
    col = (t % 64).astype(np.float32)
    inv = (np.float32(10000.0) ** (-np.arange(16, dtype=np.float32) / np.float32(16))).astype(np.float32)
    cosT = np.zeros((64, n), np.float32)
    sinT = np.zeros((64, n), np.float32)
    for d in range(64):
        pos = row if d < 32 else col
        ang = (pos * inv[d % 16]).astype(np.float32)
        cosT[d] = np.cos(ang)
        sinT[d] = np.sin(ang) * (-1.0 if (d % 32) < 16 else 1.0)
    return cosT, sinT


def swap_cols64(w):
    idx = np.array([d + 16 if (d % 32) < 16 else d - 16 for d in range(64)])
    return np.ascontiguousarray(w[..., idx])


def split_mod(m):
    return [np.ascontiguousarray(m[i * D:(i + 1) * D]) for i in range(6)]


def run_l1(inp, mod):
    P = build_l1()
    x = inp['x'][0]
    ctx = inp['ctx'][0]
    xT = np.ascontiguousarray(x.T)
    ctxT = np.ascontiguousarray(ctx.T)
    sh1, sc1 = split_mod(mod[0, 0])[:2]
    csh1, csc1 = split_mod(mod[0, 1])[:2]
    w_in = np.ascontiguousarray(inp['ev_w_in'][0])
    w_uq = np.ascontiguousarray(inp['ev_w_uq'][0])
    w_uq_sw = np.concatenate([swap_cols64(w_uq[:, 192 * h + 128:192 * h + 192]) for h in range(8)], axis=1)
    common = {
        "w_in": w_in, "w_in_sw": swap_cols64(w_in[:, KR0:3680]), "w_uq": w_uq, "w_uq_sw": np.ascontiguousarray(w_uq_sw),
        "w_ukv": np.ascontiguousarray(inp['ev_w_ukv'][0]),
        "nmw": vec_layout(inp['norm_mix_w'][0]), "sh1": vec_layout(sh1), "sc1": vec_layout(sc1),
        "csh1": vec_layout(csh1), "csc1": vec_layout(csc1),
        "qnw": vec_layout(inp['ev_q_norm_w'][0]), "kvnw": vec_layout(inp['ev_kv_norm_w'][0]),
        "convw": np.ascontiguousarray(inp['ev_conv_w'][0].T.reshape(12, 128, 3).transpose(1, 0, 2)),
        "convb": vec_layout(inp['ev_conv_b'][0]),
        "dtb": np.ascontiguousarray(inp['ev_dt_bias'][0].reshape(32, 1)),
    }
    in_maps = []
    for i in range(NCORE):
        t0 = i * TPC
        halo = np.zeros((D, 8), np.float32)
        hmask = np.zeros((128, 8), np.float32)
        for j in range(4):
            for s, tok in ((0, t0 + TT * j - 1), (1, t0 + TT * j + TT)):
                if 0 <= tok < SEQ:
                    halo[:, 2 * j + s] = xT[:, tok]
                    hmask[:, 2 * j + s] = 1.0
        xh = np.concatenate([halo, xT[:, t0:t0 + TPC], ctxT], axis=1)
        cosT, sinT = rope_tables(t0, TPC)
        cosT = np.concatenate([cosT, np.ones((64, CTX), np.float32)], axis=1)
        sinT = np.concatenate([sinT, np.zeros((64, CTX), np.float32)], axis=1)
        m = dict(common)
        m.update({"xh": np.ascontiguousarray(xh), "hmask": hmask, "cosT": np.ascontiguousarray(cosT), "sinT": np.ascontiguousarray(sinT)})
        in_maps.append(m)
    res = run_spmd(P, in_maps)
    o = {}
    o['zT'] = np.concatenate([r['zT'] for r in res], axis=1)
    o['qnT'] = np.concatenate([r['qnT'] for r in res], axis=2)
    o['qrT'] = np.concatenate([r['qrT'] for r in res], axis=2)
    for k in ['xbcT', 'dtT', 'kvT', 'krT']:
        lat = np.concatenate([r[k][:, :TPC] for r in res], axis=1)
        o[k] = np.concatenate([res[0][k][:, TPC:], lat], axis=1)
    return o


NT = CTX + SEQ
NCH = NT // 128
QB = 512


def build_l2():
    P = Prog()
    P.psum_ring = 6
    c = emit_consts(P)
    ring = P.psum[:6]
    state = {'n': 0}

    def ps():
        p = ring[state['n'] % 6]
        state['n'] += 1
        return p
    ps_o, ps_s = P.psum[6], P.psum[7]
    din = {}
    for d_ in "fb":
        din["xs_" + d_] = P.din("xs_" + d_, [NT, 128])
        din["dt_" + d_] = P.din("dt_" + d_, [NT, 2])
        din["Bt_" + d_] = P.din("Bt_" + d_, [128, NT])
        din["Ct_" + d_] = P.din("Ct_" + d_, [128, NT])
        din["Bk_" + d_] = P.din("Bk_" + d_, [NT, 128])
    alog = P.din("alog", [128, 4])
    tri_d = P.din("tri", [128, 128])
    mneg_d = P.din("mneg", [128, 128])
    qa_d = P.din("qa", [128, SEQ])
    qb_d = P.din("qb", [64, SEQ])
    ka_d = P.din("ka", [128, NT])
    kb_d = P.din("kb", [64, NT])
    v_d = P.din("v", [NT, 128])
    y_out = {d_: P.dout("y_" + d_, [SEQ, 128]) for d_ in "fb"}
    oT = P.dout("oT", [128, SEQ])
    tri = P.sb("tri_sb", [128, 128])
    mneg = P.sb("mneg_sb", [128, 128])
    P.dma(tri[:], tri_d[:], reads=[tri_d], writes=[tri])
    P.dma(mneg[:], mneg_d[:], reads=[mneg_d], writes=[mneg])
    A = P.sb("A_sb", [128, 4])
    P.dma(A[:], alog[:], reads=[alog], writes=[A])
    P.op('act', lambda e: e.activation(out=A[:], in_=A[:], func=AF.Exp), reads=[A], writes=[A])
    P.op('dve', lambda e: e.tensor_scalar(out=A[:], in0=A[:], scalar1=-1.0, scalar2=None, op0=ALU.mult), reads=[A], writes=[A])
    ones_b = P.sb("ones_b", [128, 128], BF16)
    P.op('dve', lambda e: e.memset(ones_b[:], 1.0), writes=[ones_b])

    hst = {}
    hbf = {}
    for d_ in "fb":
        for hh in range(2):
            hst[d_, hh] = P.sb("h_%s%d" % (d_, hh), [128, 64])
            hbf[d_, hh] = P.sb("hb_%s%d" % (d_, hh), [128, 64], BF16)
            P.op('dve', lambda e: e.memset(hst[d_, hh][:], 0.0), writes=[hst[d_, hh]])
            P.op('dve', lambda e: e.memset(hbf[d_, hh][:], 0.0), writes=[hbf[d_, hh]])
    NB = 2
    tl = []
    for b in range(NB):
        t = {}
        for nm, shp, dt_ in [("xc", [128, 128], F32), ("dtc", [128, 2], F32), ("btc", [128, 128], F32), ("ctc", [128, 128], F32), ("bkc", [128, 128], F32),
                             ("bt_bf", [128, 128], BF16), ("ct_bf", [128, 128], BF16), ("bk_bf", [128, 128], BF16),
                             ("a_t", [128, 2], F32), ("nacum", [128, 2], F32), ("eac", [128, 2], F32), ("ysb", [128, 128], F32)]:
            t[nm] = P.sb("%s_%d" % (nm, b), shp, dt_)
        for hh in range(2):
            for nm, shp, dt_ in [("abc", [128, 128], F32), ("alast", [128, 1], F32), ("seg", [128, 128], F32), ("Lm", [128, 128], F32),
                                 ("M", [128, 128], BF16), ("xdt", [128, 64], BF16), ("yd", [128, 64], F32), ("toend", [128, 1], F32),
                                 ("w2", [128, 1], F32), ("xw", [128, 64], BF16), ("cd", [128, 1], F32)]:
                t[nm, hh] = P.sb("%s_%d_%d" % (nm, b, hh), shp, dt_)
        tl.append(t)

    def ssd_unit(d_, ci, t):
        di = 0 if d_ == "f" else 1
        t0 = ci * 128
        lat = ci >= CTX // 128
        P.dma(t["xc"][:], din["xs_" + d_][t0:t0 + 128, :], reads=[din["xs_" + d_]], writes=[t["xc"]])
        P.dma(t["dtc"][:], din["dt_" + d_][t0:t0 + 128, :], reads=[din["dt_" + d_]], writes=[t["dtc"]])
        P.dma(t["btc"][:], din["Bt_" + d_][:, t0:t0 + 128], reads=[din["Bt_" + d_]], writes=[t["btc"]])
        P.dma(t["ctc"][:], din["Ct_" + d_][:, t0:t0 + 128], reads=[din["Ct_" + d_]], writes=[t["ctc"]])
        P.dma(t["bkc"][:], din["Bk_" + d_][t0:t0 + 128, :], reads=[din["Bk_" + d_]], writes=[t["bkc"]])
        for s, dd in (("btc", "bt_bf"), ("ctc", "ct_bf"), ("bkc", "bk_bf")):
            P.op('pool', lambda e: e.tensor_copy(out=t[dd][:], in_=t[s][:]), reads=[t[s]], writes=[t[dd]])
        P.op('dve', lambda e: e.tensor_tensor(out=t["a_t"][:], in0=t["dtc"][:], in1=A[:, 2 * di:2 * di + 2], op=ALU.mult), reads=[t["dtc"], A], writes=[t["a_t"]])
        p_acj = ps()
        P.op('pe', lambda e: e.matmul(p_acj[:, 0:2], tri[:], t["a_t"][:], start=True, stop=True), reads=[tri, t["a_t"]], writes=[p_acj])
        P.op('dve', lambda e: e.tensor_scalar(out=t["nacum"][:], in0=p_acj[:, 0:2], scalar1=-1.0, scalar2=None, op0=ALU.mult), reads=[p_acj], writes=[t["nacum"]])
        P.op('act', lambda e: e.activation(out=t["eac"][:], in_=p_acj[:, 0:2], func=AF.Exp), reads=[p_acj], writes=[t["eac"]])
        if lat:
            p_g = ps()
            P.op('pe', lambda e: e.matmul(p_g[:, 0:128], t["bt_bf"][:], t["ct_bf"][:], start=True, stop=True), reads=[t["bt_bf"], t["ct_bf"]], writes=[p_g])
        for hh in range(2):
            hs = slice(64 * hh, 64 * hh + 64)
            abc, alast, seg, Lm, M, xdt, yd, toend, w2, xw, cd = [t[nm, hh] for nm in ("abc", "alast", "seg", "Lm", "M", "xdt", "yd", "toend", "w2", "xw", "cd")]
            P.op('dve', lambda e: e.tensor_scalar(out=abc[:], in0=c['ones_f'][:], scalar1=t["a_t"][:, hh:hh + 1], scalar2=None, op0=ALU.mult),
                 reads=[c['ones_f'], t["a_t"]], writes=[abc])
            p_row = ps()
            P.op('pe', lambda e: e.matmul(p_row[:, 0:128], abc[:], tri[:], start=True, stop=True), reads=[abc, tri], writes=[p_row])
            P.op('act', lambda e: e.copy(out=alast[:], in_=p_row[:, 127:128]), reads=[p_row], writes=[alast])
            if lat:
                P.op('dve', lambda e: e.tensor_tensor(out=seg[:], in0=p_row[:, 0:128], in1=mneg[:], op=ALU.add), reads=[p_row, mneg], writes=[seg])
                P.op('act', lambda e: e.activation(out=Lm[:], in_=seg[:], func=AF.Exp, bias=t["nacum"][:, hh:hh + 1], scale=1.0), reads=[seg, t["nacum"]], writes=[Lm])
                P.op('dve', lambda e: e.tensor_tensor(out=M[:], in0=p_g[:, 0:128], in1=Lm[:], op=ALU.mult), reads=[p_g, Lm], writes=[M])
                P.op('dve', lambda e: e.tensor_scalar(out=xdt[:], in0=t["xc"][:, hs], scalar1=t["dtc"][:, hh:hh + 1], scalar2=None, op0=ALU.mult),
                     reads=[t["xc"], t["dtc"]], writes=[xdt])
                p_yd = ps()
                P.op('pe', lambda e: e.matmul(p_yd[:, 0:64], M[:], xdt[:], start=True, stop=True), reads=[M, xdt], writes=[p_yd])
                p_yo = ps()
                P.op('pe', lambda e: e.matmul(p_yo[:, 0:64], t["ct_bf"][:], hbf[d_, hh][:], start=True, stop=True), reads=[t["ct_bf"], hbf[d_, hh]], writes=[p_yo])
                P.op('act', lambda e: e.copy(out=yd[:], in_=p_yd[:, 0:64]), reads=[p_yd], writes=[yd])
                P.op('dve', lambda e: e.scalar_tensor_tensor(out=t["ysb"][:, hs], in0=p_yo[:, 0:64], scalar=t["eac"][:, hh:hh + 1], in1=yd[:], op0=ALU.mult, op1=ALU.add),
                     reads=[p_yo, t["eac"], yd], writes=[t["ysb"]])
            P.op('act', lambda e: e.activation(out=toend[:], in_=t["nacum"][:, hh:hh + 1], func=AF.Exp, bias=alast[:, 0:1], scale=1.0), reads=[t["nacum"], alast], writes=[toend])
            P.op('dve', lambda e: e.tensor_tensor(out=w2[:], in0=toend[:], in1=t["dtc"][:, hh:hh + 1], op=ALU.mult), reads=[toend, t["dtc"]], writes=[w2])
            P.op('dve', lambda e: e.tensor_scalar(out=xw[:], in0=t["xc"][:, hs], scalar1=w2[:, 0:1], scalar2=None, op0=ALU.mult), reads=[t["xc"], w2], writes=[xw])
            p_st = ps()
            P.op('pe', lambda e: e.matmul(p_st[:, 0:64], t["bk_bf"][:], xw[:], start=True, stop=True), reads=[t["bk_bf"], xw], writes=[p_st])
            P.op('act', lambda e: e.activation(out=cd[:], in_=alast[:], func=AF.Exp), reads=[alast], writes=[cd])
            P.op('dve', lambda e: e.scalar_tensor_tensor(out=hst[d_, hh][:], in0=hst[d_, hh][:], scalar=cd[:, 0:1], in1=p_st[:, 0:64], op0=ALU.mult, op1=ALU.add),
                 reads=[hst[d_, hh], cd, p_st], writes=[hst[d_, hh]])
            P.op('pool', lambda e: e.tensor_copy(out=hbf[d_, hh][:], in_=hst[d_, hh][:]), reads=[hst[d_, hh]], writes=[hbf[d_, hh]])
        if lat:
            o0 = (ci - CTX // 128) * 128
            P.dma(y_out[d_][o0:o0 + 128, :], t["ysb"][:], reads=[t["ysb"]], writes=[y_out[d_]], is_out=True)

    def ssd_gen():
        n = 0
        for ci in range(NCH):
            for d_ in "fb":
                ssd_unit(d_, ci, tl[n % NB])
                n += 1
                yield

    ka = P.sb("ka_bf", [128, NT], BF16)
    kb = P.sb("kb_bf", [64, NT], BF16)
    vb = P.sb("v_bf", [128, NCH, 128], BF16)
    stg = [P.sb("astg%d" % i, [128, 1280]) for i in range(2)]
    qa = [P.sb("qa%d" % i, [128, QB], BF16) for i in range(2)]
    qb = [P.sb("qb%d" % i, [64, QB], BF16) for i in range(2)]
    Et = [P.sb("E%d" % i, [128, QB], BF16) for i in range(3)]
    osb = P.sb("osb", [128, QB])
    rs = P.sb("rs", [128, QB])

    def attn_gen():
        n = 0
        CH = 1040
        for i in range(NT // CH):
            s = stg[n % 2]
            n += 1
            P.dma(s[:, :CH], ka_d[:, i * CH:(i + 1) * CH], reads=[ka_d], writes=[s])
            P.op('act', lambda e: e.copy(out=ka[:, i * CH:(i + 1) * CH], in_=s[:, :CH]), reads=[s], writes=[ka])
            s = stg[n % 2]
            n += 1
            P.dma(s[:64, :CH], kb_d[:, i * CH:(i + 1) * CH], reads=[kb_d], writes=[s])
            P.op('act', lambda e: e.copy(out=kb[:, i * CH:(i + 1) * CH], in_=s[:64, :CH]), reads=[s], writes=[kb])
        vv = v_d[:, :].rearrange("(kt p) d -> p kt d", p=128)
        for i in range(NCH // 10):
            s = stg[n % 2]
            n += 1
            P.dma(s[:, :1280].rearrange("p (k d) -> p k d", k=10), vv[:, i * 10:(i + 1) * 10, :], reads=[v_d], writes=[s])
            P.op('act', lambda e: e.copy(out=vb[:, i * 10:(i + 1) * 10, :], in_=s[:, :1280].rearrange("p (k d) -> p k d", k=10)), reads=[s], writes=[vb])
        yield
        ne = 0
        for qi in range(SEQ // QB):
            q0 = qi * QB
            qat, qbt = qa[qi % 2], qb[qi % 2]
            s = stg[n % 2]
            n += 1
            P.dma(s[:, :QB], qa_d[:, q0:q0 + QB], reads=[qa_d], writes=[s])
            P.op('act', lambda e: e.copy(out=qat[:], in_=s[:, :QB]), reads=[s], writes=[qat])
            s = stg[n % 2]
            n += 1
            P.dma(s[:64, :QB], qb_d[:, q0:q0 + QB], reads=[qb_d], writes=[s])
            P.op('act', lambda e: e.copy(out=qbt[:], in_=s[:64, :QB]), reads=[s], writes=[qbt])
            for kt in range(NCH):
                p_s = ps()
                P.op('pe', lambda e: e.matmul(p_s[:, :QB], ka[:, kt * 128:(kt + 1) * 128], qat[:], start=True, stop=False), reads=[ka, qat], writes=[p_s])
                P.op('pe', lambda e: e.matmul(p_s[:, :QB], kb[:, kt * 128:(kt + 1) * 128], qbt[:], start=False, stop=True), reads=[kb, qbt], writes=[p_s])
                E = Et[ne % 3]
                ne += 1
                P.op('act', lambda e: e.activation(out=E[:], in_=p_s[:, :QB], func=AF.Exp), reads=[p_s], writes=[E])
                P.op('pe', lambda e: e.matmul(ps_o[:, :QB], vb[:, kt, :], E[:], start=(kt == 0), stop=(kt == NCH - 1)), reads=[vb, E], writes=[ps_o])
                P.op('pe', lambda e: e.matmul(ps_s[:, :QB], ones_b[:], E[:], start=(kt == 0), stop=(kt == NCH - 1)), reads=[ones_b, E], writes=[ps_s])
                yield
            P.op('dve', lambda e: e.reciprocal(out=rs[:], in_=ps_s[:, :QB]), reads=[ps_s], writes=[rs])
            P.op('dve', lambda e: e.tensor_tensor(out=osb[:], in0=ps_o[:, :QB], in1=rs[:], op=ALU.mult), reads=[ps_o, rs], writes=[osb])
            P.dma(oT[:, q0:q0 + QB], osb[:], reads=[osb], writes=[oT], is_out=True)

    ag = attn_gen()
    sg = ssd_gen()
    a_done = s_done = False
    k = 0
    while not (a_done and s_done):
        if not a_done:
            try:
                next(ag)
            except StopIteration:
                a_done = True
        k += 1
        if (k % 16 == 0 or a_done) and not s_done:
            try:
                next(sg)
            except StopIteration:
                s_done = True
    return P


def run_l2(inp, o1):
    P = build_l2()
    xbcT, dtT, kvT, krT = o1['xbcT'], o1['dtT'], o1['kvT'], o1['krT']
    idx_b = np.concatenate([np.arange(CTX - 1, -1, -1), CTX + np.arange(SEQ - 1, -1, -1)])
    a_log = inp['ev_a_log'][0]
    tri = np.triu(np.ones((128, 128), np.float32))
    mneg = np.where(np.arange(128)[:, None] <= np.arange(128)[None, :], 0.0, -30000.0).astype(np.float32)
    in_maps = []
    for i in range(NCORE):
        g = i // 4
        xs_f = np.ascontiguousarray(xbcT[128 * i:128 * i + 128, :].T)
        dt_f = np.ascontiguousarray(dtT[2 * i:2 * i + 2, :].T)
        dt_b = np.ascontiguousarray(dtT[16 + 2 * i:16 + 2 * i + 2, :].T[idx_b])
        Bt_f = np.ascontiguousarray(xbcT[1024 + 128 * g:1024 + 128 * g + 128, :])
        Ct_f = np.ascontiguousarray(xbcT[1280 + 128 * g:1280 + 128 * g + 128, :])
        Bt_b = np.ascontiguousarray(Bt_f[:, idx_b])
        Ct_b = np.ascontiguousarray(Ct_f[:, idx_b])
        al = np.array([a_log[0, 2 * i], a_log[0, 2 * i + 1], a_log[1, 2 * i], a_log[1, 2 * i + 1]], np.float32)
        m = {
            "xs_f": xs_f, "xs_b": np.ascontiguousarray(xs_f[idx_b]), "dt_f": dt_f, "dt_b": dt_b,
            "Bt_f": Bt_f, "Bt_b": Bt_b, "Ct_f": Ct_f, "Ct_b": Ct_b,
            "Bk_f": np.ascontiguousarray(Bt_f.T), "Bk_b": np.ascontiguousarray(Bt_b.T),
            "alog": np.ascontiguousarray(np.tile(al[None, :], (128, 1))), "tri": tri, "mneg": mneg,
            "qa": np.ascontiguousarray(o1['qnT'][i]), "qb": np.ascontiguousarray(o1['qrT'][i]),
            "ka": np.ascontiguousarray(kvT[256 * i:256 * i + 128, :]), "kb": np.ascontiguousarray(krT),
            "v": np.ascontiguousarray(kvT[256 * i + 128:256 * i + 256, :].T),
        }
        in_maps.append(m)
    res = run_spmd(P, in_maps)
    o = {}
    o['yfT'] = np.ascontiguousarray(np.concatenate([r['y_f'] for r in res], axis=1).T)
    o['ybT'] = np.ascontiguousarray(np.concatenate([r['y_b'][::-1] for r in res], axis=1).T)
    o['oT'] = np.concatenate([r['oT'] for r in res], axis=0)
    return o


def build_l4():
    P = Prog()
    c = emit_consts(P)
    xT = P.din("xT", [D, TPC])
    srcs = {k: P.din(k, [1024, TPC]) for k in ["zT", "yfT", "ybT", "xsT", "oT"]}
    dsk_d = P.din("dsk", [128, 8, 2])
    w_o = P.din("w_o", [D, D])
    w1 = P.din("w1", [D, DFF])
    w3 = P.din("w3", [D, DFF])
    w2 = P.din("w2", [DFF, D])
    hw = P.din("hy_w_in", [D, 3 * D])
    x2T = P.dout("x2T", [D, TPC])
    projT = P.dout("projT", [3 * D, TPC])
    snw = load_vec(P, "snw", 8)
    g1 = load_vec(P, "g1")
    nfw = load_vec(P, "nfw")
    sc2 = load_vec(P, "sc2")
    sh2 = load_vec(P, "sh2")
    g2 = load_vec(P, "g2")
    nmw1 = load_vec(P, "nmw1")
    sc1b = load_vec(P, "sc1b")
    sh1b = load_vec(P, "sh1b")
    hb = load_vec(P, "hb", 48)
    dsk2 = P.sb("dsk2", [128, 8, 2])
    P.dma(dsk2[:], dsk_d[:], reads=[dsk_d], writes=[dsk2])
    dsk = P.sb("dsk_sum", [128, 8])
    P.op('dve', lambda e: e.tensor_tensor(out=dsk[:], in0=dsk2[:, :, 0], in1=dsk2[:, :, 1], op=ALU.add), reads=[dsk2], writes=[dsk])
    scl2 = P.sb("scl2", [128, 16])
    P.op('dve', lambda e: e.scalar_tensor_tensor(out=scl2[:], in0=sc2[:], scalar=1.0, in1=nfw[:], op0=ALU.add, op1=ALU.mult), reads=[sc2, nfw], writes=[scl2])
    scl1b = P.sb("scl1b", [128, 16])
    P.op('dve', lambda e: e.scalar_tensor_tensor(out=scl1b[:], in0=sc1b[:], scalar=1.0, in1=nmw1[:], op0=ALU.add, op1=ALU.mult), reads=[sc1b, nmw1], writes=[scl1b])
    ws = WStream(P)
    x = P.sb("x", [128, 16, TT])
    h = P.sb("h", [128, 16, TT], BF16)
    u = P.sb("u", [128, DFF // 128, TT], BF16)
    gy = P.sb("gy", [128, 8, TT])
    sq = P.sb("sq", [128, TT])
    rstd = P.sb("rstd", [128, TT])
    tmp = P.sb("tmp", [128, TT])
    ld = {k: P.sb("ld_" + k, [128, TT]) for k in ["zT", "yfT", "ybT", "xsT"]}
    stage = [P.sb("stage%d" % i, [128, TT]) for i in range(2)]
    xv = xT[:, :].rearrange("(kc p) t -> p kc t", p=128)
    x2v = x2T[:, :].rearrange("(kc p) t -> p kc t", p=128)
    sv = {k: v[:, :].rearrange("(kc p) t -> p kc t", p=128) for k, v in srcs.items()}
    pv = projT[:, :].rearrange("(kc p) t -> p kc t", p=128)
    ns = 0
    for tt in range(TPC // TT):
        t0 = tt * TT
        P.dma(x[:], xv[:, :, t0:t0 + TT], reads=[xT], writes=[x])
        for kc in range(8):
            for k in ["zT", "yfT", "ybT", "xsT"]:
                P.dma(ld[k][:], sv[k][:, kc, t0:t0 + TT], reads=[srcs[k]], writes=[ld[k]])
            P.op('dve', lambda e: e.tensor_tensor(out=tmp[:], in0=ld["yfT"][:], in1=ld["ybT"][:], op=ALU.add), reads=[ld["yfT"], ld["ybT"]], writes=[tmp])
            P.op('dve', lambda e: e.scalar_tensor_tensor(out=tmp[:], in0=ld["xsT"][:], scalar=dsk[:, kc:kc + 1], in1=tmp[:], op0=ALU.mult, op1=ALU.add),
                 reads=[ld["xsT"], dsk, tmp], writes=[tmp])
            P.op('act', lambda e: e.activation(out=sq[:], in_=ld["zT"][:], func=AF.Silu), reads=[ld["zT"]], writes=[sq])
            P.op('dve', lambda e: e.tensor_tensor(out=gy[:, kc, :], in0=tmp[:], in1=sq[:], op=ALU.mult), reads=[tmp, sq], writes=[gy])
            P.dma(ld["zT"][:], sv["oT"][:, kc, t0:t0 + TT], reads=[srcs["oT"]], writes=[ld["zT"]])
            P.op('act', lambda e: e.copy(out=h[:, kc, :], in_=ld["zT"][:]), reads=[ld["zT"]], writes=[h])
        for g in range(2):
            emit_rmsnorm_mod(P, c, gy, h, snw, None, TT, sq, rstd, nch=4, dim=512, ki=4 * g, ko=8 + 4 * g, ks=4 * g)

        def ev_o(dc, ps, m):
            P.op('dve', lambda e: e.scalar_tensor_tensor(out=x[:, dc, :], in0=ps[:, :TT], scalar=g1[:, dc:dc + 1], in1=x[:, dc, :], op0=ALU.mult, op1=ALU.add),
                 reads=[ps, g1, x], writes=[x])
        emit_linear(P, ws, w_o, D, 0, D, h, TT, ev_o)
        emit_rmsnorm_mod(P, c, x, h, scl2, sh2, TT, sq, rstd)
        emit_ffn(P, c, ws, x, h, u, tmp, w1, w3, w2, g2, TT)
        P.dma(x2v[:, :, t0:t0 + TT], x[:], reads=[x], writes=[x2T], is_out=True)
        emit_rmsnorm_mod(P, c, x, h, scl1b, sh1b, TT, sq, rstd)

        def ev_p(fc, ps, m):
            nonlocal ns
            st = stage[ns % 2]
            ns += 1
            P.op('act', lambda e: e.activation(out=st[:], in_=ps[:, :TT], func=AF.Identity, bias=hb[:, fc:fc + 1], scale=1.0), reads=[ps, hb], writes=[st])
            P.dma(pv[:, fc, t0:t0 + TT], st[:], reads=[st], writes=[projT], is_out=True)
        emit_linear(P, ws, hw, D, 0, 3 * D, h, TT, ev_p)
    return P


def tok_shard(a, i):
    return np.ascontiguousarray(a[:, i * TPC:(i + 1) * TPC])


def run_l4(inp, mod, o1, o2):
    P = build_l4()
    xT = np.ascontiguousarray(inp['x'][0].T)
    m0 = split_mod(mod[0, 0])
    m1 = split_mod(mod[1, 0])
    dsk = np.repeat(inp['ev_d_skip'][0].T, 64, axis=0)
    common = {
        "dsk": np.ascontiguousarray(dsk.reshape(8, 128, 2).transpose(1, 0, 2)),
        "w_o": np.ascontiguousarray(inp['ev_w_o'][0]), "w1": np.ascontiguousarray(inp['ffn_w1'][0]), "w3": np.ascontiguousarray(inp['ffn_w3'][0]),
        "w2": np.ascontiguousarray(inp['ffn_w2'][0]), "hy_w_in": np.ascontiguousarray(inp['hy_w_in'][0]),
        "snw": vec_layout(inp['ev_ssd_norm_w'][0]), "g1": vec_layout(m0[2]), "nfw": vec_layout(inp['norm_ffn_w'][0]),
        "sc2": vec_layout(m0[4]), "sh2": vec_layout(m0[3]), "g2": vec_layout(m0[5]),
        "nmw1": vec_layout(inp['norm_mix_w'][1]), "sc1b": vec_layout(m1[1]), "sh1b": vec_layout(m1[0]),
        "hb": vec_layout(inp['hy_b_in'][0]),
    }
    xsT = o1['xbcT'][:1024, CTX:]
    in_maps = []
    for i in range(NCORE):
        m = dict(common)
        m.update({"xT": tok_shard(xT, i), "zT": tok_shard(o1['zT'], i), "yfT": tok_shard(o2['yfT'], i), "ybT": tok_shard(o2['ybT'], i),
                  "xsT": tok_shard(xsT, i), "oT": tok_shard(o2['oT'], i)})
        in_maps.append(m)
    res = run_spmd(P, in_maps)
    return {"x2T": np.concatenate([r['x2T'] for r in res], axis=1), "projT": np.concatenate([r['projT'] for r in res], axis=1)}


F32R = mybir.dt.float32r


def r32(ap):
    return ap.bitcast(F32R)


CPC = D // NCORE
CBK = 32
NFFT = 2 * SEQ
PI = float(np.pi)


def build_l5():
    P = Prog()
    c = emit_consts(P)
    pj = P.din("pj", [3, CPC, SEQ + 2])
    sw_d = P.din("sw", [128, 6, 3])
    sb_d = P.din("sb", [128, 6])
    embs = [P.din("embT", [33, SEQ]), P.din("embTr", [33, SEQ])]
    fw1_d = P.din("fw1", [33, 64])
    fwm_d = P.din("fwm", [2, 64, 64])
    fq_d = P.din("fq", [64, 1])
    fb_d = P.din("fb", [64, 3])
    fwo_d = P.din("fwo", [64, 2, 2, CPC])
    dpp_d = P.din("dpp", [128, 2])
    ident_d = P.din("ident", [128, 128])
    trows = [P.din("trow", [128, SEQ]), P.din("trowr", [128, SEQ])]
    fbias_d = P.din("fbias", [128, 2, CPC])
    cn = {}
    for nm, shp in [("W1", [2, 128, 512]), ("TWrr", [128, 512]), ("TWii", [128, 512]), ("W2r", [128, 128]), ("W2i", [128, 128]),
                    ("WIa", [128, 256]), ("WIb", [128, 256]), ("TWIrr", [2, 128, 256]), ("TWIii", [2, 128, 256]),
                    ("Cos", [2, 128, 128]), ("NSin", [2, 128, 128])]:
        cn[nm] = (P.din(nm, shp), shp)
    y2 = P.dout("y2", [CPC, SEQ])
    cv = P.dscratch("cv", [3, CPC, SEQ])
    kd = P.dscratch("kd", [2, 2, CPC, SEQ])
    rn = P.sb("rn", [128, 2, CPC])
    fbias = P.sb("fbias_sb", [128, 2, CPC])
    P.dma(fbias[:], fbias_d[:], reads=[fbias_d], writes=[fbias])

    P.push()
    _sc = P.nc.named_scope("phaseA")
    _sc.__enter__()
    swt = P.sb("swt", [128, 6, 3])
    sbt = P.sb("sbt", [128, 6])
    P.dma(swt[:], sw_d[:], reads=[sw_d], writes=[swt])
    P.dma(sbt[:], sb_d[:], reads=[sb_d], writes=[sbt])
    HC = SEQ // 2
    U = [P.sb("U%d" % i, [128, HC + 2]) for i in range(2)]
    V = [P.sb("V%d" % i, [128, HC]) for i in range(2)]
    n = 0
    for s_ in range(3):
        for gq in range(2):
            for hc in range(2):
                u_, v_ = U[n % 2], V[n % 2]
                n += 1
                P.dma(u_[:], pj[s_, gq * 128:(gq + 1) * 128, hc * HC:hc * HC + HC + 2], reads=[pj], writes=[u_])
                k = s_ * 2 + gq
                for b0 in range(0, HC, 2048):
                    sl = slice(b0, b0 + 2048)
                    P.op('dve', lambda e: e.tensor_scalar(out=v_[:, sl], in0=u_[:, b0:b0 + 2048], scalar1=swt[:, k, 0:1], scalar2=None, op0=ALU.mult), reads=[u_, swt], writes=[v_])
                    P.op('dve', lambda e: e.scalar_tensor_tensor(out=v_[:, sl], in0=u_[:, b0 + 1:b0 + 2049], scalar=swt[:, k, 1:2], in1=v_[:, sl], op0=ALU.mult, op1=ALU.add),
                         reads=[u_, swt, v_], writes=[v_])
                    P.op('dve', lambda e: e.scalar_tensor_tensor(out=v_[:, sl], in0=u_[:, b0 + 2:b0 + 2050], scalar=swt[:, k, 2:3], in1=v_[:, sl], op0=ALU.mult, op1=ALU.add),
                         reads=[u_, swt, v_], writes=[v_])
                    P.op('act', lambda e: e.activation(out=v_[:, sl], in_=v_[:, sl], func=AF.Identity, bias=sbt[:, k:k + 1], scale=1.0), reads=[v_, sbt], writes=[v_])
                P.dma(cv[s_, gq * 128:(gq + 1) * 128, hc * HC:(hc + 1) * HC], v_[:], reads=[v_], writes=[cv])
    P.pop()
    _sc.__exit__(None, None, None)

    P.push()
    _sc = P.nc.named_scope("phaseB")
    _sc.__enter__()
    fw1 = P.sb("fw1_sb", [33, 64])
    fwm = P.sb("fwm_sb", [64, 2, 64])
    fq = P.sb("fq_sb", [64, 1])
    fb = P.sb("fb_sb", [64, 3])
    fbq = P.sb("fbq", [64, 3])
    fwo = P.sb("fwo_sb", [64, 2, 2, CPC])
    dpp = P.sb("dpp_sb", [128, 2])
    ident = P.sb("ident_sb", [128, 128])
    P.dma(fw1[:], fw1_d[:], reads=[fw1_d], writes=[fw1])
    P.dma(fwm[:], fwm_d[:].rearrange("j a b -> a j b"), reads=[fwm_d], writes=[fwm])
    P.dma(fq[:], fq_d[:], reads=[fq_d], writes=[fq])
    P.dma(fb[:], fb_d[:], reads=[fb_d], writes=[fb])
    P.dma(fwo[:], fwo_d[:], reads=[fwo_d], writes=[fwo])
    P.dma(dpp[:], dpp_d[:], reads=[dpp_d], writes=[dpp])
    P.dma(ident[:], ident_d[:], reads=[ident_d], writes=[ident])
    P.op('dve', lambda e: e.tensor_scalar(out=fbq[:], in0=fb[:], scalar1=fq[:, 0:1], scalar2=None, op0=ALU.mult), reads=[fb, fq], writes=[fbq])
    hid = P.sb("hid", [64, SEQ])
    embt = [P.sb("embt%d" % i, [33, 512]) for i in range(2)]
    arg = [P.sb("arg%d" % i, [64, 512]) for i in range(2)]
    hcur = [P.sb("hcur%d" % i, [64, 512]) for i in range(2)]
    wr1 = P.sb("wr1", [64, 512])
    wr2 = P.sb("wr2", [64, 512])
    trt = [P.sb("trt%d" % i, [128, 512]) for i in range(2)]
    wint = [P.sb("wint%d" % i, [128, 512]) for i in range(2)]
    kstg = [P.sb("kstg%d" % i, [128, 2048]) for i in range(2)]
    acc = {(o, dc): P.sb("acc_%d_%d" % (o, dc), [128, 64]) for o in range(2) for dc in range(2)}

    def sin_layer(ps, j, dst_ap, dst_tr, a):
        P.op('dve', lambda e: e.tensor_scalar(out=a[:], in0=ps[:64, :512], scalar1=fq[:, 0:1], scalar2=fbq[:, j:j + 1], op0=ALU.mult, op1=ALU.add), reads=[ps, fq, fbq], writes=[a])
        for _ in range(2):
            P.op('dve', lambda e: e.tensor_scalar(out=wr1[:], in0=a[:], scalar1=PI, scalar2=-2.0 * PI, op0=ALU.is_gt, op1=ALU.mult), reads=[a], writes=[wr1])
            P.op('dve', lambda e: e.tensor_scalar(out=wr2[:], in0=a[:], scalar1=-PI, scalar2=2.0 * PI, op0=ALU.is_lt, op1=ALU.mult), reads=[a], writes=[wr2])
            P.op('dve', lambda e: e.tensor_tensor(out=a[:], in0=a[:], in1=wr1[:], op=ALU.add), reads=[a, wr1], writes=[a])
            P.op('dve', lambda e: e.tensor_tensor(out=a[:], in0=a[:], in1=wr2[:], op=ALU.add), reads=[a, wr2], writes=[a])
        P.op('act', lambda e: e.activation(out=dst_ap, in_=a[:], func=AF.Sin), reads=[a], writes=[dst_tr])

    nq = 0
    nst = 0
    for half in range(2):
        for ti in range(SEQ // 512):
            et = embt[ti % 2]
            a = arg[ti % 2]
            hc_ = hcur[ti % 2]
            P.dma(et[:], embs[half][:, ti * 512:(ti + 1) * 512], reads=[embs[half]], writes=[et])
            ps = P.ps()
            P.op('pe', lambda e: e.matmul(ps[:64, :512], fw1[:], et[:], start=True, stop=True), reads=[fw1, et], writes=[ps])
            sin_layer(ps, 0, hc_[:], hc_, a)
            ps = P.ps()
            P.op('pe', lambda e: e.matmul(ps[:64, :512], fwm[:, 0, :], hc_[:], start=True, stop=True), reads=[fwm, hc_], writes=[ps])
            sin_layer(ps, 1, hc_[:], hc_, a)
            ps = P.ps()
            P.op('pe', lambda e: e.matmul(ps[:64, :512], fwm[:, 1, :], hc_[:], start=True, stop=True), reads=[fwm, hc_], writes=[ps])
            sin_layer(ps, 2, hid[:, ti * 512:(ti + 1) * 512], hid, a)
        for o in range(2):
            for dc in range(2):
                for ti in range(SEQ // 512):
                    tr_ = trt[nq % 2]
                    wi_ = wint[nq % 2]
                    nq += 1
                    if ti % 4 == 0:
                        kst_ = kstg[nst % 2]
                        nst += 1
                    ksl = kst_[:, (ti % 4) * 512:(ti % 4 + 1) * 512]
                    P.dma(tr_[:], trows[half][:, ti * 512:(ti + 1) * 512], reads=[trows[half]], writes=[tr_])
                    ps = P.ps()
                    P.op('pe', lambda e: e.matmul(ps[:, :512], fwo[:, o, half, dc * 128:(dc + 1) * 128], hid[:, ti * 512:(ti + 1) * 512], start=True, stop=True), reads=[fwo, hid], writes=[ps])
                    P.op('act', lambda e: e.activation(out=wi_[:], in_=tr_[:], func=AF.Exp, scale=dpp[:, dc:dc + 1]), reads=[tr_, dpp], writes=[wi_])
                    P.op('dve', lambda e: e.tensor_tensor(out=ksl, in0=ps[:, :512], in1=wi_[:], op=ALU.mult), reads=[ps, wi_], writes=[kst_])
                    if half == 1 and ti == 0:
                        P.op('dve', lambda e: e.memset(kst_[:, 0:1], 0.0), writes=[kst_])
                    col = half * 32 + ti
                    P.op('dve', lambda e: e.tensor_reduce(out=acc[o, dc][:, col:col + 1], in_=ksl, axis=AX.X, op=ALU.add, apply_absolute_value=True), reads=[kst_], writes=[acc[o, dc]])
                    if ti % 4 == 3:
                        P.dma(kd[o, half, dc * 128:(dc + 1) * 128, (ti - 3) * 512:(ti + 1) * 512], kst_[:], reads=[kst_], writes=[kd])
    nrm = P.sb("nrm", [128, 1])
    dg = P.sb("dg", [128, 128])
    for o in range(2):
        for dc in range(2):
            P.op('dve', lambda e: e.tensor_reduce(out=nrm[:], in_=acc[o, dc][:], axis=AX.X, op=ALU.add), reads=[acc[o, dc]], writes=[nrm])
            P.op('dve', lambda e: e.reciprocal(out=nrm[:], in_=nrm[:]), reads=[nrm], writes=[nrm])
            P.op('dve', lambda e: e.tensor_scalar(out=dg[:], in0=ident[:], scalar1=nrm[:, 0:1], scalar2=None, op0=ALU.mult), reads=[ident, nrm], writes=[dg])
            ps = P.ps()
            P.op('pe', lambda e: e.matmul(ps[:, :128], c['ones_f'][:], dg[:], start=True, stop=True), reads=[c['ones_f'], dg], writes=[ps])
            P.op('dve', lambda e: e.tensor_copy(out=rn[:, o, dc * 128:(dc + 1) * 128], in_=ps[:, :128]), reads=[ps], writes=[rn])
    P.pop()
    _sc.__exit__(None, None, None)

    P.push()
    _sc = P.nc.named_scope("phaseC")
    _sc.__enter__()
    ct = {}
    Kst = P.sb("Kst", [128, CBK, 128])
    cst = P.sb("cst", [128, 1024])
    for nm, (dr, shp) in cn.items():
        rnd = nm in ("W1", "W2r", "W2i", "WIa", "WIb")
        if len(shp) == 3:
            t = P.sb(nm + "_sb", [shp[1], shp[0], shp[2]])
            sv = cst[:, 0:shp[0] * shp[2]].rearrange("p (a f) -> p a f", a=shp[0])
            P.dma(sv if rnd else t[:], dr[:].rearrange("a p f -> p a f"), reads=[dr], writes=[cst if rnd else t])
            if rnd:
                P.op('act', lambda e: e.copy(out=r32(t[:]), in_=sv), reads=[cst], writes=[t])
        else:
            t = P.sb(nm + "_sb", shp)
            sv = cst[:, 0:shp[1]]
            P.dma(sv if rnd else t[:], dr[:], reads=[dr], writes=[cst if rnd else t])
            if rnd:
                P.op('act', lambda e: e.copy(out=r32(t[:]), in_=sv), reads=[cst], writes=[t])
        ct[nm] = t
    Yt = [P.sb("Yt%d" % i, [128, CBK, 128]) for i in range(2)]
    Xg = [P.sb("Xg%d" % i, [128, CBK, 128]) for i in range(2)]
    Yr = P.sb("Yr", [128, CBK, 128])
    Kp = P.sb("Kp", [128, CBK, 128])
    Kf = P.sb("Kf", [128, CBK, 128])
    W = {nm: P.sb("w_" + nm, [128, 512]) for nm in ["P1", "P2", "B", "B2", "XkRR", "XkII", "Zr", "Zi"]}
    Dr = [P.sb("Dr%d" % i, [128, 128]) for i in range(2)]
    Di = [P.sb("Di%d" % i, [128, 128]) for i in range(2)]
    t1 = P.sb("t1", [128, 128])

    def cv_view(s_, db):
        return cv[s_, db * CBK:(db + 1) * CBK, :].rearrange("d (p j) -> p d j", j=128)

    def twiddle(psA, rr, ii, width, outs):
        w = width
        P.op('dve', lambda e: e.tensor_tensor(out=W["P1"][:, :2 * w], in0=psA[:, :2 * w], in1=rr, op=ALU.mult), reads=[psA] + outs['tabs'], writes=[W["P1"]])
        P.op('dve', lambda e: e.tensor_tensor(out=W["P2"][:, :2 * w], in0=psA[:, :2 * w], in1=ii, op=ALU.mult), reads=[psA] + outs['tabs'], writes=[W["P2"]])

    def fwd_fft(lhs_list, rhs_list, reads):
        psA = P.ps()
        nl = len(lhs_list)
        for i in range(nl):
            P.op('pe', lambda e: e.matmul(psA[:, :512], r32(lhs_list[i]), r32(rhs_list[i]), start=(i == 0), stop=(i == nl - 1)), reads=reads + [ct["W1"]], writes=[psA])
        twiddle(psA, ct["TWrr"][:], ct["TWii"][:], 256, {'tabs': [ct["TWrr"], ct["TWii"]]})
        P.op('pool', lambda e: e.tensor_tensor(out=r32(W["B"][:, 0:256]), in0=W["P1"][:, 0:256], in1=W["P2"][:, 256:512], op=ALU.subtract), reads=[W["P1"], W["P2"]], writes=[W["B"]])
        P.op('pool', lambda e: e.tensor_tensor(out=r32(W["B"][:, 256:512]), in0=W["P2"][:, 0:256], in1=W["P1"][:, 256:512], op=ALU.add), reads=[W["P1"], W["P2"]], writes=[W["B"]])
        P.op('act', lambda e: e.copy(out=r32(W["B2"][:, 256:512]), in_=W["B"][:, 0:256]), reads=[W["B"]], writes=[W["B2"]])
        P.op('act', lambda e: e.activation(out=r32(W["B2"][:, 0:256]), in_=W["B"][:, 256:512], func=AF.Copy, scale=-1.0), reads=[W["B"]], writes=[W["B2"]])
        psX = P.ps()
        P.op('pe', lambda e: e.matmul(psX[:, :512], r32(ct["W2r"][:]), r32(W["B"][:]), start=True, stop=False), reads=[ct["W2r"], W["B"]], writes=[psX])
        P.op('pe', lambda e: e.matmul(psX[:, :512], r32(ct["W2i"][:]), r32(W["B2"][:]), start=False, stop=True), reads=[ct["W2i"], W["B2"]], writes=[psX])
        return psX

    ny = 0
    for db in range(CPC // CBK):
        ycur = Yt[ny % 2]
        ny += 1
        P.dma(ycur[:], cv_view(2, db), reads=[cv], writes=[ycur])
        for o in range(2):
            xg = Xg[o]
            P.dma(xg[:], cv_view(o, db), reads=[cv], writes=[xg])
            P.dma(Kst[:], kd[o, 0, db * CBK:(db + 1) * CBK, :].rearrange("d (p j) -> p d j", j=128), reads=[kd], writes=[Kst])
            P.op('act', lambda e: e.copy(out=r32(Kp[:]), in_=Kst[:]), reads=[Kst], writes=[Kp])
            P.dma(Kst[:], kd[o, 1, db * CBK:(db + 1) * CBK, :].rearrange("d (p j) -> p d j", j=128), reads=[kd], writes=[Kst])
            P.op('pool', lambda e: e.tensor_copy(out=r32(Kf[:]), in_=Kst[:]), reads=[Kst], writes=[Kf])
            P.op('pool', lambda e: e.tensor_copy(out=r32(Yr[:]), in_=ycur[:]), reads=[ycur], writes=[Yr])
            ynew = Yt[ny % 2]
            ny += 1
            for dl in range(CBK):
                dch = db * CBK + dl
                psXk = fwd_fft([Kp[:, dl, :], Kf[:, dl, :]], [ct["W1"][:, 0, :], ct["W1"][:, 1, :]], [Kp, Kf])
                P.op('act', lambda e: e.copy(out=W["XkRR"][:, 0:256], in_=psXk[:, 0:256]), reads=[psXk], writes=[W["XkRR"]])
                P.op('act', lambda e: e.copy(out=W["XkRR"][:, 256:512], in_=psXk[:, 0:256]), reads=[psXk], writes=[W["XkRR"]])
                P.op('act', lambda e: e.copy(out=W["XkII"][:, 0:256], in_=psXk[:, 256:512]), reads=[psXk], writes=[W["XkII"]])
                P.op('act', lambda e: e.copy(out=W["XkII"][:, 256:512], in_=psXk[:, 256:512]), reads=[psXk], writes=[W["XkII"]])
                psXu = fwd_fft([Yr[:, dl, :]], [ct["W1"][:, 0, :]], [Yr])
                P.op('dve', lambda e: e.tensor_tensor(out=W["P1"][:], in0=psXu[:, :512], in1=W["XkRR"][:], op=ALU.mult), reads=[psXu, W["XkRR"]], writes=[W["P1"]])
                P.op('dve', lambda e: e.tensor_tensor(out=W["P2"][:], in0=psXu[:, :512], in1=W["XkII"][:], op=ALU.mult), reads=[psXu, W["XkII"]], writes=[W["P2"]])
                P.op('pool', lambda e: e.tensor_tensor(out=r32(W["Zr"][:, 0:256]), in0=W["P1"][:, 0:256], in1=W["P2"][:, 256:512], op=ALU.subtract), reads=[W["P1"], W["P2"]], writes=[W["Zr"]])
                P.op('pool', lambda e: e.tensor_tensor(out=r32(W["Zi"][:, 0:256]), in0=W["P2"][:, 0:256], in1=W["P1"][:, 256:512], op=ALU.add), reads=[W["P1"], W["P2"]], writes=[W["Zi"]])
                for hf in range(2):
                    psC = P.ps()
                    P.op('pe', lambda e: e.matmul(psC[:, :256], r32(W["Zr"][:, 128 * hf:128 * hf + 128]), r32(ct["WIa"][:]), start=True, stop=False), reads=[W["Zr"], ct["WIa"]], writes=[psC])
                    P.op('pe', lambda e: e.matmul(psC[:, :256], r32(W["Zi"][:, 128 * hf:128 * hf + 128]), r32(ct["WIb"][:]), start=False, stop=True), reads=[W["Zi"], ct["WIb"]], writes=[psC])
                    P.op('dve', lambda e: e.tensor_tensor(out=W["P1"][:, :256], in0=psC[:, :256], in1=ct["TWIrr"][:, hf, :], op=ALU.mult), reads=[psC, ct["TWIrr"]], writes=[W["P1"]])
                    P.op('dve', lambda e: e.tensor_tensor(out=W["P2"][:, :256], in0=psC[:, :256], in1=ct["TWIii"][:, hf, :], op=ALU.mult), reads=[psC, ct["TWIii"]], writes=[W["P2"]])
                    P.op('pool', lambda e: e.tensor_tensor(out=Dr[hf][:], in0=W["P1"][:, 0:128], in1=W["P2"][:, 128:256], op=ALU.subtract), reads=[W["P1"], W["P2"]], writes=[Dr[hf]])
                    P.op('pool', lambda e: e.tensor_tensor(out=Di[hf][:], in0=W["P2"][:, 0:128], in1=W["P1"][:, 128:256], op=ALU.add), reads=[W["P1"], W["P2"]], writes=[Di[hf]])
                psY = P.ps()
                for hf in range(2):
                    P.op('pe', lambda e: e.matmul(psY[:, :128], ct["Cos"][:, hf, :], Dr[hf][:], start=(hf == 0), stop=False), reads=[ct["Cos"], Dr[hf]], writes=[psY])
                    P.op('pe', lambda e: e.matmul(psY[:, :128], ct["NSin"][:, hf, :], Di[hf][:], start=False, stop=(hf == 1)), reads=[ct["NSin"], Di[hf]], writes=[psY])
                P.op('dve', lambda e: e.tensor_scalar(out=t1[:], in0=psY[:, :128], scalar1=rn[:, o, dch:dch + 1], scalar2=None, op0=ALU.mult), reads=[psY, rn], writes=[t1])
                P.op('dve', lambda e: e.scalar_tensor_tensor(out=t1[:], in0=ycur[:, dl, :], scalar=fbias[:, o, dch:dch + 1], in1=t1[:], op0=ALU.mult, op1=ALU.add),
                     reads=[ycur, fbias, t1], writes=[t1])
                P.op('pool', lambda e: e.tensor_tensor(out=ynew[:, dl, :], in0=t1[:], in1=xg[:, dl, :], op=ALU.mult), reads=[t1, xg], writes=[ynew])
            ycur = ynew
        P.dma(y2[db * CBK:(db + 1) * CBK, :].rearrange("d (p j) -> p d j", j=128), ycur[:], reads=[ycur], writes=[y2], is_out=True)
    P.pop()
    _sc.__exit__(None, None, None)
    return P


def hyena_consts():
    L = SEQ
    N = NFFT
    f64 = np.float64
    n1 = np.arange(256, dtype=f64)[:, None]
    k1 = np.arange(256, dtype=f64)[None, :]
    th = 2 * np.pi * n1 * k1 / 256
    W1 = np.concatenate([np.cos(th), -np.sin(th)], axis=1).reshape(2, 128, 512)
    n2 = np.arange(128, dtype=f64)[:, None]
    tw = 2 * np.pi * n2 * k1 / N
    TWr, TWi = np.cos(tw), -np.sin(tw)
    k2 = np.arange(128, dtype=f64)[None, :]
    t2 = 2 * np.pi * n2 * k2 / 128
    W2r, W2i = np.cos(t2), -np.sin(t2)
    Wr_, Wi_ = np.cos(t2), np.sin(t2)
    WIa = np.concatenate([Wr_, Wi_], axis=1)
    WIb = np.concatenate([-Wi_, Wr_], axis=1)
    k1c = np.arange(256, dtype=f64)[:, None]
    n2r = np.arange(128, dtype=f64)[None, :]
    ti = 2 * np.pi * k1c * n2r / N
    Tr, Ti = np.cos(ti), np.sin(ti)
    TWIrr = np.concatenate([Tr, Tr], axis=1).reshape(2, 128, 256)
    TWIii = np.concatenate([Ti, Ti], axis=1).reshape(2, 128, 256)
    n1r = np.arange(128, dtype=f64)[None, :]
    t3 = 2 * np.pi * k1c * n1r / 256
    Cos = (np.cos(t3) / N).reshape(2, 128, 128)
    NSin = (-np.sin(t3) / N).reshape(2, 128, 128)
    d = {"W1": W1, "TWrr": np.concatenate([TWr, TWr], axis=1), "TWii": np.concatenate([TWi, TWi], axis=1), "W2r": W2r, "W2i": W2i,
         "WIa": WIa, "WIb": WIb, "TWIrr": TWIrr, "TWIii": TWIii, "Cos": Cos, "NSin": NSin}
    d = {k: np.ascontiguousarray(v.astype(np.float32)) for k, v in d.items()}
    f32 = np.float32
    t = np.linspace(0.0, 1.0, L, dtype=f32)
    w = (f32(2.0 * np.pi) * np.arange(L, dtype=f32) / f32(L)).astype(f32)
    f = np.linspace(1e-4, 15.0, 16, dtype=f32)
    fwm_ = (f[None, :] * w[:, None]).astype(f32)
    emb = np.concatenate([t[:, None], np.cos(fwm_), -np.sin(fwm_)], axis=1).astype(f32)
    ridx = np.concatenate([[0], L - np.arange(1, L)])
    d["embT"] = np.ascontiguousarray(emb.T)
    d["embTr"] = np.ascontiguousarray(emb[ridx].T)
    d["trow"] = np.ascontiguousarray(np.tile((-t)[None, :], (128, 1)))
    d["trowr"] = np.ascontiguousarray(np.tile((-t[ridx])[None, :], (128, 1)))
    d["ident"] = np.eye(128, dtype=np.float32)
    lo = np.log(1.5) / 1e-2
    hi = np.log(0.3) / 1e-2
    d["_deltas"] = np.abs(np.linspace(lo, hi, D, dtype=f32)).astype(f32)
    return d


def run_l5(inp, o4):
    P = build_l5()
    hc = hyena_consts()
    deltas = hc.pop("_deltas")
    projT = o4['projT']
    sw = inp['hy_short_w'][0]
    sb = inp['hy_short_b'][0]
    common = dict(hc)
    common.update({
        "fw1": np.ascontiguousarray(inp['hy_fw1'][0]), "fwm": np.ascontiguousarray(inp['hy_fw_mid'][0]),
        "fq": np.ascontiguousarray(inp['hy_freq'][0].reshape(64, 1)),
        "fb": np.ascontiguousarray(np.stack([inp['hy_fb1'][0], inp['hy_fb_mid'][0][0], inp['hy_fb_mid'][0][1]], axis=1)),
    })
    in_maps = []
    for i in range(NCORE):
        d0 = i * CPC
        pj = np.zeros((3, CPC, SEQ + 2), np.float32)
        swl = np.zeros((128, 6, 3), np.float32)
        sbl = np.zeros((128, 6), np.float32)
        for s_ in range(3):
            pj[s_, :, 1:SEQ + 1] = projT[s_ * D + d0:s_ * D + d0 + CPC, :]
            for gq in range(2):
                idx = s_ * D + d0 + gq * 128 + np.arange(128)
                swl[:, s_ * 2 + gq, :] = sw[:, idx].T
                sbl[:, s_ * 2 + gq] = sb[idx]
        m = dict(common)
        m.update({"pj": pj, "sw": swl, "sb": sbl,
                  "fwo": np.ascontiguousarray(inp['hy_fw_out'][0][:, :, :, d0:d0 + CPC]),
                  "dpp": np.ascontiguousarray(deltas[d0:d0 + CPC].reshape(2, 128).T),
                  "fbias": np.ascontiguousarray(np.tile(inp['hy_fbias'][0][None, :, d0:d0 + CPC], (128, 1, 1)))})
        in_maps.append(m)
    res = run_spmd(P, in_maps)
    return {"y2T": np.concatenate([r['y2'] for r in res], axis=0)}


def run_l6(inp, mod, o4, o5):
    P = build_l6()
    m1 = split_mod(mod[1, 0])
    common = {
        "w1": np.ascontiguousarray(inp['ffn_w1'][1]), "w3": np.ascontiguousarray(inp['ffn_w3'][1]), "w2": np.ascontiguousarray(inp['ffn_w2'][1]),
        "wo": np.ascontiguousarray(inp['hy_w_out'][0]), "nw": vec_layout(inp['norm_ffn_w'][1]), "sc2": vec_layout(m1[4]), "sh2": vec_layout(m1[3]),
        "g2": vec_layout(m1[5]), "bo": vec_layout(inp['hy_b_out'][0]), "g1": vec_layout(m1[2]), "fw": vec_layout(inp['final_norm_w']),
    }
    in_maps = []
    for i in range(NCORE):
        m = dict(common)
        m.update({"xT": tok_shard(o4['x2T'], i), "yT": tok_shard(o5['y2T'], i)})
        in_maps.append(m)
    res = run_spmd(P, in_maps)
    outT = np.concatenate([r['outT'] for r in res], axis=1)
    return np.ascontiguousarray(outT.T)[None].astype(np.float32)


def kernel(**inputs):
    inp = {k: np.asarray(v) for k, v in inputs.items()}
    mod = run_l0(inp['c'], inp['c_ctx'], inp['mod_w'], inp['mod_b'])
    o1 = run_l1(inp, mod)
    o2 = run_l2(inp, o1)
    o4 = run_l4(inp, mod, o1, o2)
    o5 = run_l5(inp, o4)
    return run_l6(inp, mod, o4, o5)
```

```python
from contextlib import ExitStack
import numpy as np
import concourse.bass as bass
import concourse.mybir as mybir
from concourse.bass_utils import run_bass_kernel_spmd

F32 = mybir.dt.float32
BF16 = mybir.dt.bfloat16
AF = mybir.ActivationFunctionType
ALU = mybir.AluOpType
AX = mybir.AxisListType

D = 2048
SEQ = 16384
CTX = 256
DFF = 5632
NCORE = 8
TPC = SEQ // NCORE
TT = 512
EPS = 1e-6


class TR:
    def __init__(self, h):
        self.h = h
        self.w = None
        self.r = {}

    def __getitem__(self, k):
        return self.h[k]


class Prog:
    def __init__(self, ndma=32):
        self.nc = bass.Bass("TRN2", target_bir_lowering=False)
        nc = self.nc
        self.es = ExitStack()
        self.es.enter_context(nc.allow_low_precision("bf16 matmul operands, fp32 accumulate"))
        self.engs = {'pe': nc.tensor, 'act': nc.scalar, 'dve': nc.vector, 'pool': nc.gpsimd, 'sp': nc.sync}
        self.sem = {}
        self.cnt = {}
        for e in ['pe', 'act', 'dve', 'pool']:
            self.sem[e] = self.es.enter_context(nc.semaphore("s_" + e))
            self.cnt[e] = 0
        self.ndma = ndma
        for i in range(ndma):
            self.sem[('d', i)] = self.es.enter_context(nc.semaphore("d%d" % i))
            self.cnt[('d', i)] = 0
        self.dnext = 0
        self.waited = {e: {} for e in self.engs}
        self.psum = [TR(self.es.enter_context(nc.psum_tensor("ps%d" % i, [128, 512], F32))) for i in range(8)]
        self.psn = 0
        self.out_events = []
        self.n_ins = 0

    def sb(self, name, shape, dtype=F32):
        return TR(self.es.enter_context(self.nc.sbuf_tensor(name, list(shape), dtype)))

    def din(self, name, shape, dtype=F32):
        return TR(self.nc.dram_tensor(name, list(shape), dtype, kind="ExternalInput").ap())

    def dout(self, name, shape, dtype=F32):
        return TR(self.nc.dram_tensor(name, list(shape), dtype, kind="ExternalOutput").ap())

    def dscratch(self, name, shape, dtype=F32):
        return TR(self.nc.dram_tensor(name, list(shape), dtype, kind="Internal").ap())

    def ps(self):
        p = self.psum[self.psn % 8]
        self.psn += 1
        return p

    def _wait(self, e, ev):
        key, val = ev
        if self.waited[e].get(key, 0) >= val:
            return
        self.engs[e].wait_ge(self.sem[key], val)
        self.waited[e][key] = val

    def _deps(self, e, reads, writes):
        deps = []
        for t in reads:
            if t.w is not None:
                deps.append(t.w)
        for t in writes:
            if t.w is not None:
                deps.append(t.w)
            deps.extend(t.r.items())
        for ev in deps:
            if e == 'pe' and ev[0] == 'pe':
                continue
            self._wait(e, ev)

    def _record(self, ev, reads, writes):
        for t in writes:
            t.w = ev
            t.r = {}
        for t in reads:
            if t.r.get(ev[0], 0) < ev[1]:
                t.r[ev[0]] = ev[1]

    def op(self, e, fn, reads=(), writes=()):
        self._deps(e, reads, writes)
        ins = fn(self.engs[e])
        self.cnt[e] += 1
        ins.then_inc(self.sem[e], 1)
        ev = (e, self.cnt[e])
        self._record(ev, reads, writes)
        self.n_ins += 1
        return ev

    def dma(self, out_ap, in_ap, reads=(), writes=(), q='sp', is_out=False):
        self._deps(q, reads, writes)
        i = self.dnext
        self.dnext = (self.dnext + 1) % self.ndma
        key = ('d', i)
        if self.cnt[key] > 0:
            self._wait(q, (key, self.cnt[key] * 16))
        self.cnt[key] += 1
        self.engs[q].dma_start(out=out_ap, in_=in_ap).then_inc(self.sem[key], 16)
        ev = (key, self.cnt[key] * 16)
        self._record(ev, reads, writes)
        if is_out:
            self.out_events.append(ev)
        self.n_ins += 1
        return ev

    def barrier(self):
        for e in self.engs:
            for key, cnt in self.cnt.items():
                if cnt > 0:
                    val = cnt * 16 if isinstance(key, tuple) else cnt
                    self._wait(e, (key, val))

    def push(self):
        self._saved_es = getattr(self, '_saved_es', [])
        self._saved_es.append(self.es)
        self.es = ExitStack()

    def pop(self):
        self.barrier()
        self.es.close()
        self.es = self._saved_es.pop()

    def finish(self):
        for ev in self.out_events:
            self._wait('sp', ev)
        for e in ['pe', 'act', 'dve', 'pool']:
            if self.cnt[e] > 0:
                self._wait('sp', (e, self.cnt[e]))
        self.es.close()
        return self.nc


_TIMES = []


def run_spmd(prog, in_maps):
    nc = prog.finish()
    if _TRACE[0]:
        res = run_bass_kernel_spmd(nc, in_maps, core_ids=list(range(NCORE)), trace=True)
        _TIMES.append(getattr(res, "exec_time_ns", None))
        print("LAUNCH exec_time_ns", _TIMES[-1], "n_ins", prog.n_ins, flush=True)
        st = getattr(res, "per_core_scope_times", None)
        if st:
            print("SCOPES", {k: v.get(0) for k, v in st.items()}, flush=True)
    else:
        res = run_bass_kernel_spmd(nc, in_maps, core_ids=list(range(NCORE)))
    return res.results


_TRACE = [False]


class WStream:
    def __init__(self, P, maxk=4096, tag="w", nstg=2, nwb=4):
        self.P = P
        self.stg = [P.sb("%s_stg%d" % (tag, i), [128, maxk], F32) for i in range(nstg)]
        self.wb = [P.sb("%s_wb%d" % (tag, i), [128, maxk], BF16) for i in range(nwb)]
        self.n = 0
        self.maxel = maxk

    def load(self, W, col0, ncol, K, row0=0):
        P = self.P
        KC = K // 128
        assert KC * ncol <= self.maxel, (KC, ncol, self.maxel)
        stg = self.stg[self.n % len(self.stg)]
        wb = self.wb[self.n % len(self.wb)]
        self.n += 1
        src = W[row0:row0 + K, col0:col0 + ncol].rearrange("(kc p) f -> p kc f", p=128)
        dst = stg[:, 0:KC * ncol].rearrange("p (kc f) -> p kc f", kc=KC)
        P.dma(dst, src, reads=[W], writes=[stg], q='sp')
        if self.n % 2 == 0:
            P.op('pool', lambda e: e.tensor_copy(out=wb[:, 0:KC * ncol], in_=stg[:, 0:KC * ncol]), reads=[stg], writes=[wb])
        else:
            P.op('act', lambda e: e.copy(out=wb[:, 0:KC * ncol], in_=stg[:, 0:KC * ncol]), reads=[stg], writes=[wb])

        def view(kc, c0, c1):
            return wb[:, kc * ncol + c0: kc * ncol + c1]
        return wb, view


def emit_consts(P):
    c = {}
    c['ones_f'] = P.sb("ones_f", [128, 128], F32)
    P.op('dve', lambda e: e.memset(c['ones_f'][:], 1.0), writes=[c['ones_f']])
    return c


def emit_rmsnorm_mod(P, c, x, h, scale_t, shift_t, ntok, sq, rstd, nch=16, dim=None, ki=0, ko=0, ks=0):
    dim = dim or nch * 128
    ps = P.ps()
    for kc in range(nch):
        P.op('act', lambda e: e.activation(out=sq[:, :ntok], in_=x[:, ki + kc, :ntok], func=AF.Square), reads=[x], writes=[sq])
        P.op('pe', lambda e: e.matmul(ps[:, :ntok], c['ones_f'][:], sq[:, :ntok], start=(kc == 0), stop=(kc == nch - 1)),
             reads=[sq, c['ones_f']], writes=[ps])
    P.op('dve', lambda e: e.tensor_scalar(out=rstd[:, :ntok], in0=ps[:, :ntok], scalar1=1.0 / dim, scalar2=EPS, op0=ALU.mult, op1=ALU.add),
         reads=[ps], writes=[rstd])
    P.op('act', lambda e: e.activation(out=rstd[:, :ntok], in_=rstd[:, :ntok], func=AF.Sqrt), reads=[rstd], writes=[rstd])
    P.op('dve', lambda e: e.reciprocal(out=rstd[:, :ntok], in_=rstd[:, :ntok]), reads=[rstd], writes=[rstd])
    for kc in range(nch):
        if shift_t is not None:
            P.op('dve', lambda e: e.scalar_tensor_tensor(out=sq[:, :ntok], in0=x[:, ki + kc, :ntok], scalar=scale_t[:, ks + kc:ks + kc + 1], in1=rstd[:, :ntok],
                                                          op0=ALU.mult, op1=ALU.mult), reads=[x, scale_t, rstd], writes=[sq])
            P.op('act', lambda e: e.activation(out=h[:, ko + kc, :ntok], in_=sq[:, :ntok], func=AF.Identity, bias=shift_t[:, ks + kc:ks + kc + 1], scale=1.0),
                 reads=[sq, shift_t], writes=[h])
        else:
            P.op('dve', lambda e: e.scalar_tensor_tensor(out=h[:, ko + kc, :ntok], in0=x[:, ki + kc, :ntok], scalar=scale_t[:, ks + kc:ks + kc + 1], in1=rstd[:, :ntok],
                                                          op0=ALU.mult, op1=ALU.mult), reads=[x, scale_t, rstd], writes=[h])


def emit_linear(P, ws, W, K, f0, f1, h, ntok, evac, cb=256):
    KC = K // 128
    f = f0
    while f < f1:
        n = min(cb, f1 - f)
        wb, view = ws.load(W, f, n, K)
        for s in range(0, n, 128):
            m = min(128, n - s)
            ps = P.ps()
            for kc in range(KC):
                P.op('pe', lambda e: e.matmul(ps[:m, :ntok], view(kc, s, s + m), h[:, kc, :ntok], start=(kc == 0), stop=(kc == KC - 1)),
                     reads=[wb, h], writes=[ps])
            evac((f + s) // 128, ps, m)
        f += n


def emit_ffn(P, c, ws, x, h, u, tmp, w1, w3, w2, g2, ntok):
    KC = D // 128
    FC = DFF // 128
    f = 0
    cb = 256
    while f < DFF:
        n = min(cb, DFF - f)
        wb1, v1 = ws.load(w1, f, n, D)
        wb3, v3 = ws.load(w3, f, n, D)
        for s in range(0, n, 128):
            p1 = P.ps()
            p3 = P.ps()
            for kc in range(KC):
                P.op('pe', lambda e: e.matmul(p1[:, :ntok], v1(kc, s, s + 128), h[:, kc, :ntok], start=(kc == 0), stop=(kc == KC - 1)),
                     reads=[wb1, h], writes=[p1])
            for kc in range(KC):
                P.op('pe', lambda e: e.matmul(p3[:, :ntok], v3(kc, s, s + 128), h[:, kc, :ntok], start=(kc == 0), stop=(kc == KC - 1)),
                     reads=[wb3, h], writes=[p3])
            fc = (f + s) // 128
            P.op('act', lambda e: e.activation(out=tmp[:, :ntok], in_=p1[:, :ntok], func=AF.Silu), reads=[p1], writes=[tmp])
            P.op('dve', lambda e: e.tensor_tensor(out=u[:, fc, :ntok], in0=tmp[:, :ntok], in1=p3[:, :ntok], op=ALU.mult), reads=[tmp, p3], writes=[u])
        f += n
    HK = FC // 2
    for dc in range(KC):
        wbA, vA = ws.load(w2, dc * 128, 128, HK * 128, row0=0)
        wbB, vB = ws.load(w2, dc * 128, 128, HK * 128, row0=HK * 128)
        ps = P.ps()
        for fc in range(FC):
            wb_, v_, kk = (wbA, vA, fc) if fc < HK else (wbB, vB, fc - HK)
            P.op('pe', lambda e: e.matmul(ps[:, :ntok], v_(kk, 0, 128), u[:, fc, :ntok], start=(fc == 0), stop=(fc == FC - 1)),
                 reads=[wb_, u], writes=[ps])
        P.op('dve', lambda e: e.scalar_tensor_tensor(out=x[:, dc, :ntok], in0=ps[:, :ntok], scalar=g2[:, dc:dc + 1], in1=x[:, dc, :ntok],
                                                      op0=ALU.mult, op1=ALU.add), reads=[ps, g2, x], writes=[x])


def load_vec(P, name, n=16):
    d = P.din(name, [128, n])
    t = P.sb(name + "_sb", [128, n])
    P.dma(t[:], d[:], reads=[d], writes=[t])
    return t


def build_l0():
    P = Prog()
    NCOL = 12288 // NCORE
    cc = P.din("cc", [128, 16, 2])
    mw = P.din("mw", [2, D, NCOL])
    mb = P.din("mb", [2, 2, NCOL])
    out = P.dout("out", [2, 2, NCOL])
    cs = P.sb("cs", [128, 16, 2])
    ca = P.sb("ca", [128, 16, 2])
    P.dma(cs[:], cc[:], reads=[cc], writes=[cs])
    P.op('act', lambda e: e.activation(out=ca[:], in_=cs[:], func=AF.Silu), reads=[cs], writes=[ca])
    bsb = P.sb("bsb", [2, 2, NCOL])
    P.dma(bsb[:], mb[:].rearrange("l r n -> r l n"), reads=[mb], writes=[bsb])
    wt = [P.sb("wt%d" % i, [128, 16, 512]) for i in range(2)]
    res = P.sb("res", [2, 2, NCOL])
    n = 0
    for l in range(2):
        for cb in range(NCOL // 512):
            w = wt[n % 2]
            n += 1
            P.dma(w[:], mw[l, :, cb * 512:(cb + 1) * 512].rearrange("(kc p) f -> p kc f", p=128), reads=[mw], writes=[w])
            ps = P.ps()
            for kc in range(16):
                P.op('pe', lambda e: e.matmul(ps[:2, :], ca[:, kc, :], w[:, kc, :], start=(kc == 0), stop=(kc == 15)), reads=[ca, w], writes=[ps])
            P.op('dve', lambda e: e.tensor_tensor(out=res[:, l, cb * 512:(cb + 1) * 512], in0=ps[:2, :], in1=bsb[:, l, cb * 512:(cb + 1) * 512], op=ALU.add),
                 reads=[ps, bsb], writes=[res])
    P.dma(out[:].rearrange("l r n -> r l n"), res[:], reads=[res], writes=[out], is_out=True)
    return P


def vec_layout(v):
    v = np.asarray(v, np.float32)
    return np.ascontiguousarray(v.reshape(-1, 128).T)


def run_l0(c, c_ctx, mod_w, mod_b):
    P = build_l0()
    NCOL = 12288 // NCORE
    cc = np.stack([vec_layout(c.reshape(-1)), vec_layout(c_ctx.reshape(-1))], axis=-1)
    in_maps = []
    for i in range(NCORE):
        sl = slice(i * NCOL, (i + 1) * NCOL)
        in_maps.append({"cc": np.ascontiguousarray(cc),
                        "mw": np.ascontiguousarray(mod_w[:, :, sl]),
                        "mb": np.ascontiguousarray(np.repeat(mod_b[:, None, sl], 2, axis=1))})
    res = run_spmd(P, in_maps)
    out = np.concatenate([r["out"] for r in res], axis=-1)
    return out


def build_l6(do_proj=True, do_final=True):
    P = Prog()
    c = emit_consts(P)
    xT = P.din("xT", [D, TPC])
    out = P.dout("outT", [D, TPC])
    w1 = P.din("w1", [D, DFF])
    w3 = P.din("w3", [D, DFF])
    w2 = P.din("w2", [DFF, D])
    nw = load_vec(P, "nw")
    sc2 = load_vec(P, "sc2")
    sh2 = load_vec(P, "sh2")
    g2 = load_vec(P, "g2")
    if do_proj:
        yT = P.din("yT", [D, TPC])
        wo = P.din("wo", [D, D])
        bo = load_vec(P, "bo")
        g1 = load_vec(P, "g1")
        gb = P.sb("gb", [128, 16])
        P.op('dve', lambda e: e.tensor_tensor(out=gb[:], in0=g1[:], in1=bo[:], op=ALU.mult), reads=[g1, bo], writes=[gb])
    if do_final:
        fw = load_vec(P, "fw")
    scl = P.sb("scl", [128, 16])
    P.op('dve', lambda e: e.scalar_tensor_tensor(out=scl[:], in0=sc2[:], scalar=1.0, in1=nw[:], op0=ALU.add, op1=ALU.mult), reads=[sc2, nw], writes=[scl])
    ws = WStream(P)
    x = P.sb("x", [128, 16, TT])
    h = P.sb("h", [128, 16, TT], BF16)
    u = P.sb("u", [128, DFF // 128, TT], BF16)
    sq = P.sb("sq", [128, TT])
    rstd = P.sb("rstd", [128, TT])
    tmp = P.sb("tmp", [128, TT])
    xv = xT[:, :].rearrange("(kc p) t -> p kc t", p=128)
    ov = out[:, :].rearrange("(kc p) t -> p kc t", p=128)
    for tt in range(TPC // TT):
        t0 = tt * TT
        P.dma(x[:], xv[:, :, t0:t0 + TT], reads=[xT], writes=[x])
        if do_proj:
            yv = yT[:, :].rearrange("(kc p) t -> p kc t", p=128)
            for kc in range(16):
                P.dma(tmp[:], yv[:, kc, t0:t0 + TT], reads=[yT], writes=[tmp])
                P.op('dve', lambda e: e.tensor_copy(out=h[:, kc, :], in_=tmp[:]), reads=[tmp], writes=[h])

            def evac(dc, ps, m):
                P.op('dve', lambda e: e.scalar_tensor_tensor(out=x[:, dc, :], in0=ps[:, :TT], scalar=g1[:, dc:dc + 1], in1=x[:, dc, :],
                                                              op0=ALU.mult, op1=ALU.add), reads=[ps, g1, x], writes=[x])
                P.op('dve', lambda e: e.tensor_scalar(out=x[:, dc, :], in0=x[:, dc, :], scalar1=gb[:, dc:dc + 1], scalar2=None, op0=ALU.add),
                     reads=[x, gb], writes=[x])
            emit_linear(P, ws, wo, D, 0, D, h, TT, evac)
        emit_rmsnorm_mod(P, c, x, h, scl, sh2, TT, sq, rstd)
        emit_ffn(P, c, ws, x, h, u, tmp, w1, w3, w2, g2, TT)
        if do_final:
            ps = P.ps()
            for kc in range(16):
                P.op('act', lambda e: e.activation(out=sq[:], in_=x[:, kc, :], func=AF.Square), reads=[x], writes=[sq])
                P.op('pe', lambda e: e.matmul(ps[:, :TT], c['ones_f'][:], sq[:], start=(kc == 0), stop=(kc == 15)), reads=[sq, c['ones_f']], writes=[ps])
            P.op('dve', lambda e: e.tensor_scalar(out=rstd[:], in0=ps[:, :TT], scalar1=1.0 / D, scalar2=EPS, op0=ALU.mult, op1=ALU.add), reads=[ps], writes=[rstd])
            P.op('act', lambda e: e.activation(out=rstd[:], in_=rstd[:], func=AF.Sqrt), reads=[rstd], writes=[rstd])
            P.op('dve', lambda e: e.reciprocal(out=rstd[:], in_=rstd[:]), reads=[rstd], writes=[rstd])
            for kc in range(16):
                P.op('dve', lambda e: e.scalar_tensor_tensor(out=x[:, kc, :], in0=x[:, kc, :], scalar=fw[:, kc:kc + 1], in1=rstd[:],
                                                              op0=ALU.mult, op1=ALU.mult), reads=[x, fw, rstd], writes=[x])
        P.dma(ov[:, :, t0:t0 + TT], x[:], reads=[x], writes=[out], is_out=True)
    return P


NQ = 1024 + 512
XBC0, DT0, CKV0, KR0 = 1536, 3072, 3104, 3616
L1COLS = 8 + TPC + CTX
QSCALE = 192.0 ** -0.5


def build_l1():
    P = Prog()
    c = emit_consts(P)
    xh = P.din("xh", [D, L1COLS])
    hmask = P.din("hmask", [128, 8])
    cosd = P.din("cosT", [64, TPC + CTX])
    sind = P.din("sinT", [64, TPC + CTX])
    w_in = P.din("w_in", [D, 3680])
    w_in_sw = P.din("w_in_sw", [D, 64])
    w_uq = P.din("w_uq", [512, 1536])
    w_uq_sw = P.din("w_uq_sw", [512, 512])
    w_ukv = P.din("w_ukv", [512, 2048])
    nmw = load_vec(P, "nmw")
    mods = {k: load_vec(P, k) for k in ["sh1", "sc1", "csh1", "csc1"]}
    qnw = load_vec(P, "qnw", 4)
    kvnw = load_vec(P, "kvnw", 4)
    cw = P.din("convw", [128, 12, 3])
    cwt = P.sb("cwt", [128, 12, 3])
    P.dma(cwt[:], cw[:], reads=[cw], writes=[cwt])
    cb = load_vec(P, "convb", 12)
    dtb_d = P.din("dtb", [32, 1])
    dtb = P.sb("dtb_sb", [32, 1])
    P.dma(dtb[:], dtb_d[:], reads=[dtb_d], writes=[dtb])
    hm = P.sb("hm", [128, 8])
    P.dma(hm[:], hmask[:], reads=[hmask], writes=[hm])
    zT = P.dout("zT", [1024, TPC])
    qnT = P.dout("qnT", [8, 128, TPC])
    qrT = P.dout("qrT", [8, 64, TPC])
    xbcT = P.dout("xbcT", [1536, TPC + CTX])
    dtT = P.dout("dtT", [32, TPC + CTX])
    kvT = P.dout("kvT", [2048, TPC + CTX])
    krT = P.dout("krT", [64, TPC + CTX])
    scl = {}
    for k, s in [("lat", "sc1"), ("ctx", "csc1")]:
        scl[k] = P.sb("scl_" + k, [128, 16])
        P.op('dve', lambda e: e.scalar_tensor_tensor(out=scl[k][:], in0=mods[s][:], scalar=1.0, in1=nmw[:], op0=ALU.add, op1=ALU.mult),
             reads=[mods[s], nmw], writes=[scl[k]])
    shf = {"lat": mods["sh1"], "ctx": mods["csh1"]}
    ws = WStream(P)
    x = P.sb("x", [128, 16, TT])
    h = P.sb("h", [128, 16, TT], BF16)
    sq = P.sb("sq", [128, TT])
    rstd = P.sb("rstd", [128, TT])
    tmp = P.sb("tmp", [128, TT])
    tmp2 = P.sb("tmp2", [128, TT])
    xbc = P.sb("xbc", [128, 12, TT + 2])
    xbch = P.sb("xbch", [128, 12, 8])
    cq = P.sb("cq", [128, 4, TT])
    hq = P.sb("hq", [128, 4, TT], BF16)
    cost = P.sb("cost", [64, TT])
    sint = P.sb("sint", [64, TT])
    xv = xh[:, :].rearrange("(kc p) t -> p kc t", p=128)

    P.dma(x[:, :, 0:8], xv[:, :, 0:8], reads=[xh], writes=[x])
    emit_rmsnorm_mod(P, c, x, h, scl["lat"], shf["lat"], 8, sq, rstd)

    def ev_h(fc, ps, m):
        i = fc - XBC0 // 128
        P.op('dve', lambda e: e.tensor_tensor(out=xbch[:, i, :], in0=ps[:, 0:8], in1=hm[:], op=ALU.mult), reads=[ps, hm], writes=[xbch])
    emit_linear(P, ws, w_in, D, XBC0, DT0, h, 8, ev_h)

    def rope_out(psA, psB, n, dst_ap, dst_tr, scale):
        P.op('dve', lambda e: e.tensor_tensor(out=tmp[:64, :n], in0=psA[:64, :n], in1=cost[:, :n], op=ALU.mult), reads=[psA, cost], writes=[tmp])
        P.op('dve', lambda e: e.tensor_tensor(out=tmp2[:64, :n], in0=psB[:64, :n], in1=sint[:, :n], op=ALU.mult), reads=[psB, sint], writes=[tmp2])
        P.op('dve', lambda e: e.scalar_tensor_tensor(out=tmp[:64, :n], in0=tmp[:64, :n], scalar=scale, in1=tmp2[:64, :n], op0=ALU.mult, op1=ALU.add),
             reads=[tmp, tmp2], writes=[tmp])
        P.dma(dst_ap, tmp[:64, :n], reads=[tmp], writes=[dst_tr], is_out=True)

    segs = [("lat", 8 + TT * j, TT, TT * j, j) for j in range(TPC // TT)] + [("ctx", 8 + TPC, CTX, TPC, None)]
    for kind, c0, n, oc, hj in segs:
        P.dma(x[:, :, :n], xv[:, :, c0:c0 + n], reads=[xh], writes=[x])
        P.dma(cost[:, :n], cosd[:, oc:oc + n], reads=[cosd], writes=[cost])
        P.dma(sint[:, :n], sind[:, oc:oc + n], reads=[sind], writes=[sint])
        emit_rmsnorm_mod(P, c, x, h, scl[kind], shf[kind], n, sq, rstd)
        if kind == "lat":
            def ev_z(fc, ps, m):
                P.op('act', lambda e: e.copy(out=x[:, fc, :n], in_=ps[:, :n]), reads=[ps], writes=[x])
            emit_linear(P, ws, w_in, D, 0, 1024, h, n, ev_z)
            P.dma(zT[:, oc:oc + n].rearrange("(c p) t -> p c t", p=128), x[:, 0:8, :n], reads=[x], writes=[zT], is_out=True)
            def ev_cq(fc, ps, m):
                P.op('act', lambda e: e.copy(out=cq[:, fc - 8, :n], in_=ps[:, :n]), reads=[ps], writes=[cq])
            emit_linear(P, ws, w_in, D, 1024, 1536, h, n, ev_cq)
            emit_rmsnorm_mod(P, c, cq, hq, qnw, None, n, sq, rstd, nch=4)
            for hh in range(8):
                def ev_qn(fc, ps, m):
                    P.op('act', lambda e: e.activation(out=x[:, hh, :n], in_=ps[:, :n], func=AF.Copy, scale=QSCALE), reads=[ps], writes=[x])
                emit_linear(P, ws, w_uq, 512, 192 * hh, 192 * hh + 128, hq, n, ev_qn)
            P.dma(qnT[:, :, oc:oc + n].rearrange("h p t -> p h t"), x[:, 0:8, :n], reads=[x], writes=[qnT], is_out=True)
            for hh in range(8):
                got = {}

                def ev_a(fc, ps, m):
                    got['a'] = ps

                def ev_b(fc, ps, m):
                    got['b'] = ps
                emit_linear(P, ws, w_uq, 512, 192 * hh + 128, 192 * hh + 192, hq, n, ev_a)
                emit_linear(P, ws, w_uq_sw, 512, 64 * hh, 64 * hh + 64, hq, n, ev_b)
                P.op('dve', lambda e: e.tensor_scalar(out=tmp2[:64, :n], in0=got['b'][:64, :n], scalar1=QSCALE, scalar2=None, op0=ALU.mult), reads=[got['b']], writes=[tmp2])
                P.op('dve', lambda e: e.tensor_tensor(out=tmp2[:64, :n], in0=tmp2[:64, :n], in1=sint[:, :n], op=ALU.mult), reads=[tmp2, sint], writes=[tmp2])
                P.op('dve', lambda e: e.tensor_tensor(out=tmp[:64, :n], in0=got['a'][:64, :n], in1=cost[:, :n], op=ALU.mult), reads=[got['a'], cost], writes=[tmp])
                P.op('dve', lambda e: e.scalar_tensor_tensor(out=tmp[:64, :n], in0=tmp[:64, :n], scalar=QSCALE, in1=tmp2[:64, :n], op0=ALU.mult, op1=ALU.add),
                     reads=[tmp, tmp2], writes=[tmp])
                P.dma(qrT[hh, :, oc:oc + n], tmp[:64, :n], reads=[tmp], writes=[qrT], is_out=True)
        def ev_x(fc, ps, m):
            P.op('act', lambda e: e.copy(out=xbc[:, fc - XBC0 // 128, 1:n + 1], in_=ps[:, :n]), reads=[ps], writes=[xbc])
        emit_linear(P, ws, w_in, D, XBC0, DT0, h, n, ev_x)
        if kind == "lat":
            P.op('dve', lambda e: e.tensor_copy(out=xbc[:, :, 0:1], in_=xbch[:, :, 2 * hj:2 * hj + 1]), reads=[xbch], writes=[xbc])
            P.op('dve', lambda e: e.tensor_copy(out=xbc[:, :, n + 1:n + 2], in_=xbch[:, :, 2 * hj + 1:2 * hj + 2]), reads=[xbch], writes=[xbc])
        else:
            P.op('dve', lambda e: e.memset(xbc[:, :, 0:1], 0.0), writes=[xbc])
            P.op('dve', lambda e: e.memset(xbc[:, :, n + 1:n + 2], 0.0), writes=[xbc])
        for ci in range(12):
            P.op('dve', lambda e: e.tensor_scalar(out=tmp[:, :n], in0=xbc[:, ci, 0:n], scalar1=cwt[:, ci, 0:1], scalar2=None, op0=ALU.mult), reads=[xbc, cwt], writes=[tmp])
            P.op('dve', lambda e: e.scalar_tensor_tensor(out=tmp[:, :n], in0=xbc[:, ci, 1:n + 1], scalar=cwt[:, ci, 1:2], in1=tmp[:, :n], op0=ALU.mult, op1=ALU.add),
                 reads=[xbc, cwt, tmp], writes=[tmp])
            P.op('dve', lambda e: e.scalar_tensor_tensor(out=tmp[:, :n], in0=xbc[:, ci, 2:n + 2], scalar=cwt[:, ci, 2:3], in1=tmp[:, :n], op0=ALU.mult, op1=ALU.add),
                 reads=[xbc, cwt, tmp], writes=[tmp])
            P.op('act', lambda e: e.activation(out=x[:, ci, :n], in_=tmp[:, :n], func=AF.Silu, bias=cb[:, ci:ci + 1], scale=1.0), reads=[tmp, cb], writes=[x])
        P.dma(xbcT[:, oc:oc + n].rearrange("(c p) t -> p c t", p=128), x[:, 0:12, :n], reads=[x], writes=[xbcT], is_out=True)
        def ev_dt(fc, ps, m):
            P.op('act', lambda e: e.activation(out=tmp[:32, :n], in_=ps[:32, :n], func=AF.Exp, bias=dtb[:, 0:1], scale=1.0), reads=[ps, dtb], writes=[tmp])
            P.op('act', lambda e: e.activation(out=tmp[:32, :n], in_=tmp[:32, :n], func=AF.Ln, bias=1.0, scale=1.0), reads=[tmp], writes=[tmp])
            P.dma(dtT[:, oc:oc + n], tmp[:32, :n], reads=[tmp], writes=[dtT], is_out=True)
        emit_linear(P, ws, w_in, D, DT0, CKV0, h, n, ev_dt)
        def ev_ckv(fc_unused, ps, m, _st=[0]):
            i = _st[0] % 4
            _st[0] += 1
            P.op('act', lambda e: e.copy(out=cq[:, i, :n], in_=ps[:, :n]), reads=[ps], writes=[cq])
        emit_linear(P, ws, w_in, D, CKV0, KR0, h, n, ev_ckv)
        emit_rmsnorm_mod(P, c, cq, hq, kvnw, None, n, sq, rstd, nch=4)

        def ev_kv(fc, ps, m):
            P.op('act', lambda e: e.copy(out=x[:, fc, :n], in_=ps[:, :n]), reads=[ps], writes=[x])
        emit_linear(P, ws, w_ukv, 512, 0, 2048, hq, n, ev_kv)
        P.dma(kvT[:, oc:oc + n].rearrange("(c p) t -> p c t", p=128), x[:, :, :n], reads=[x], writes=[kvT], is_out=True)
        got = {}

        def ev_a(fc, ps, m):
            got['a'] = ps

        def ev_b(fc, ps, m):
            got['b'] = ps
        emit_linear(P, ws, w_in, D, KR0, 3680, h, n, ev_a)
        emit_linear(P, ws, w_in_sw, D, 0, 64, h, n, ev_b)
        P.op('dve', lambda e: e.tensor_tensor(out=tmp2[:64, :n], in0=got['b'][:64, :n], in1=sint[:, :n], op=ALU.mult), reads=[got['b'], sint], writes=[tmp2])
        P.op('dve', lambda e: e.tensor_tensor(out=tmp[:64, :n], in0=got['a'][:64, :n], in1=cost[:, :n], op=ALU.mult), reads=[got['a'], cost], writes=[tmp])
        P.op('dve', lambda e: e.tensor_tensor(out=tmp[:64, :n], in0=tmp[:64, :n], in1=tmp2[:64, :n], op=ALU.add), reads=[tmp, tmp2], writes=[tmp])
        P.dma(krT[:, oc:oc + n], tmp[:64, :n], reads=[tmp], writes=[krT], is_out=True)
    return P


def rope_tables(t0, n):
    t = np.arange(t0, t0 + n)
    row = (t // 64).astype(np.float32)
    col = (t % 64).astype(np.float32)
    inv = (np.float32(10000.0) ** (-np.arange(16, dtype=np.float32) / np.float32(16))).astype(np.float32)
    cosT = np.zeros((64, n), np.float32)
    sinT = np.zeros((64, n), np.float32)
    for d in range(64):
        pos = row if d < 32 else col
        ang = (pos * inv[d % 16]).astype(np.float32)
        cosT[d] = np.cos(ang)
        sinT[d] = np.sin(ang) * (-1.0 if (d % 32) < 16 else 1.0)
    return cosT, sinT


def swap_cols64(w):
    idx = np.array([d + 16 if (d % 32) < 16 else d - 16 for d in range(64)])
    return np.ascontiguousarray(w[..., idx])


def split_mod(m):
    return [np.ascontiguousarray(m[i * D:(i + 1) * D]) for i in range(6)]


def run_l1(inp, mod):
    P = build_l1()
    x = inp['x'][0]
    ctx = inp['ctx'][0]
    xT = np.ascontiguousarray(x.T)
    ctxT = np.ascontiguousarray(ctx.T)
    sh1, sc1 = split_mod(mod[0, 0])[:2]
    csh1, csc1 = split_mod(mod[0, 1])[:2]
    w_in = np.ascontiguousarray(inp['ev_w_in'][0])
    w_uq = np.ascontiguousarray(inp['ev_w_uq'][0])
    w_uq_sw = np.concatenate([swap_cols64(w_uq[:, 192 * h + 128:192 * h + 192]) for h in range(8)], axis=1)
    common = {
        "w_in": w_in, "w_in_sw": swap_cols64(w_in[:, KR0:3680]), "w_uq": w_uq, "w_uq_sw": np.ascontiguousarray(w_uq_sw),
        "w_ukv": np.ascontiguousarray(inp['ev_w_ukv'][0]),
        "nmw": vec_layout(inp['norm_mix_w'][0]), "sh1": vec_layout(sh1), "sc1": vec_layout(sc1),
        "csh1": vec_layout(csh1), "csc1": vec_layout(csc1),
        "qnw": vec_layout(inp['ev_q_norm_w'][0]), "kvnw": vec_layout(inp['ev_kv_norm_w'][0]),
        "convw": np.ascontiguousarray(inp['ev_conv_w'][0].T.reshape(12, 128, 3).transpose(1, 0, 2)),
        "convb": vec_layout(inp['ev_conv_b'][0]),
        "dtb": np.ascontiguousarray(inp['ev_dt_bias'][0].reshape(32, 1)),
    }
    in_maps = []
    for i in range(NCORE):
        t0 = i * TPC
        halo = np.zeros((D, 8), np.float32)
        hmask = np.zeros((128, 8), np.float32)
        for j in range(4):
            for s, tok in ((0, t0 + TT * j - 1), (1, t0 + TT * j + TT)):
                if 0 <= tok < SEQ:
                    halo[:, 2 * j + s] = xT[:, tok]
                    hmask[:, 2 * j + s] = 1.0
        xh = np.concatenate([halo, xT[:, t0:t0 + TPC], ctxT], axis=1)
        cosT, sinT = rope_tables(t0, TPC)
        cosT = np.concatenate([cosT, np.ones((64, CTX), np.float32)], axis=1)
        sinT = np.concatenate([sinT, np.zeros((64, CTX), np.float32)], axis=1)
        m = dict(common)
        m.update({"xh": np.ascontiguousarray(xh), "hmask": hmask, "cosT": np.ascontiguousarray(cosT), "sinT": np.ascontiguousarray(sinT)})
        in_maps.append(m)
    res = run_spmd(P, in_maps)
    o = {}
    o['zT'] = np.concatenate([r['zT'] for r in res], axis=1)
    o['qnT'] = np.concatenate([r['qnT'] for r in res], axis=2)
    o['qrT'] = np.concatenate([r['qrT'] for r in res], axis=2)
    for k in ['xbcT', 'dtT', 'kvT', 'krT']:
        lat = np.concatenate([r[k][:, :TPC] for r in res], axis=1)
        o[k] = np.concatenate([res[0][k][:, TPC:], lat], axis=1)
    return o


NT = CTX + SEQ
NCH = NT // 128
QB = 512


def build_l2():
    P = Prog()
    P.psum_ring = 6
    c = emit_consts(P)
    ring = P.psum[:6]
    state = {'n': 0}

    def ps():
        p = ring[state['n'] % 6]
        state['n'] += 1
        return p
    ps_o, ps_s = P.psum[6], P.psum[7]
    din = {}
    for d_ in "fb":
        din["xs_" + d_] = P.din("xs_" + d_, [NT, 128])
        din["dt_" + d_] = P.din("dt_" + d_, [NT, 2])
        din["Bt_" + d_] = P.din("Bt_" + d_, [128, NT])
        din["Ct_" + d_] = P.din("Ct_" + d_, [128, NT])
        din["Bk_" + d_] = P.din("Bk_" + d_, [NT, 128])
    alog = P.din("alog", [128, 4])
    tri_d = P.din("tri", [128, 128])
    mneg_d = P.din("mneg", [128, 128])
    qa_d = P.din("qa", [128, SEQ])
    qb_d = P.din("qb", [64, SEQ])
    ka_d = P.din("ka", [128, NT])
    kb_d = P.din("kb", [64, NT])
    v_d = P.din("v", [NT, 128])
    y_out = {d_: P.dout("y_" + d_, [SEQ, 128]) for d_ in "fb"}
    oT = P.dout("oT", [128, SEQ])
    tri = P.sb("tri_sb", [128, 128])
    mneg = P.sb("mneg_sb", [128, 128])
    P.dma(tri[:], tri_d[:], reads=[tri_d], writes=[tri])
    P.dma(mneg[:], mneg_d[:], reads=[mneg_d], writes=[mneg])
    A = P.sb("A_sb", [128, 4])
    P.dma(A[:], alog[:], reads=[alog], writes=[A])
    P.op('act', lambda e: e.activation(out=A[:], in_=A[:], func=AF.Exp), reads=[A], writes=[A])
    P.op('dve', lambda e: e.tensor_scalar(out=A[:], in0=A[:], scalar1=-1.0, scalar2=None, op0=ALU.mult), reads=[A], writes=[A])
    ones_b = P.sb("ones_b", [128, 128], BF16)
    P.op('dve', lambda e: e.memset(ones_b[:], 1.0), writes=[ones_b])

    hst = {}
    hbf = {}
    for d_ in "fb":
        for hh in range(2):
            hst[d_, hh] = P.sb("h_%s%d" % (d_, hh), [128, 64])
            hbf[d_, hh] = P.sb("hb_%s%d" % (d_, hh), [128, 64], BF16)
            P.op('dve', lambda e: e.memset(hst[d_, hh][:], 0.0), writes=[hst[d_, hh]])
            P.op('dve', lambda e: e.memset(hbf[d_, hh][:], 0.0), writes=[hbf[d_, hh]])
    NB = 2
    tl = []
    for b in range(NB):
        t = {}
        for nm, shp, dt_ in [("xc", [128, 128], F32), ("dtc", [128, 2], F32), ("btc", [128, 128], F32), ("ctc", [128, 128], F32), ("bkc", [128, 128], F32),
                             ("bt_bf", [128, 128], BF16), ("ct_bf", [128, 128], BF16), ("bk_bf", [128, 128], BF16),
                             ("a_t", [128, 2], F32), ("nacum", [128, 2], F32), ("eac", [128, 2], F32), ("ysb", [128, 128], F32)]:
            t[nm] = P.sb("%s_%d" % (nm, b), shp, dt_)
        for hh in range(2):
            for nm, shp, dt_ in [("abc", [128, 128], F32), ("alast", [128, 1], F32), ("seg", [128, 128], F32), ("Lm", [128, 128], F32),
                                 ("M", [128, 128], BF16), ("xdt", [128, 64], BF16), ("yd", [128, 64], F32), ("toend", [128, 1], F32),
                                 ("w2", [128, 1], F32), ("xw", [128, 64], BF16), ("cd", [128, 1], F32)]:
                t[nm, hh] = P.sb("%s_%d_%d" % (nm, b, hh), shp, dt_)
        tl.append(t)

    def ssd_pro(d_, ci, t):
        di = 0 if d_ == "f" else 1
        t0 = ci * 128
        lat = ci >= CTX // 128
        P.dma(t["xc"][:], din["xs_" + d_][t0:t0 + 128, :], reads=[din["xs_" + d_]], writes=[t["xc"]])
        P.dma(t["dtc"][:], din["dt_" + d_][t0:t0 + 128, :], reads=[din["dt_" + d_]], writes=[t["dtc"]])
        P.dma(t["btc"][:], din["Bt_" + d_][:, t0:t0 + 128], reads=[din["Bt_" + d_]], writes=[t["btc"]])
        P.dma(t["ctc"][:], din["Ct_" + d_][:, t0:t0 + 128], reads=[din["Ct_" + d_]], writes=[t["ctc"]])
        P.dma(t["bkc"][:], din["Bk_" + d_][t0:t0 + 128, :], reads=[din["Bk_" + d_]], writes=[t["bkc"]])
        for s, dd in (("btc", "bt_bf"), ("ctc", "ct_bf"), ("bkc", "bk_bf")):
            P.op('pool', lambda e: e.tensor_copy(out=t[dd][:], in_=t[s][:]), reads=[t[s]], writes=[t[dd]])
        P.op('dve', lambda e: e.tensor_tensor(out=t["a_t"][:], in0=t["dtc"][:], in1=A[:, 2 * di:2 * di + 2], op=ALU.mult), reads=[t["dtc"], A], writes=[t["a_t"]])

    def ssd_body(d_, ci, t):
        di = 0 if d_ == "f" else 1
        lat = ci >= CTX // 128
        p_acj = ps()
        P.op('pe', lambda e: e.matmul(p_acj[:, 0:2], tri[:], t["a_t"][:], start=True, stop=True), reads=[tri, t["a_t"]], writes=[p_acj])
        P.op('dve', lambda e: e.tensor_scalar(out=t["nacum"][:], in0=p_acj[:, 0:2], scalar1=-1.0, scalar2=None, op0=ALU.mult), reads=[p_acj], writes=[t["nacum"]])
        P.op('act', lambda e: e.activation(out=t["eac"][:], in_=p_acj[:, 0:2], func=AF.Exp), reads=[p_acj], writes=[t["eac"]])
        if lat:
            p_g = ps()
            P.op('pe', lambda e: e.matmul(p_g[:, 0:128], t["bt_bf"][:], t["ct_bf"][:], start=True, stop=True), reads=[t["bt_bf"], t["ct_bf"]], writes=[p_g])
        for hh in range(2):
            hs = slice(64 * hh, 64 * hh + 64)
            abc, alast, seg, Lm, M, xdt, yd, toend, w2, xw, cd = [t[nm, hh] for nm in ("abc", "alast", "seg", "Lm", "M", "xdt", "yd", "toend", "w2", "xw", "cd")]
            P.op('dve', lambda e: e.tensor_scalar(out=abc[:], in0=c['ones_f'][:], scalar1=t["a_t"][:, hh:hh + 1], scalar2=None, op0=ALU.mult),
                 reads=[c['ones_f'], t["a_t"]], writes=[abc])
            p_row = ps()
            P.op('pe', lambda e: e.matmul(p_row[:, 0:128], abc[:], tri[:], start=True, stop=True), reads=[abc, tri], writes=[p_row])
            P.op('act', lambda e: e.copy(out=alast[:], in_=p_row[:, 127:128]), reads=[p_row], writes=[alast])
            if lat:
                P.op('dve', lambda e: e.tensor_tensor(out=seg[:], in0=p_row[:, 0:128], in1=mneg[:], op=ALU.add), reads=[p_row, mneg], writes=[seg])
                P.op('act', lambda e: e.activation(out=Lm[:], in_=seg[:], func=AF.Exp, bias=t["nacum"][:, hh:hh + 1], scale=1.0), reads=[seg, t["nacum"]], writes=[Lm])
                P.op('dve', lambda e: e.tensor_tensor(out=M[:], in0=p_g[:, 0:128], in1=Lm[:], op=ALU.mult), reads=[p_g, Lm], writes=[M])
                P.op('dve', lambda e: e.tensor_scalar(out=xdt[:], in0=t["xc"][:, hs], scalar1=t["dtc"][:, hh:hh + 1], scalar2=None, op0=ALU.mult),
                     reads=[t["xc"], t["dtc"]], writes=[xdt])
                p_yd = ps()
                P.op('pe', lambda e: e.matmul(p_yd[:, 0:64], M[:], xdt[:], start=True, stop=True), reads=[M, xdt], writes=[p_yd])
                p_yo = ps()
                P.op('pe', lambda e: e.matmul(p_yo[:, 0:64], t["ct_bf"][:], hbf[d_, hh][:], start=True, stop=True), reads=[t["ct_bf"], hbf[d_, hh]], writes=[p_yo])
                P.op('act', lambda e: e.copy(out=yd[:], in_=p_yd[:, 0:64]), reads=[p_yd], writes=[yd])
                P.op('dve', lambda e: e.scalar_tensor_tensor(out=t["ysb"][:, hs], in0=p_yo[:, 0:64], scalar=t["eac"][:, hh:hh + 1], in1=yd[:], op0=ALU.mult, op1=ALU.add),
                     reads=[p_yo, t["eac"], yd], writes=[t["ysb"]])
            P.op('act', lambda e: e.activation(out=toend[:], in_=t["nacum"][:, hh:hh + 1], func=AF.Exp, bias=alast[:, 0:1], scale=1.0), reads=[t["nacum"], alast], writes=[toend])
            P.op('dve', lambda e: e.tensor_tensor(out=w2[:], in0=toend[:], in1=t["dtc"][:, hh:hh + 1], op=ALU.mult), reads=[toend, t["dtc"]], writes=[w2])
            P.op('dve', lambda e: e.tensor_scalar(out=xw[:], in0=t["xc"][:, hs], scalar1=w2[:, 0:1], scalar2=None, op0=ALU.mult), reads=[t["xc"], w2], writes=[xw])
            p_st = ps()
            P.op('pe', lambda e: e.matmul(p_st[:, 0:64], t["bk_bf"][:], xw[:], start=True, stop=True), reads=[t["bk_bf"], xw], writes=[p_st])
            P.op('act', lambda e: e.activation(out=cd[:], in_=alast[:], func=AF.Exp), reads=[alast], writes=[cd])
            P.op('dve', lambda e: e.scalar_tensor_tensor(out=hst[d_, hh][:], in0=hst[d_, hh][:], scalar=cd[:, 0:1], in1=p_st[:, 0:64], op0=ALU.mult, op1=ALU.add),
                 reads=[hst[d_, hh], cd, p_st], writes=[hst[d_, hh]])
            P.op('pool', lambda e: e.tensor_copy(out=hbf[d_, hh][:], in_=hst[d_, hh][:]), reads=[hst[d_, hh]], writes=[hbf[d_, hh]])
        if lat:
            o0 = (ci - CTX // 128) * 128
            P.dma(y_out[d_][o0:o0 + 128, :], t["ysb"][:], reads=[t["ysb"]], writes=[y_out[d_]], is_out=True)

    def ssd_gen():
        units = [(d_, ci) for ci in range(NCH) for d_ in "fb"]
        ssd_pro(units[0][0], units[0][1], tl[0])
        yield
        for n, (d_, ci) in enumerate(units):
            ssd_body(d_, ci, tl[n % NB])
            if n + 1 < len(units):
                ssd_pro(units[n + 1][0], units[n + 1][1], tl[(n + 1) % NB])
            yield

    ka = P.sb("ka_bf", [128, NT], BF16)
    kb = P.sb("kb_bf", [64, NT], BF16)
    vb = P.sb("v_bf", [128, NCH, 128], BF16)
    stg = [P.sb("astg%d" % i, [128, 1280]) for i in range(2)]
    qa = [P.sb("qa%d" % i, [128, QB], BF16) for i in range(2)]
    qb = [P.sb("qb%d" % i, [64, QB], BF16) for i in range(2)]
    Et = [P.sb("E%d" % i, [128, QB], BF16) for i in range(5)]
    osb = P.sb("osb", [128, QB])
    rs = P.sb("rs", [128, QB])

    def attn_gen():
        n = 0
        CH = 1040
        for i in range(NT // CH):
            s = stg[n % 2]
            n += 1
            P.dma(s[:, :CH], ka_d[:, i * CH:(i + 1) * CH], reads=[ka_d], writes=[s])
            P.op('act', lambda e: e.copy(out=ka[:, i * CH:(i + 1) * CH], in_=s[:, :CH]), reads=[s], writes=[ka])
            s = stg[n % 2]
            n += 1
            P.dma(s[:64, :CH], kb_d[:, i * CH:(i + 1) * CH], reads=[kb_d], writes=[s])
            P.op('act', lambda e: e.copy(out=kb[:, i * CH:(i + 1) * CH], in_=s[:64, :CH]), reads=[s], writes=[kb])
        vv = v_d[:, :].rearrange("(kt p) d -> p kt d", p=128)
        for i in range(NCH // 10):
            s = stg[n % 2]
            n += 1
            P.dma(s[:, :1280].rearrange("p (k d) -> p k d", k=10), vv[:, i * 10:(i + 1) * 10, :], reads=[v_d], writes=[s])
            P.op('act', lambda e: e.copy(out=vb[:, i * 10:(i + 1) * 10, :], in_=s[:, :1280].rearrange("p (k d) -> p k d", k=10)), reads=[s], writes=[vb])
        yield
        ne = 0
        for qi in range(SEQ // QB):
            q0 = qi * QB
            qat, qbt = qa[qi % 2], qb[qi % 2]
            s = stg[n % 2]
            n += 1
            P.dma(s[:, :QB], qa_d[:, q0:q0 + QB], reads=[qa_d], writes=[s])
            P.op('act', lambda e: e.copy(out=qat[:], in_=s[:, :QB]), reads=[s], writes=[qat])
            s = stg[n % 2]
            n += 1
            P.dma(s[:64, :QB], qb_d[:, q0:q0 + QB], reads=[qb_d], writes=[s])
            P.op('act', lambda e: e.copy(out=qbt[:], in_=s[:64, :QB]), reads=[s], writes=[qbt])
            SK = 2
            Es = {}
            for kk in range(NCH + SK):
                if kk < NCH:
                    kt = kk
                    p_s = ps()
                    P.op('pe', lambda e: e.matmul(p_s[:, :QB], ka[:, kt * 128:(kt + 1) * 128], qat[:], start=True, stop=False), reads=[ka, qat], writes=[p_s])
                    P.op('pe', lambda e: e.matmul(p_s[:, :QB], kb[:, kt * 128:(kt + 1) * 128], qbt[:], start=False, stop=True), reads=[kb, qbt], writes=[p_s])
                    E = Et[ne % len(Et)]
                    ne += 1
                    P.op('act', lambda e: e.activation(out=E[:], in_=p_s[:, :QB], func=AF.Exp), reads=[p_s], writes=[E])
                    Es[kt] = E
                if kk >= SK:
                    kt = kk - SK
                    E = Es.pop(kt)
                    P.op('pe', lambda e: e.matmul(ps_o[:, :QB], vb[:, kt, :], E[:], start=(kt == 0), stop=(kt == NCH - 1)), reads=[vb, E], writes=[ps_o])
                    P.op('pe', lambda e: e.matmul(ps_s[:, :QB], ones_b[:], E[:], start=(kt == 0), stop=(kt == NCH - 1)), reads=[ones_b, E], writes=[ps_s])
                yield
            P.op('dve', lambda e: e.reciprocal(out=rs[:], in_=ps_s[:, :QB]), reads=[ps_s], writes=[rs])
            P.op('dve', lambda e: e.tensor_tensor(out=osb[:], in0=ps_o[:, :QB], in1=rs[:], op=ALU.mult), reads=[ps_o, rs], writes=[osb])
            P.dma(oT[:, q0:q0 + QB], osb[:], reads=[osb], writes=[oT], is_out=True)

    ag = attn_gen()
    sg = ssd_gen()
    a_done = s_done = False
    k = 0
    while not (a_done and s_done):
        if not a_done:
            try:
                next(ag)
            except StopIteration:
                a_done = True
        k += 1
        if (k % 16 == 0 or a_done) and not s_done:
            try:
                next(sg)
            except StopIteration:
                s_done = True
    return P


def run_l2(inp, o1):
    P = build_l2()
    xbcT, dtT, kvT, krT = o1['xbcT'], o1['dtT'], o1['kvT'], o1['krT']
    idx_b = np.concatenate([np.arange(CTX - 1, -1, -1), CTX + np.arange(SEQ - 1, -1, -1)])
    a_log = inp['ev_a_log'][0]
    tri = np.triu(np.ones((128, 128), np.float32))
    mneg = np.where(np.arange(128)[:, None] <= np.arange(128)[None, :], 0.0, -30000.0).astype(np.float32)
    in_maps = []
    for i in range(NCORE):
        g = i // 4
        xs_f = np.ascontiguousarray(xbcT[128 * i:128 * i + 128, :].T)
        dt_f = np.ascontiguousarray(dtT[2 * i:2 * i + 2, :].T)
        dt_b = np.ascontiguousarray(dtT[16 + 2 * i:16 + 2 * i + 2, :].T[idx_b])
        Bt_f = np.ascontiguousarray(xbcT[1024 + 128 * g:1024 + 128 * g + 128, :])
        Ct_f = np.ascontiguousarray(xbcT[1280 + 128 * g:1280 + 128 * g + 128, :])
        Bt_b = np.ascontiguousarray(Bt_f[:, idx_b])
        Ct_b = np.ascontiguousarray(Ct_f[:, idx_b])
        al = np.array([a_log[0, 2 * i], a_log[0, 2 * i + 1], a_log[1, 2 * i], a_log[1, 2 * i + 1]], np.float32)
        m = {
            "xs_f": xs_f, "xs_b": np.ascontiguousarray(xs_f[idx_b]), "dt_f": dt_f, "dt_b": dt_b,
            "Bt_f": Bt_f, "Bt_b": Bt_b, "Ct_f": Ct_f, "Ct_b": Ct_b,
            "Bk_f": np.ascontiguousarray(Bt_f.T), "Bk_b": np.ascontiguousarray(Bt_b.T),
            "alog": np.ascontiguousarray(np.tile(al[None, :], (128, 1))), "tri": tri, "mneg": mneg,
            "qa": np.ascontiguousarray(o1['qnT'][i]), "qb": np.ascontiguousarray(o1['qrT'][i]),
            "ka": np.ascontiguousarray(kvT[256 * i:256 * i + 128, :]), "kb": np.ascontiguousarray(krT),
            "v": np.ascontiguousarray(kvT[256 * i + 128:256 * i + 256, :].T),
        }
        in_maps.append(m)
    res = run_spmd(P, in_maps)
    o = {}
    o['yfT'] = np.ascontiguousarray(np.concatenate([r['y_f'] for r in res], axis=1).T)
    o['ybT'] = np.ascontiguousarray(np.concatenate([r['y_b'][::-1] for r in res], axis=1).T)
    o['oT'] = np.concatenate([r['oT'] for r in res], axis=0)
    return o


def build_l4():
    P = Prog()
    c = emit_consts(P)
    xT = P.din("xT", [D, TPC])
    srcs = {k: P.din(k, [1024, TPC]) for k in ["zT", "yfT", "ybT", "xsT", "oT"]}
    dsk_d = P.din("dsk", [128, 8, 2])
    w_o = P.din("w_o", [D, D])
    w1 = P.din("w1", [D, DFF])
    w3 = P.din("w3", [D, DFF])
    w2 = P.din("w2", [DFF, D])
    hw = P.din("hy_w_in", [D, 3 * D])
    x2T = P.dout("x2T", [D, TPC])
    projT = P.dout("projT", [3 * D, TPC])
    snw = load_vec(P, "snw", 8)
    g1 = load_vec(P, "g1")
    nfw = load_vec(P, "nfw")
    sc2 = load_vec(P, "sc2")
    sh2 = load_vec(P, "sh2")
    g2 = load_vec(P, "g2")
    nmw1 = load_vec(P, "nmw1")
    sc1b = load_vec(P, "sc1b")
    sh1b = load_vec(P, "sh1b")
    hb = load_vec(P, "hb", 48)
    dsk2 = P.sb("dsk2", [128, 8, 2])
    P.dma(dsk2[:], dsk_d[:], reads=[dsk_d], writes=[dsk2])
    dsk = P.sb("dsk_sum", [128, 8])
    P.op('dve', lambda e: e.tensor_tensor(out=dsk[:], in0=dsk2[:, :, 0], in1=dsk2[:, :, 1], op=ALU.add), reads=[dsk2], writes=[dsk])
    scl2 = P.sb("scl2", [128, 16])
    P.op('dve', lambda e: e.scalar_tensor_tensor(out=scl2[:], in0=sc2[:], scalar=1.0, in1=nfw[:], op0=ALU.add, op1=ALU.mult), reads=[sc2, nfw], writes=[scl2])
    scl1b = P.sb("scl1b", [128, 16])
    P.op('dve', lambda e: e.scalar_tensor_tensor(out=scl1b[:], in0=sc1b[:], scalar=1.0, in1=nmw1[:], op0=ALU.add, op1=ALU.mult), reads=[sc1b, nmw1], writes=[scl1b])
    ws = WStream(P)
    x = P.sb("x", [128, 16, TT])
    h = P.sb("h", [128, 16, TT], BF16)
    u = P.sb("u", [128, DFF // 128, TT], BF16)
    gy = P.sb("gy", [128, 8, TT])
    sq = P.sb("sq", [128, TT])
    rstd = P.sb("rstd", [128, TT])
    tmp = P.sb("tmp", [128, TT])
    ld = {k: P.sb("ld_" + k, [128, TT]) for k in ["zT", "yfT", "ybT", "xsT"]}
    stage = [P.sb("stage%d" % i, [128, TT]) for i in range(2)]
    xv = xT[:, :].rearrange("(kc p) t -> p kc t", p=128)
    x2v = x2T[:, :].rearrange("(kc p) t -> p kc t", p=128)
    sv = {k: v[:, :].rearrange("(kc p) t -> p kc t", p=128) for k, v in srcs.items()}
    pv = projT[:, :].rearrange("(kc p) t -> p kc t", p=128)
    ns = 0
    for tt in range(TPC // TT):
        t0 = tt * TT
        P.dma(x[:], xv[:, :, t0:t0 + TT], reads=[xT], writes=[x])
        for kc in range(8):
            for k in ["zT", "yfT", "ybT", "xsT"]:
                P.dma(ld[k][:], sv[k][:, kc, t0:t0 + TT], reads=[srcs[k]], writes=[ld[k]])
            P.op('dve', lambda e: e.tensor_tensor(out=tmp[:], in0=ld["yfT"][:], in1=ld["ybT"][:], op=ALU.add), reads=[ld["yfT"], ld["ybT"]], writes=[tmp])
            P.op('dve', lambda e: e.scalar_tensor_tensor(out=tmp[:], in0=ld["xsT"][:], scalar=dsk[:, kc:kc + 1], in1=tmp[:], op0=ALU.mult, op1=ALU.add),
                 reads=[ld["xsT"], dsk, tmp], writes=[tmp])
            P.op('act', lambda e: e.activation(out=sq[:], in_=ld["zT"][:], func=AF.Silu), reads=[ld["zT"]], writes=[sq])
            P.op('dve', lambda e: e.tensor_tensor(out=gy[:, kc, :], in0=tmp[:], in1=sq[:], op=ALU.mult), reads=[tmp, sq], writes=[gy])
            P.dma(ld["zT"][:], sv["oT"][:, kc, t0:t0 + TT], reads=[srcs["oT"]], writes=[ld["zT"]])
            P.op('act', lambda e: e.copy(out=h[:, kc, :], in_=ld["zT"][:]), reads=[ld["zT"]], writes=[h])
        for g in range(2):
            emit_rmsnorm_mod(P, c, gy, h, snw, None, TT, sq, rstd, nch=4, dim=512, ki=4 * g, ko=8 + 4 * g, ks=4 * g)

        def ev_o(dc, ps, m):
            P.op('dve', lambda e: e.scalar_tensor_tensor(out=x[:, dc, :], in0=ps[:, :TT], scalar=g1[:, dc:dc + 1], in1=x[:, dc, :], op0=ALU.mult, op1=ALU.add),
                 reads=[ps, g1, x], writes=[x])
        emit_linear(P, ws, w_o, D, 0, D, h, TT, ev_o)
        emit_rmsnorm_mod(P, c, x, h, scl2, sh2, TT, sq, rstd)
        emit_ffn(P, c, ws, x, h, u, tmp, w1, w3, w2, g2, TT)
        P.dma(x2v[:, :, t0:t0 + TT], x[:], reads=[x], writes=[x2T], is_out=True)
        emit_rmsnorm_mod(P, c, x, h, scl1b, sh1b, TT, sq, rstd)

        def ev_p(fc, ps, m):
            nonlocal ns
            st = stage[ns % 2]
            ns += 1
            P.op('act', lambda e: e.activation(out=st[:], in_=ps[:, :TT], func=AF.Identity, bias=hb[:, fc:fc + 1], scale=1.0), reads=[ps, hb], writes=[st])
            P.dma(pv[:, fc, t0:t0 + TT], st[:], reads=[st], writes=[projT], is_out=True)
        emit_linear(P, ws, hw, D, 0, 3 * D, h, TT, ev_p)
    return P


def tok_shard(a, i):
    return np.ascontiguousarray(a[:, i * TPC:(i + 1) * TPC])


def run_l4(inp, mod, o1, o2):
    P = build_l4()
    xT = np.ascontiguousarray(inp['x'][0].T)
    m0 = split_mod(mod[0, 0])
    m1 = split_mod(mod[1, 0])
    dsk = np.repeat(inp['ev_d_skip'][0].T, 64, axis=0)
    common = {
        "dsk": np.ascontiguousarray(dsk.reshape(8, 128, 2).transpose(1, 0, 2)),
        "w_o": np.ascontiguousarray(inp['ev_w_o'][0]), "w1": np.ascontiguousarray(inp['ffn_w1'][0]), "w3": np.ascontiguousarray(inp['ffn_w3'][0]),
        "w2": np.ascontiguousarray(inp['ffn_w2'][0]), "hy_w_in": np.ascontiguousarray(inp['hy_w_in'][0]),
        "snw": vec_layout(inp['ev_ssd_norm_w'][0]), "g1": vec_layout(m0[2]), "nfw": vec_layout(inp['norm_ffn_w'][0]),
        "sc2": vec_layout(m0[4]), "sh2": vec_layout(m0[3]), "g2": vec_layout(m0[5]),
        "nmw1": vec_layout(inp['norm_mix_w'][1]), "sc1b": vec_layout(m1[1]), "sh1b": vec_layout(m1[0]),
        "hb": vec_layout(inp['hy_b_in'][0]),
    }
    xsT = o1['xbcT'][:1024, CTX:]
    in_maps = []
    for i in range(NCORE):
        m = dict(common)
        m.update({"xT": tok_shard(xT, i), "zT": tok_shard(o1['zT'], i), "yfT": tok_shard(o2['yfT'], i), "ybT": tok_shard(o2['ybT'], i),
                  "xsT": tok_shard(xsT, i), "oT": tok_shard(o2['oT'], i)})
        in_maps.append(m)
    res = run_spmd(P, in_maps)
    return {"x2T": np.concatenate([r['x2T'] for r in res], axis=1), "projT": np.concatenate([r['projT'] for r in res], axis=1)}


F32R = mybir.dt.float32r


def r32(ap):
    return ap.bitcast(F32R)


CPC = D // NCORE
CBK = 32
NFFT = 2 * SEQ
PI = float(np.pi)


def build_l5():
    P = Prog()
    c = emit_consts(P)
    pj = P.din("pj", [3, CPC, SEQ + 2])
    sw_d = P.din("sw", [128, 6, 3])
    sb_d = P.din("sb", [128, 6])
    embs = [P.din("embT", [33, SEQ]), P.din("embTr", [33, SEQ])]
    fw1_d = P.din("fw1", [33, 64])
    fwm_d = P.din("fwm", [2, 64, 64])
    fq_d = P.din("fq", [64, 1])
    fb_d = P.din("fb", [64, 3])
    fwo_d = P.din("fwo", [64, 2, 2, CPC])
    dpp_d = P.din("dpp", [128, 2])
    ident_d = P.din("ident", [128, 128])
    trows = [P.din("trow", [128, SEQ]), P.din("trowr", [128, SEQ])]
    fbias_d = P.din("fbias", [128, 2, CPC])
    cn = {}
    for nm, shp in [("W1", [2, 128, 512]), ("TWrr", [128, 512]), ("TWii", [128, 512]), ("W2r", [128, 128]), ("W2i", [128, 128]),
                    ("WIa", [128, 256]), ("WIb", [128, 256]), ("TWIrr", [2, 128, 256]), ("TWIii", [2, 128, 256]),
                    ("Cos", [2, 128, 128]), ("NSin", [2, 128, 128])]:
        cn[nm] = (P.din(nm, shp), shp)
    y2 = P.dout("y2", [CPC, SEQ])
    cv = P.dscratch("cv", [3, CPC, SEQ])
    kd = P.dscratch("kd", [2, 2, CPC, SEQ])
    rn = P.sb("rn", [128, 2, CPC])
    fbias = P.sb("fbias_sb", [128, 2, CPC])
    P.dma(fbias[:], fbias_d[:], reads=[fbias_d], writes=[fbias])

    P.push()
    _sc = P.nc.named_scope("phaseA")
    _sc.__enter__()
    swt = P.sb("swt", [128, 6, 3])
    sbt = P.sb("sbt", [128, 6])
    P.dma(swt[:], sw_d[:], reads=[sw_d], writes=[swt])
    P.dma(sbt[:], sb_d[:], reads=[sb_d], writes=[sbt])
    HC = SEQ // 2
    U = [P.sb("U%d" % i, [128, HC + 2]) for i in range(2)]
    V = [P.sb("V%d" % i, [128, HC]) for i in range(2)]
    n = 0
    for s_ in range(3):
        for gq in range(2):
            for hc in range(2):
                u_, v_ = U[n % 2], V[n % 2]
                n += 1
                P.dma(u_[:], pj[s_, gq * 128:(gq + 1) * 128, hc * HC:hc * HC + HC + 2], reads=[pj], writes=[u_])
                k = s_ * 2 + gq
                for b0 in range(0, HC, 2048):
                    sl = slice(b0, b0 + 2048)
                    P.op('dve', lambda e: e.tensor_scalar(out=v_[:, sl], in0=u_[:, b0:b0 + 2048], scalar1=swt[:, k, 0:1], scalar2=None, op0=ALU.mult), reads=[u_, swt], writes=[v_])
                    P.op('dve', lambda e: e.scalar_tensor_tensor(out=v_[:, sl], in0=u_[:, b0 + 1:b0 + 2049], scalar=swt[:, k, 1:2], in1=v_[:, sl], op0=ALU.mult, op1=ALU.add),
                         reads=[u_, swt, v_], writes=[v_])
                    P.op('dve', lambda e: e.scalar_tensor_tensor(out=v_[:, sl], in0=u_[:, b0 + 2:b0 + 2050], scalar=swt[:, k, 2:3], in1=v_[:, sl], op0=ALU.mult, op1=ALU.add),
                         reads=[u_, swt, v_], writes=[v_])
                    P.op('act', lambda e: e.activation(out=v_[:, sl], in_=v_[:, sl], func=AF.Identity, bias=sbt[:, k:k + 1], scale=1.0), reads=[v_, sbt], writes=[v_])
                P.dma(cv[s_, gq * 128:(gq + 1) * 128, hc * HC:(hc + 1) * HC], v_[:], reads=[v_], writes=[cv])
    P.pop()
    _sc.__exit__(None, None, None)

    P.push()
    _sc = P.nc.named_scope("phaseB")
    _sc.__enter__()
    fw1 = P.sb("fw1_sb", [33, 64])
    fwm = P.sb("fwm_sb", [64, 2, 64])
    fq = P.sb("fq_sb", [64, 1])
    fb = P.sb("fb_sb", [64, 3])
    fbq = P.sb("fbq", [64, 3])
    fwo = P.sb("fwo_sb", [64, 2, 2, CPC])
    dpp = P.sb("dpp_sb", [128, 2])
    ident = P.sb("ident_sb", [128, 128])
    P.dma(fw1[:], fw1_d[:], reads=[fw1_d], writes=[fw1])
    P.dma(fwm[:], fwm_d[:].rearrange("j a b -> a j b"), reads=[fwm_d], writes=[fwm])
    P.dma(fq[:], fq_d[:], reads=[fq_d], writes=[fq])
    P.dma(fb[:], fb_d[:], reads=[fb_d], writes=[fb])
    P.dma(fwo[:], fwo_d[:], reads=[fwo_d], writes=[fwo])
    P.dma(dpp[:], dpp_d[:], reads=[dpp_d], writes=[dpp])
    P.dma(ident[:], ident_d[:], reads=[ident_d], writes=[ident])
    P.op('dve', lambda e: e.tensor_scalar(out=fbq[:], in0=fb[:], scalar1=fq[:, 0:1], scalar2=None, op0=ALU.mult), reads=[fb, fq], writes=[fbq])
    hid = P.sb("hid", [64, SEQ])
    embt = [P.sb("embt%d" % i, [33, 512]) for i in range(2)]
    arg = [P.sb("arg%d" % i, [64, 512]) for i in range(2)]
    hcur = [P.sb("hcur%d" % i, [64, 512]) for i in range(2)]
    wr1 = P.sb("wr1", [64, 512])
    wr2 = P.sb("wr2", [64, 512])
    trt = [P.sb("trt%d" % i, [128, 512]) for i in range(2)]
    wint = [P.sb("wint%d" % i, [128, 512]) for i in range(2)]
    kstg = [P.sb("kstg%d" % i, [128, 2048]) for i in range(2)]
    acc = {(o, dc): P.sb("acc_%d_%d" % (o, dc), [128, 64]) for o in range(2) for dc in range(2)}

    def sin_layer(ps, j, dst_ap, dst_tr, a):
        P.op('dve', lambda e: e.tensor_scalar(out=a[:], in0=ps[:64, :512], scalar1=fq[:, 0:1], scalar2=fbq[:, j:j + 1], op0=ALU.mult, op1=ALU.add), reads=[ps, fq, fbq], writes=[a])
        for _ in range(2):
            P.op('dve', lambda e: e.tensor_scalar(out=wr1[:], in0=a[:], scalar1=PI, scalar2=-2.0 * PI, op0=ALU.is_gt, op1=ALU.mult), reads=[a], writes=[wr1])
            P.op('dve', lambda e: e.tensor_scalar(out=wr2[:], in0=a[:], scalar1=-PI, scalar2=2.0 * PI, op0=ALU.is_lt, op1=ALU.mult), reads=[a], writes=[wr2])
            P.op('dve', lambda e: e.tensor_tensor(out=a[:], in0=a[:], in1=wr1[:], op=ALU.add), reads=[a, wr1], writes=[a])
            P.op('dve', lambda e: e.tensor_tensor(out=a[:], in0=a[:], in1=wr2[:], op=ALU.add), reads=[a, wr2], writes=[a])
        P.op('act', lambda e: e.activation(out=dst_ap, in_=a[:], func=AF.Sin), reads=[a], writes=[dst_tr])

    nq = 0
    nst = 0
    for half in range(2):
        for ti in range(SEQ // 512):
            et = embt[ti % 2]
            a = arg[ti % 2]
            hc_ = hcur[ti % 2]
            P.dma(et[:], embs[half][:, ti * 512:(ti + 1) * 512], reads=[embs[half]], writes=[et])
            ps = P.ps()
            P.op('pe', lambda e: e.matmul(ps[:64, :512], fw1[:], et[:], start=True, stop=True), reads=[fw1, et], writes=[ps])
            sin_layer(ps, 0, hc_[:], hc_, a)
            ps = P.ps()
            P.op('pe', lambda e: e.matmul(ps[:64, :512], fwm[:, 0, :], hc_[:], start=True, stop=True), reads=[fwm, hc_], writes=[ps])
            sin_layer(ps, 1, hc_[:], hc_, a)
            ps = P.ps()
            P.op('pe', lambda e: e.matmul(ps[:64, :512], fwm[:, 1, :], hc_[:], start=True, stop=True), reads=[fwm, hc_], writes=[ps])
            sin_layer(ps, 2, hid[:, ti * 512:(ti + 1) * 512], hid, a)
        for o in range(2):
            for dc in range(2):
                for ti in range(SEQ // 512):
                    tr_ = trt[nq % 2]
                    wi_ = wint[nq % 2]
                    nq += 1
                    if ti % 4 == 0:
                        kst_ = kstg[nst % 2]
                        nst += 1
                    ksl = kst_[:, (ti % 4) * 512:(ti % 4 + 1) * 512]
                    P.dma(tr_[:], trows[half][:, ti * 512:(ti + 1) * 512], reads=[trows[half]], writes=[tr_])
                    ps = P.ps()
                    P.op('pe', lambda e: e.matmul(ps[:, :512], fwo[:, o, half, dc * 128:(dc + 1) * 128], hid[:, ti * 512:(ti + 1) * 512], start=True, stop=True), reads=[fwo, hid], writes=[ps])
                    P.op('act', lambda e: e.activation(out=wi_[:], in_=tr_[:], func=AF.Exp, scale=dpp[:, dc:dc + 1]), reads=[tr_, dpp], writes=[wi_])
                    P.op('dve', lambda e: e.tensor_tensor(out=ksl, in0=ps[:, :512], in1=wi_[:], op=ALU.mult), reads=[ps, wi_], writes=[kst_])
                    if half == 1 and ti == 0:
                        P.op('dve', lambda e: e.memset(kst_[:, 0:1], 0.0), writes=[kst_])
                    col = half * 32 + ti
                    P.op('dve', lambda e: e.tensor_reduce(out=acc[o, dc][:, col:col + 1], in_=ksl, axis=AX.X, op=ALU.add, apply_absolute_value=True), reads=[kst_], writes=[acc[o, dc]])
                    if ti % 4 == 3:
                        P.dma(kd[o, half, dc * 128:(dc + 1) * 128, (ti - 3) * 512:(ti + 1) * 512], kst_[:], reads=[kst_], writes=[kd])
    nrm = P.sb("nrm", [128, 1])
    dg = P.sb("dg", [128, 128])
    for o in range(2):
        for dc in range(2):
            P.op('dve', lambda e: e.tensor_reduce(out=nrm[:], in_=acc[o, dc][:], axis=AX.X, op=ALU.add), reads=[acc[o, dc]], writes=[nrm])
            P.op('dve', lambda e: e.reciprocal(out=nrm[:], in_=nrm[:]), reads=[nrm], writes=[nrm])
            P.op('dve', lambda e: e.tensor_scalar(out=dg[:], in0=ident[:], scalar1=nrm[:, 0:1], scalar2=None, op0=ALU.mult), reads=[ident, nrm], writes=[dg])
            ps = P.ps()
            P.op('pe', lambda e: e.matmul(ps[:, :128], c['ones_f'][:], dg[:], start=True, stop=True), reads=[c['ones_f'], dg], writes=[ps])
            P.op('dve', lambda e: e.tensor_copy(out=rn[:, o, dc * 128:(dc + 1) * 128], in_=ps[:, :128]), reads=[ps], writes=[rn])
    P.pop()
    _sc.__exit__(None, None, None)

    P.push()
    _sc = P.nc.named_scope("phaseC")
    _sc.__enter__()
    ct = {}
    Kst = P.sb("Kst", [128, CBK, 128])
    cst = P.sb("cst", [128, 1024])
    for nm, (dr, shp) in cn.items():
        rnd = nm in ("W1", "W2r", "W2i", "WIa", "WIb")
        if len(shp) == 3:
            t = P.sb(nm + "_sb", [shp[1], shp[0], shp[2]])
            sv = cst[:, 0:shp[0] * shp[2]].rearrange("p (a f) -> p a f", a=shp[0])
            P.dma(sv if rnd else t[:], dr[:].rearrange("a p f -> p a f"), reads=[dr], writes=[cst if rnd else t])
            if rnd:
                P.op('act', lambda e: e.copy(out=r32(t[:]), in_=sv), reads=[cst], writes=[t])
        else:
            t = P.sb(nm + "_sb", shp)
            sv = cst[:, 0:shp[1]]
            P.dma(sv if rnd else t[:], dr[:], reads=[dr], writes=[cst if rnd else t])
            if rnd:
                P.op('act', lambda e: e.copy(out=r32(t[:]), in_=sv), reads=[cst], writes=[t])
        ct[nm] = t
    Yt = [P.sb("Yt%d" % i, [128, CBK, 128]) for i in range(2)]
    Xg = [P.sb("Xg%d" % i, [128, CBK, 128]) for i in range(2)]
    Yr = P.sb("Yr", [128, CBK, 128])
    Kp = P.sb("Kp", [128, CBK, 128])
    Kf = P.sb("Kf", [128, CBK, 128])
    W = {nm: P.sb("w_" + nm, [128, 512]) for nm in ["P1", "P2", "B", "B2", "XkRR", "XkII", "Zr", "Zi"]}
    Dr = [P.sb("Dr%d" % i, [128, 128]) for i in range(2)]
    Di = [P.sb("Di%d" % i, [128, 128]) for i in range(2)]
    t1 = P.sb("t1", [128, 128])

    def cv_view(s_, db):
        return cv[s_, db * CBK:(db + 1) * CBK, :].rearrange("d (p j) -> p d j", j=128)

    def twiddle(psA, rr, ii, width, outs):
        w = width
        P.op('dve', lambda e: e.tensor_tensor(out=W["P1"][:, :2 * w], in0=psA[:, :2 * w], in1=rr, op=ALU.mult), reads=[psA] + outs['tabs'], writes=[W["P1"]])
        P.op('dve', lambda e: e.tensor_tensor(out=W["P2"][:, :2 * w], in0=psA[:, :2 * w], in1=ii, op=ALU.mult), reads=[psA] + outs['tabs'], writes=[W["P2"]])

    def fwd_fft(lhs_list, rhs_list, reads):
        psA = P.ps()
        nl = len(lhs_list)
        for i in range(nl):
            P.op('pe', lambda e: e.matmul(psA[:, :512], r32(lhs_list[i]), r32(rhs_list[i]), start=(i == 0), stop=(i == nl - 1)), reads=reads + [ct["W1"]], writes=[psA])
        twiddle(psA, ct["TWrr"][:], ct["TWii"][:], 256, {'tabs': [ct["TWrr"], ct["TWii"]]})
        P.op('pool', lambda e: e.tensor_tensor(out=r32(W["B"][:, 0:256]), in0=W["P1"][:, 0:256], in1=W["P2"][:, 256:512], op=ALU.subtract), reads=[W["P1"], W["P2"]], writes=[W["B"]])
        P.op('pool', lambda e: e.tensor_tensor(out=r32(W["B"][:, 256:512]), in0=W["P2"][:, 0:256], in1=W["P1"][:, 256:512], op=ALU.add), reads=[W["P1"], W["P2"]], writes=[W["B"]])
        P.op('act', lambda e: e.copy(out=r32(W["B2"][:, 256:512]), in_=W["B"][:, 0:256]), reads=[W["B"]], writes=[W["B2"]])
        P.op('act', lambda e: e.activation(out=r32(W["B2"][:, 0:256]), in_=W["B"][:, 256:512], func=AF.Copy, scale=-1.0), reads=[W["B"]], writes=[W["B2"]])
        psX = P.ps()
        P.op('pe', lambda e: e.matmul(psX[:, :512], r32(ct["W2r"][:]), r32(W["B"][:]), start=True, stop=False), reads=[ct["W2r"], W["B"]], writes=[psX])
        P.op('pe', lambda e: e.matmul(psX[:, :512], r32(ct["W2i"][:]), r32(W["B2"][:]), start=False, stop=True), reads=[ct["W2i"], W["B2"]], writes=[psX])
        return psX

    ny = 0
    for db in range(CPC // CBK):
        ycur = Yt[ny % 2]
        ny += 1
        P.dma(ycur[:], cv_view(2, db), reads=[cv], writes=[ycur])
        for o in range(2):
            xg = Xg[o]
            P.dma(xg[:], cv_view(o, db), reads=[cv], writes=[xg])
            P.dma(Kst[:], kd[o, 0, db * CBK:(db + 1) * CBK, :].rearrange("d (p j) -> p d j", j=128), reads=[kd], writes=[Kst])
            P.op('act', lambda e: e.copy(out=r32(Kp[:]), in_=Kst[:]), reads=[Kst], writes=[Kp])
            P.dma(Kst[:], kd[o, 1, db * CBK:(db + 1) * CBK, :].rearrange("d (p j) -> p d j", j=128), reads=[kd], writes=[Kst])
            P.op('pool', lambda e: e.tensor_copy(out=r32(Kf[:]), in_=Kst[:]), reads=[Kst], writes=[Kf])
            P.op('pool', lambda e: e.tensor_copy(out=r32(Yr[:]), in_=ycur[:]), reads=[ycur], writes=[Yr])
            ynew = Yt[ny % 2]
            ny += 1
            for dl in range(CBK):
                dch = db * CBK + dl
                psXk = fwd_fft([Kp[:, dl, :], Kf[:, dl, :]], [ct["W1"][:, 0, :], ct["W1"][:, 1, :]], [Kp, Kf])
                P.op('act', lambda e: e.copy(out=W["XkRR"][:, 0:256], in_=psXk[:, 0:256]), reads=[psXk], writes=[W["XkRR"]])
                P.op('act', lambda e: e.copy(out=W["XkRR"][:, 256:512], in_=psXk[:, 0:256]), reads=[psXk], writes=[W["XkRR"]])
                P.op('act', lambda e: e.copy(out=W["XkII"][:, 0:256], in_=psXk[:, 256:512]), reads=[psXk], writes=[W["XkII"]])
                P.op('act', lambda e: e.copy(out=W["XkII"][:, 256:512], in_=psXk[:, 256:512]), reads=[psXk], writes=[W["XkII"]])
                psXu = fwd_fft([Yr[:, dl, :]], [ct["W1"][:, 0, :]], [Yr])
                P.op('dve', lambda e: e.tensor_tensor(out=W["P1"][:], in0=psXu[:, :512], in1=W["XkRR"][:], op=ALU.mult), reads=[psXu, W["XkRR"]], writes=[W["P1"]])
                P.op('dve', lambda e: e.tensor_tensor(out=W["P2"][:], in0=psXu[:, :512], in1=W["XkII"][:], op=ALU.mult), reads=[psXu, W["XkII"]], writes=[W["P2"]])
                P.op('pool', lambda e: e.tensor_tensor(out=r32(W["Zr"][:, 0:256]), in0=W["P1"][:, 0:256], in1=W["P2"][:, 256:512], op=ALU.subtract), reads=[W["P1"], W["P2"]], writes=[W["Zr"]])
                P.op('pool', lambda e: e.tensor_tensor(out=r32(W["Zi"][:, 0:256]), in0=W["P2"][:, 0:256], in1=W["P1"][:, 256:512], op=ALU.add), reads=[W["P1"], W["P2"]], writes=[W["Zi"]])
                for hf in range(2):
                    psC = P.ps()
                    P.op('pe', lambda e: e.matmul(psC[:, :256], r32(W["Zr"][:, 128 * hf:128 * hf + 128]), r32(ct["WIa"][:]), start=True, stop=False), reads=[W["Zr"], ct["WIa"]], writes=[psC])
                    P.op('pe', lambda e: e.matmul(psC[:, :256], r32(W["Zi"][:, 128 * hf:128 * hf + 128]), r32(ct["WIb"][:]), start=False, stop=True), reads=[W["Zi"], ct["WIb"]], writes=[psC])
                    P.op('dve', lambda e: e.tensor_tensor(out=W["P1"][:, :256], in0=psC[:, :256], in1=ct["TWIrr"][:, hf, :], op=ALU.mult), reads=[psC, ct["TWIrr"]], writes=[W["P1"]])
                    P.op('dve', lambda e: e.tensor_tensor(out=W["P2"][:, :256], in0=psC[:, :256], in1=ct["TWIii"][:, hf, :], op=ALU.mult), reads=[psC, ct["TWIii"]], writes=[W["P2"]])
                    P.op('pool', lambda e: e.tensor_tensor(out=Dr[hf][:], in0=W["P1"][:, 0:128], in1=W["P2"][:, 128:256], op=ALU.subtract), reads=[W["P1"], W["P2"]], writes=[Dr[hf]])
                    P.op('pool', lambda e: e.tensor_tensor(out=Di[hf][:], in0=W["P2"][:, 0:128], in1=W["P1"][:, 128:256], op=ALU.add), reads=[W["P1"], W["P2"]], writes=[Di[hf]])
                psY = P.ps()
                for hf in range(2):
                    P.op('pe', lambda e: e.matmul(psY[:, :128], ct["Cos"][:, hf, :], Dr[hf][:], start=(hf == 0), stop=False), reads=[ct["Cos"], Dr[hf]], writes=[psY])
                    P.op('pe', lambda e: e.matmul(psY[:, :128], ct["NSin"][:, hf, :], Di[hf][:], start=False, stop=(hf == 1)), reads=[ct["NSin"], Di[hf]], writes=[psY])
                P.op('dve', lambda e: e.tensor_scalar(out=t1[:], in0=psY[:, :128], scalar1=rn[:, o, dch:dch + 1], scalar2=None, op0=ALU.mult), reads=[psY, rn], writes=[t1])
                P.op('dve', lambda e: e.scalar_tensor_tensor(out=t1[:], in0=ycur[:, dl, :], scalar=fbias[:, o, dch:dch + 1], in1=t1[:], op0=ALU.mult, op1=ALU.add),
                     reads=[ycur, fbias, t1], writes=[t1])
                P.op('pool', lambda e: e.tensor_tensor(out=ynew[:, dl, :], in0=t1[:], in1=xg[:, dl, :], op=ALU.mult), reads=[t1, xg], writes=[ynew])
            ycur = ynew
        P.dma(y2[db * CBK:(db + 1) * CBK, :].rearrange("d (p j) -> p d j", j=128), ycur[:], reads=[ycur], writes=[y2], is_out=True)
    P.pop()
    _sc.__exit__(None, None, None)
    return P


def hyena_consts():
    L = SEQ
    N = NFFT
    f64 = np.float64
    n1 = np.arange(256, dtype=f64)[:, None]
    k1 = np.arange(256, dtype=f64)[None, :]
    th = 2 * np.pi * n1 * k1 / 256
    W1 = np.concatenate([np.cos(th), -np.sin(th)], axis=1).reshape(2, 128, 512)
    n2 = np.arange(128, dtype=f64)[:, None]
    tw = 2 * np.pi * n2 * k1 / N
    TWr, TWi = np.cos(tw), -np.sin(tw)
    k2 = np.arange(128, dtype=f64)[None, :]
    t2 = 2 * np.pi * n2 * k2 / 128
    W2r, W2i = np.cos(t2), -np.sin(t2)
    Wr_, Wi_ = np.cos(t2), np.sin(t2)
    WIa = np.concatenate([Wr_, Wi_], axis=1)
    WIb = np.concatenate([-Wi_, Wr_], axis=1)
    k1c = np.arange(256, dtype=f64)[:, None]
    n2r = np.arange(128, dtype=f64)[None, :]
    ti = 2 * np.pi * k1c * n2r / N
    Tr, Ti = np.cos(ti), np.sin(ti)
    TWIrr = np.concatenate([Tr, Tr], axis=1).reshape(2, 128, 256)
    TWIii = np.concatenate([Ti, Ti], axis=1).reshape(2, 128, 256)
    n1r = np.arange(128, dtype=f64)[None, :]
    t3 = 2 * np.pi * k1c * n1r / 256
    Cos = (np.cos(t3) / N).reshape(2, 128, 128)
    NSin = (-np.sin(t3) / N).reshape(2, 128, 128)
    d = {"W1": W1, "TWrr": np.concatenate([TWr, TWr], axis=1), "TWii": np.concatenate([TWi, TWi], axis=1), "W2r": W2r, "W2i": W2i,
         "WIa": WIa, "WIb": WIb, "TWIrr": TWIrr, "TWIii": TWIii, "Cos": Cos, "NSin": NSin}
    d = {k: np.ascontiguousarray(v.astype(np.float32)) for k, v in d.items()}
    f32 = np.float32
    t = np.linspace(0.0, 1.0, L, dtype=f32)
    w = (f32(2.0 * np.pi) * np.arange(L, dtype=f32) / f32(L)).astype(f32)
    f = np.linspace(1e-4, 15.0, 16, dtype=f32)
    fwm_ = (f[None, :] * w[:, None]).astype(f32)
    emb = np.concatenate([t[:, None], np.cos(fwm_), -np.sin(fwm_)], axis=1).astype(f32)
    ridx = np.concatenate([[0], L - np.arange(1, L)])
    d["embT"] = np.ascontiguousarray(emb.T)
    d["embTr"] = np.ascontiguousarray(emb[ridx].T)
    d["trow"] = np.ascontiguousarray(np.tile((-t)[None, :], (128, 1)))
    d["trowr"] = np.ascontiguousarray(np.tile((-t[ridx])[None, :], (128, 1)))
    d["ident"] = np.eye(128, dtype=np.float32)
    lo = np.log(1.5) / 1e-2
    hi = np.log(0.3) / 1e-2
    d["_deltas"] = np.abs(np.linspace(lo, hi, D, dtype=f32)).astype(f32)
    return d


def run_l5(inp, o4):
    P = build_l5()
    hc = hyena_consts()
    deltas = hc.pop("_deltas")
    projT = o4['projT']
    sw = inp['hy_short_w'][0]
    sb = inp['hy_short_b'][0]
    common = dict(hc)
    common.update({
        "fw1": np.ascontiguousarray(inp['hy_fw1'][0]), "fwm": np.ascontiguousarray(inp['hy_fw_mid'][0]),
        "fq": np.ascontiguousarray(inp['hy_freq'][0].reshape(64, 1)),
        "fb": np.ascontiguousarray(np.stack([inp['hy_fb1'][0], inp['hy_fb_mid'][0][0], inp['hy_fb_mid'][0][1]], axis=1)),
    })
    in_maps = []
    for i in range(NCORE):
        d0 = i * CPC
        pj = np.zeros((3, CPC, SEQ + 2), np.float32)
        swl = np.zeros((128, 6, 3), np.float32)
        sbl = np.zeros((128, 6), np.float32)
        for s_ in range(3):
            pj[s_, :, 1:SEQ + 1] = projT[s_ * D + d0:s_ * D + d0 + CPC, :]
            for gq in range(2):
                idx = s_ * D + d0 + gq * 128 + np.arange(128)
                swl[:, s_ * 2 + gq, :] = sw[:, idx].T
                sbl[:, s_ * 2 + gq] = sb[idx]
        m = dict(common)
        m.update({"pj": pj, "sw": swl, "sb": sbl,
                  "fwo": np.ascontiguousarray(inp['hy_fw_out'][0][:, :, :, d0:d0 + CPC]),
                  "dpp": np.ascontiguousarray(deltas[d0:d0 + CPC].reshape(2, 128).T),
                  "fbias": np.ascontiguousarray(np.tile(inp['hy_fbias'][0][None, :, d0:d0 + CPC], (128, 1, 1)))})
        in_maps.append(m)
    res = run_spmd(P, in_maps)
    return {"y2T": np.concatenate([r['y2'] for r in res], axis=0)}


def run_l6(inp, mod, o4, o5):
    P = build_l6()
    m1 = split_mod(mod[1, 0])
    common = {
        "w1": np.ascontiguousarray(inp['ffn_w1'][1]), "w3": np.ascontiguousarray(inp['ffn_w3'][1]), "w2": np.ascontiguousarray(inp['ffn_w2'][1]),
        "wo": np.ascontiguousarray(inp['hy_w_out'][0]), "nw": vec_layout(inp['norm_ffn_w'][1]), "sc2": vec_layout(m1[4]), "sh2": vec_layout(m1[3]),
        "g2": vec_layout(m1[5]), "bo": vec_layout(inp['hy_b_out'][0]), "g1": vec_layout(m1[2]), "fw": vec_layout(inp['final_norm_w']),
    }
    in_maps = []
    for i in range(NCORE):
        m = dict(common)
        m.update({"xT": tok_shard(o4['x2T'], i), "yT": tok_shard(o5['y2T'], i)})
        in_maps.append(m)
    res = run_spmd(P, in_maps)
    outT = np.concatenate([r['outT'] for r in res], axis=1)
    return np.ascontiguousarray(outT.T)[None].astype(np.float32)


def kernel(**inputs):
    inp = {k: np.asarray(v) for k, v in inputs.items()}
    mod = run_l0(inp['c'], inp['c_ctx'], inp['mod_w'], inp['mod_b'])
    o1 = run_l1(inp, mod)
    o2 = run_l2(inp, o1)
    o4 = run_l4(inp, mod, o1, o2)
    o5 = run_l5(inp, o4)
    return run_l6(inp, mod, o4, o5)
```
